# Optimizing a Trainium2 kernel written in Bass

```python
import jax, jax.numpy as jnp
from jax import lax
import numpy as np

D_MODEL = 4096
BATCH = 1
SEQ = 8192
DEPTH = 1

HEAD_DIM = 128
FOX_HEADS = D_MODEL // (2 * HEAD_DIM)
DSA_HEADS = D_MODEL // (2 * HEAD_DIM)
FOX_WIDTH = FOX_HEADS * HEAD_DIM
DSA_WIDTH = DSA_HEADS * HEAD_DIM
MIX_WIDTH = FOX_WIDTH + DSA_WIDTH
IDX_HEADS = 16
IDX_DIM = 64
DSA_TOPK_MAX = 256
Q_BLOCK = 128
ROPE_THETA = 10000.0
NORM_EPS = 1e-6
PEER_HEADS = 8
PEER_NKEYS = 128
PEER_EXPERTS = PEER_NKEYS * PEER_NKEYS
PEER_QDIM = 128
PEER_SUBDIM = PEER_QDIM // 2
PEER_TOPK = 16
PEER_TOK_BLOCK = 64

IN_SIZES = (FOX_WIDTH, FOX_WIDTH, FOX_WIDTH, FOX_HEADS,
            DSA_WIDTH, HEAD_DIM, HEAD_DIM,
            IDX_HEADS * IDX_DIM, IDX_DIM, IDX_HEADS)
IN_WIDTH = FOX_WIDTH * 3 + FOX_HEADS + DSA_WIDTH + 2 * HEAD_DIM + IDX_HEADS * IDX_DIM + IDX_DIM + IDX_HEADS

kernel_name = "hymba_fox_dsa_peer_block"


def _split_points(sizes):
    pts, acc = [], 0
    for s in sizes[:-1]:
        acc += s
        pts.append(acc)
    return pts


def rmsnorm(x, g):
    xf = x.astype(jnp.float32)
    y = xf * lax.rsqrt(jnp.mean(xf * xf, axis=-1, keepdims=True) + NORM_EPS)
    return (y * g.astype(jnp.float32)).astype(x.dtype)


def rope(x, positions):
    d = x.shape[-1]
    inv = ROPE_THETA ** (-jnp.arange(0, d, 2, dtype=jnp.float32) / d)
    ang = positions.astype(jnp.float32)[..., None] * inv
    cos = jnp.cos(ang)[:, :, None, :]
    sin = jnp.sin(ang)[:, :, None, :]
    xf = x.astype(jnp.float32)
    x1, x2 = xf[..., : d // 2], xf[..., d // 2:]
    return jnp.concatenate([x1 * cos - x2 * sin, x2 * cos + x1 * sin], axis=-1).astype(x.dtype)


def fox_attention(q, k, v, log_f):
    B, S, H, hd = q.shape
    c = jnp.cumsum(log_f, axis=1).transpose(0, 2, 1)
    scale = hd ** -0.5
    kpos = jnp.arange(S)

    def block(i):
        start = i * Q_BLOCK
        qb = lax.dynamic_slice_in_dim(q, start, Q_BLOCK, axis=1)
        cb = lax.dynamic_slice_in_dim(c, start, Q_BLOCK, axis=2)
        s = jnp.einsum('bthd,bshd->bhts', qb, k, preferred_element_type=jnp.float32) * scale
        s = s + cb[..., None] - c[:, :, None, :]
        qpos = start + jnp.arange(Q_BLOCK)
        s = jnp.where(kpos[None, :] <= qpos[:, None], s, -jnp.inf)
        p = jax.nn.softmax(s, axis=-1).astype(v.dtype)
        return jnp.einsum('bhts,bshd->bthd', p, v)

    out = lax.map(block, jnp.arange(S // Q_BLOCK))
    return out.transpose(1, 0, 2, 3, 4).reshape(B, S, H, hd)


def dsa_attention(q, k, v, q_idx, k_idx, w_idx, topk):
    B, S, H, hd = q.shape
    scale = hd ** -0.5
    idx_scale = IDX_DIM ** -0.5
    w_scale = IDX_HEADS ** -0.5
    kpos = jnp.arange(S)
    bidx = jnp.arange(B)[:, None, None]

    def block(i):
        start = i * Q_BLOCK
        qb = lax.dynamic_slice_in_dim(q, start, Q_BLOCK, axis=1)
        qib = lax.dynamic_slice_in_dim(q_idx, start, Q_BLOCK, axis=1)
        wb = lax.dynamic_slice_in_dim(w_idx, start, Q_BLOCK, axis=1).astype(jnp.float32) * w_scale
        qpos = start + jnp.arange(Q_BLOCK)
        logits = jnp.einsum('bthd,bsd->bths', qib, k_idx, preferred_element_type=jnp.float32) * idx_scale
        score = jnp.einsum('bths,bth->bts', jax.nn.relu(logits), wb)
        score = jnp.where(kpos[None, :] <= qpos[:, None], score, -jnp.inf)
        _, sel = lax.top_k(score, topk)
        k_sel = k[bidx, sel]
        v_sel = v[bidx, sel]
        s = jnp.einsum('bthd,btkd->bhtk', qb, k_sel, preferred_element_type=jnp.float32) * scale
        valid = sel <= qpos[None, :, None]
        s = jnp.where(valid[:, None], s, -jnp.inf)
        p = jax.nn.softmax(s, axis=-1).astype(v.dtype)
        return jnp.einsum('bhtk,btkd->bthd', p, v_sel)

    out = lax.map(block, jnp.arange(S // Q_BLOCK))
    return out.transpose(1, 0, 2, 3, 4).reshape(B, S, H, hd)


def peer(x, w_q, keys1, keys2, u, v):
    B, S, D = x.shape
    T = B * S
    xt = x.reshape(T, D)
    q = (xt @ w_q).reshape(T, PEER_HEADS, 2, PEER_SUBDIM)
    s1 = jnp.einsum('thd,hnd->thn', q[:, :, 0], keys1, preferred_element_type=jnp.float32)
    s2 = jnp.einsum('thd,hnd->thn', q[:, :, 1], keys2, preferred_element_type=jnp.float32)
    v1, i1 = lax.top_k(s1, PEER_TOPK)
    v2, i2 = lax.top_k(s2, PEER_TOPK)
    cand = (v1[..., :, None] + v2[..., None, :]).reshape(T, PEER_HEADS, PEER_TOPK * PEER_TOPK)
    cand_idx = (i1[..., :, None] * PEER_NKEYS + i2[..., None, :]).reshape(T, PEER_HEADS, PEER_TOPK * PEER_TOPK)
    best, pos = lax.top_k(cand, PEER_TOPK)
    expert = jnp.take_along_axis(cand_idx, pos, axis=-1)
    gate = jax.nn.softmax(best, axis=-1)
    nb = T // PEER_TOK_BLOCK

    def block(args):
        xb, eb, gb = args
        ub = u[eb]
        h = jnp.einsum('td,thkd->thk', xb, ub, preferred_element_type=jnp.float32)
        a = (gb * jax.nn.gelu(h, approximate=False)).astype(xb.dtype)
        return jnp.einsum('thk,thkd->td', a, v[eb])

    out = lax.map(block, (xt.reshape(nb, PEER_TOK_BLOCK, D),
                          expert.reshape(nb, PEER_TOK_BLOCK, PEER_HEADS, PEER_TOPK),
                          gate.reshape(nb, PEER_TOK_BLOCK, PEER_HEADS, PEER_TOPK)))
    return out.reshape(B, S, D)


def setup_inputs(seed: int = 0) -> dict:
    key = jax.random.key(seed)
    ks = jax.random.split(key, 16)
    f32 = jnp.float32
    n = lambda k, shape, s: jax.random.normal(k, shape, f32) * s
    return {
        "x": jax.random.normal(ks[0], (BATCH, SEQ, D_MODEL), f32),
        "positions": jnp.broadcast_to(jnp.arange(SEQ, dtype=jnp.int32), (BATCH, SEQ)),
        "ln1_g": 1.0 + n(ks[1], (DEPTH, D_MODEL), 0.02),
        "w_in": n(ks[2], (DEPTH, D_MODEL, IN_WIDTH), D_MODEL ** -0.5),
        "fox_forget_b": jax.random.uniform(ks[3], (DEPTH, FOX_HEADS), f32, 1.0, 5.0),
        "fox_qn_g": 1.0 + n(ks[4], (DEPTH, HEAD_DIM), 0.02),
        "fox_kn_g": 1.0 + n(ks[5], (DEPTH, HEAD_DIM), 0.02),
        "dsa_qn_g": 1.0 + n(ks[6], (DEPTH, HEAD_DIM), 0.02),
        "dsa_kn_g": 1.0 + n(ks[7], (DEPTH, HEAD_DIM), 0.02),
        "w_o": n(ks[8], (DEPTH, MIX_WIDTH, D_MODEL), MIX_WIDTH ** -0.5),
        "ln2_g": 1.0 + n(ks[9], (DEPTH, D_MODEL), 0.02),
        "peer_wq": n(ks[10], (DEPTH, D_MODEL, PEER_HEADS * PEER_QDIM), D_MODEL ** -0.5),
        "peer_keys1": n(ks[11], (DEPTH, PEER_HEADS, PEER_NKEYS, PEER_SUBDIM), PEER_SUBDIM ** -0.5),
        "peer_keys2": n(ks[12], (DEPTH, PEER_HEADS, PEER_NKEYS, PEER_SUBDIM), PEER_SUBDIM ** -0.5),
        "peer_u": n(ks[13], (DEPTH, PEER_EXPERTS, D_MODEL), D_MODEL ** -0.5),
        "peer_v": n(ks[14], (DEPTH, PEER_EXPERTS, D_MODEL), (PEER_HEADS * PEER_TOPK) ** -0.5),
    }


def reference(x, positions, ln1_g, w_in, fox_forget_b, fox_qn_g, fox_kn_g, dsa_qn_g, dsa_kn_g,
              w_o, ln2_g, peer_wq, peer_keys1, peer_keys2, peer_u, peer_v):
    B, S, D = x.shape
    topk = min(DSA_TOPK_MAX, S // 4)
    splits = _split_points(IN_SIZES)
    for l in range(DEPTH):
        h = rmsnorm(x, ln1_g[l])
        proj = h @ w_in[l]
        fq, fk, fv, ff, dq, dk, dv, iq, ik, iw = jnp.split(proj, splits, axis=-1)
        fq = rmsnorm(fq.reshape(B, S, FOX_HEADS, HEAD_DIM), fox_qn_g[l])
        fk = rmsnorm(fk.reshape(B, S, FOX_HEADS, HEAD_DIM), fox_kn_g[l])
        fv = fv.reshape(B, S, FOX_HEADS, HEAD_DIM)
        log_f = jax.nn.log_sigmoid(ff.astype(jnp.float32) + fox_forget_b[l].astype(jnp.float32))
        fox_out = fox_attention(fq, fk, fv, log_f)
        dq = rope(rmsnorm(dq.reshape(B, S, DSA_HEADS, HEAD_DIM), dsa_qn_g[l]), positions)
        dk = rope(rmsnorm(dk.reshape(B, S, 1, HEAD_DIM), dsa_kn_g[l]), positions)[:, :, 0]
        iq = rope(iq.reshape(B, S, IDX_HEADS, IDX_DIM), positions)
        ik = rope(ik.reshape(B, S, 1, IDX_DIM), positions)[:, :, 0]
        dsa_out = dsa_attention(dq, dk, dv, iq, ik, iw, topk)
        mix = jnp.concatenate([fox_out.reshape(B, S, FOX_WIDTH), dsa_out.reshape(B, S, DSA_WIDTH)], axis=-1)
        x = x + mix @ w_o[l]
        h2 = rmsnorm(x, ln2_g[l])
        x = x + peer(h2, peer_wq[l], peer_keys1[l], peer_keys2[l], peer_u[l], peer_v[l])
    return x
```

```python
import numpy as np
from contextlib import ExitStack
import concourse.bass as bass
import concourse.mybir as mybir
from concourse.bass_utils import run_bass_kernel_spmd

F32 = mybir.dt.float32; BF16 = mybir.dt.bfloat16; I32 = mybir.dt.int32; U32 = mybir.dt.uint32
AF = mybir.ActivationFunctionType; ALU = mybir.AluOpType; AX = mybir.AxisListType

D = 4096; S = 8192; NTB = 64; NSLOT = 8; HD = 128; NH = 16
INW = 9568
C_FQ, C_FK, C_FV, C_FF, C_DQ, C_DK, C_DV, C_IQ, C_IK, C_IW = 0, 2048, 4096, 6144, 6160, 8208, 8336, 8464, 9488, 9552
EPS = 1e-6
NEG = -30000.0
TOPK = 256
NBIS = 18


class KB:
    def __init__(self, nc, es):
        self.nc = nc; self.es = es; self.mem = es
        self.eng = {'pe': nc.tensor, 'act': nc.scalar, 'dve': nc.vector, 'pool': nc.gpsimd, 'sp': nc.sync}
        self.sem = {}; self.cnt = {}
        for e in ('pe', 'act', 'dve', 'pool'):
            self.sem[e] = es.enter_context(nc.semaphore("sem_" + e)); self.cnt[e] = 0
        self.waited = {e: {} for e in self.eng}
        self.lastw = {}; self.readers = {}
        self.ninst = 0

    def sb(self, name, shape, dt):
        return self.mem.enter_context(self.nc.sbuf_tensor(name, shape, dt))

    def ps(self, name, shape, dt):
        return self.mem.enter_context(self.nc.psum_tensor(name, shape, dt))

    def _wait(self, e, deps):
        best = {}
        for (src, n) in deps:
            if src is None:
                continue
            if src == 'pe' and e == 'pe':
                continue
            if n > best.get(src, 0):
                best[src] = n
        for src, n in best.items():
            if self.waited[e].get(src, 0) >= n:
                continue
            self.waited[e][src] = n
            val = n * 16 if src.startswith('dma:') else n
            self.eng[e].wait_ge(self.sem[src], val); self.ninst += 1

    def _deps(self, r, w):
        deps = []
        for k in r:
            deps.append(self.lastw.get(k, (None, 0)))
        for k in w:
            deps.append(self.lastw.get(k, (None, 0)))
            deps.extend(self.readers.get(k, []))
        return deps

    def _commit(self, tag, r, w):
        for k in r:
            self.readers.setdefault(k, []).append(tag)
        for k in w:
            self.lastw[k] = tag; self.readers[k] = []

    def op(self, e, fn, r=(), w=()):
        self._wait(e, self._deps(r, w))
        inst = fn(self.eng[e]); self.ninst += 1
        self.cnt[e] += 1
        inst.then_inc(self.sem[e], 1)
        self._commit((e, self.cnt[e]), r, w)

    def _dsem(self, key):
        km = self.__dict__.setdefault('keymap', {}); fr = self.__dict__.setdefault('free', [])
        if key in km:
            return km[key]
        if fr:
            src = fr.pop()
        else:
            src = 'dma:%d' % len([s for s in self.sem if s.startswith('dma:')])
            self.sem[src] = self.es.enter_context(self.nc.semaphore("sd_%s" % src[4:])); self.cnt[src] = 0
        km[key] = src
        return src

    def dma(self, q, key, out, in_, r=(), w=(), **kw):
        src = self._dsem(key)
        deps = self._deps(r, w)
        deps.append((src, self.cnt[src]))
        self._wait(q, deps)
        inst = self.eng[q].dma_start(out=out, in_=in_, **kw); self.ninst += 1
        self.cnt[src] += 1
        inst.then_inc(self.sem[src], 16)
        self._commit((src, self.cnt[src]), r, w)

    def gather(self, key, out, table, idx_ap, r=(), w=()):
        src = self._dsem(key)
        deps = self._deps(r, w)
        deps.append((src, self.cnt[src]))
        self._wait('pool', deps)
        inst = self.nc.gpsimd.indirect_dma_start(out=out, out_offset=None, in_=table,
                                                 in_offset=bass.IndirectOffsetOnAxis(ap=idx_ap, axis=0))
        self.ninst += 1
        self.cnt[src] += 1
        inst.then_inc(self.sem[src], 16)
        self._commit((src, self.cnt[src]), r, w)

    def barrier(self):
        srcs = [(s, n) for s, n in self.cnt.items() if n > 0]
        for e in self.eng:
            self._wait(e, srcs)
        self.lastw = {}; self.readers = {}
        self.keymap = {}; self.free = [s for s in self.sem if s.startswith('dma:')]


def build(dbg=None):
    nc = bass.Bass("TRN2", target_bir_lowering=False)
    dbg = dbg or {}
    STOP = dbg.get('stop', 99)

    def din(name, shape, dt=F32):
        return nc.dram_tensor(name, shape, dt, kind="ExternalInput").ap()

    def dscr(name, shape, dt):
        kind = "ExternalOutput" if name in dbg.get('dump', ()) else "Internal"
        return nc.dram_tensor(name, shape, dt, kind=kind).ap()

    x_all = din("x_all", [S, D]); x_own = din("x_own", [1024, D])
    pos_all = din("pos_all", [128, NTB], I32); pos_own = din("pos_own", [128, NSLOT], I32)
    qidx_p = din("qidx_p", [128, NSLOT]); selblk = din("selblk", [1, NSLOT * NTB])
    ln1_g = din("ln1_g", [1, D]); ln2_g = din("ln2_g", [1, D])
    w_in = din("w_in", [D, INW]); w_o = din("w_o", [D, D]); wq = din("peer_wq", [D, 1024])
    ffb = din("ffb", [1, NH]); gq_f = din("gq_f", [1, HD]); gk_f = din("gk_f", [1, HD])
    gq_d = din("gq_d", [1, HD]); gk_d = din("gk_d", [1, HD])
    kk = din("kk", [8, 128, 256]); pu = din("peer_u", [16384, D]); pv = din("peer_v", [16384, D])
    c_ident = din("c_ident", [128, 128]); c_tri = din("c_tri", [128, 128]); c_sel16 = din("c_sel16", [16, NH * 128])
    c_inv = din("c_inv", [1, 96]); c_kiota = din("c_kiota", [1, 1024])
    out_own = nc.dram_tensor("out_own", [1024, D], F32, kind="ExternalOutput").ap()

    hT_all = dscr("hT_all", [NTB, 128, D], BF16); hT_own = dscr("hT_own", [NSLOT, 128, D], BF16)
    fkT = dscr("fkT", [NH, 128, S], BF16); fv = dscr("fv", [NH, 128, NTB, 128], BF16)
    dkT = dscr("dkT", [128, S], BF16); dv = dscr("dv", [128, NTB, 128], BF16); ikT = dscr("ikT", [64, S], BF16)
    fqT = dscr("fqT", [NH, 128, 1024], BF16); dqT = dscr("dqT", [128, NSLOT, NH, 128], BF16)
    iqT = dscr("iqT", [64, NSLOT, NH, 128], BF16)
    mix = dscr("mix", [NSLOT, 128, D], BF16); mixT = dscr("mixT", [NSLOT, 128, D], BF16)
    x1 = dscr("x1", [1024, D], F32); h2b = dscr("h2b", [1024, D], BF16); uvb = dscr("uvb", [16384, 2 * D], BF16); h2T = dscr("h2T", [NSLOT, 128, D], BF16)
    qTs = dscr("qTs", [NSLOT, 128, 1024], BF16)
    dbg_nc = dscr("dbg_nc", [128, NTB * NH], F32)
    dbg_tau = dscr("dbg_tau", [128, NSLOT], F32)

    with ExitStack() as es:
        k = KB(nc, es)
        ident_f = k.sb("ident_f", [128, 128], F32); ident_b = k.sb("ident_b", [128, 128], BF16)
        tri_f = k.sb("tri_f", [128, 128], F32); ones_f = k.sb("ones_f", [128, 128], F32)
        sel16 = k.sb("sel16", [16, NH * 128], BF16); sel16f = k.sb("sel16f", [16, NH * 128], F32)
        inv_t = k.sb("inv_t", [128, 96], F32)
        nc_all = k.sb("nc_all", [128, NTB * NH], F32)
        aw = k.sb("aw", [128, NSLOT * NH], F32)
        qidx_t = k.sb("qidx_t", [128, NSLOT], F32)
        k.dma('sp', 'c0', ident_f[:], c_ident[:, :], w=['ident_f'])
        k.dma('sp', 'c1', tri_f[:], c_tri[:, :], w=['tri_f'])
        k.dma('sp', 'c2', sel16f[:], c_sel16[:, :], w=['sel16f'])
        k.dma('sp', 'c3', inv_t[:], c_inv[0:1, :].partition_broadcast(128), w=['inv_t'])
        k.dma('sp', 'c4', qidx_t[:], qidx_p[:, :], w=['qidx_t'])
        k.op('dve', lambda e: e.tensor_copy(out=ident_b[:], in_=ident_f[:]), r=['ident_f'], w=['ident_b'])
        k.op('dve', lambda e: e.tensor_copy(out=sel16[:], in_=sel16f[:]), r=['sel16f'], w=['sel16'])
        k.op('dve', lambda e: e.memset(ones_f[:], 1.0), w=['ones_f'])

        def rope_tables(m, cos_t, sin_t, posf, ncol, n, inv_off, half, tmp):
            nb = ncol; W = nb * half
            ang = tmp[:, 0:W]; nn = tmp[:, W:2 * W]; ni = tmp[:, 2 * W:3 * W].bitcast(I32); t4 = tmp[:, 3 * W:4 * W]
            for (dst, shift) in ((sin_t, 0.0), (cos_t, np.pi / 2)):
                k.op('dve', lambda e: e.tensor_tensor(out=ang.rearrange("p (b d) -> p b d", b=nb), in0=inv_t[:, inv_off:inv_off + half].unsqueeze(1).to_broadcast([128, nb, half]),
                                                     in1=posf.unsqueeze(2).to_broadcast([128, nb, half]), op=ALU.mult), r=['inv_t', m + 'posf'], w=[m + 'rt'])
                if shift != 0.0:
                    k.op('dve', lambda e: e.tensor_scalar(out=ang, in0=ang, scalar1=float(shift), scalar2=None, op0=ALU.add), r=[m + 'rt'], w=[m + 'rt'])
                k.op('dve', lambda e: e.tensor_scalar(out=nn, in0=ang, scalar1=float(1.0 / (2 * np.pi)), scalar2=None, op0=ALU.mult), r=[m + 'rt'], w=[m + 'rt'])
                k.op('dve', lambda e: e.tensor_copy(out=ni, in_=nn), r=[m + 'rt'], w=[m + 'rt'])
                k.op('dve', lambda e: e.tensor_copy(out=nn, in_=ni), r=[m + 'rt'], w=[m + 'rt'])
                k.op('dve', lambda e: e.scalar_tensor_tensor(out=ang, in0=nn, scalar=-6.28125, in1=ang, op0=ALU.mult, op1=ALU.add), r=[m + 'rt'], w=[m + 'rt'])
                k.op('dve', lambda e: e.scalar_tensor_tensor(out=ang, in0=nn, scalar=float(-(2 * np.pi - 6.28125)), in1=ang, op0=ALU.mult, op1=ALU.add), r=[m + 'rt'], w=[m + 'rt'])
                k.op('dve', lambda e: e.tensor_scalar(out=t4, in0=ang, scalar1=float(np.pi), scalar2=float(-2 * np.pi), op0=ALU.is_gt, op1=ALU.mult), r=[m + 'rt'], w=[m + 'rt'])
                k.op('dve', lambda e: e.tensor_tensor(out=ang, in0=ang, in1=t4, op=ALU.add), r=[m + 'rt'], w=[m + 'rt'])
                k.op('dve', lambda e: e.tensor_scalar(out=t4, in0=ang, scalar1=float(-np.pi), scalar2=float(2 * np.pi), op0=ALU.is_lt, op1=ALU.mult), r=[m + 'rt'], w=[m + 'rt'])
                k.op('dve', lambda e: e.tensor_tensor(out=ang, in0=ang, in1=t4, op=ALU.add), r=[m + 'rt'], w=[m + 'rt'])
                k.op('dve', lambda e: e.tensor_scalar(out=ang, in0=ang, scalar1=3.14159, scalar2=-3.14159, op0=ALU.min, op1=ALU.max), r=[m + 'rt'], w=[m + 'rt'])
                k.op('act', lambda e, dst=dst: e.activation(out=dst, in_=ang, func=AF.Sin), r=[m + 'rt'], w=[m + 'tab'])

        def rope_apply(eng, m, out_ap, in_ap, cos_t, sin_t, nh, half, t1, t2):
            x1v = in_ap[:, :, 0:half]; x2v = in_ap[:, :, half:2 * half]
            cb = cos_t.unsqueeze(1).to_broadcast([128, nh, half]); sbc = sin_t.unsqueeze(1).to_broadcast([128, nh, half])
            rk = [m + 'ropein', m + 'tab']
            k.op(eng, lambda e: e.tensor_tensor(out=t1, in0=x1v, in1=cb, op=ALU.mult), r=rk, w=[m + 't1'])
            k.op(eng, lambda e: e.tensor_tensor(out=t2, in0=x2v, in1=sbc, op=ALU.mult), r=rk, w=[m + 't2'])
            k.op(eng, lambda e: e.tensor_tensor(out=out_ap[:, :, 0:half], in0=t1, in1=t2, op=ALU.subtract), r=[m + 't1', m + 't2'], w=[m + 'ropeout'])
            k.op(eng, lambda e: e.tensor_tensor(out=t1, in0=x2v, in1=cb, op=ALU.mult), r=rk, w=[m + 't1'])
            k.op(eng, lambda e: e.tensor_tensor(out=t2, in0=x1v, in1=sbc, op=ALU.mult), r=rk, w=[m + 't2'])
            k.op(eng, lambda e: e.tensor_tensor(out=out_ap[:, :, half:2 * half], in0=t1, in1=t2, op=ALU.add), r=[m + 't1', m + 't2'], w=[m + 'ropeout'])

        def norm_transpose_phase(tag, src, nblk, gain_dram, dstT, dst_f32=None, do_norm=True, src_bf16=False, dst_b16=None):
            with ExitStack() as ph:
                k.mem = ph
                g1 = k.sb(tag + "g1", [128, D], F32) if do_norm else None
                xts = [k.sb(tag + "xt%d" % i, [128, D], BF16 if src_bf16 else F32) for i in range(2)]
                hbs = [k.sb(tag + "hb%d" % i, [128, D], BF16) for i in range(2)] if not src_bf16 else xts
                hTs = [k.sb(tag + "hT%d" % i, [128, D], BF16) for i in range(2)]
                hfs = [k.sb(tag + "hf%d" % i, [128, D], F32) for i in range(2)] if dst_f32 is not None else None
                junk = k.sb(tag + "junk", [128, D], BF16)
                st = k.sb(tag + "st", [128, 8], F32)
                pts = [k.ps(tag + "pt%d" % i, [128, 1024], BF16) for i in range(4)]
                if do_norm:
                    k.dma('sp', tag + 'g', g1[:], gain_dram[0:1, :].partition_broadcast(128), w=[tag + 'g1'])
                for b in range(nblk):
                    i = b % 2
                    xt = xts[i]; hb = hbs[i]; hT = hTs[i]
                    k.dma('sp', tag + 'x%d' % i, xt[:], src[b], w=[tag + 'xt%d' % i])
                    if do_norm:
                        c0 = (b % 2) * 4
                        k.op('act', lambda e: e.activation(out=junk[:], in_=xt[:], func=AF.Square, accum_out=st[:, c0:c0 + 1]),
                             r=[tag + 'xt%d' % i], w=[tag + 'junk', tag + 'st%d' % i])
                        k.op('dve', lambda e: e.tensor_scalar(out=st[:, c0 + 1:c0 + 2], in0=st[:, c0:c0 + 1], scalar1=1.0 / D, scalar2=EPS,
                                                             op0=ALU.mult, op1=ALU.add), r=[tag + 'st%d' % i], w=[tag + 'st%d' % i])
                        k.op('act', lambda e: e.activation(out=st[:, c0 + 2:c0 + 3], in_=st[:, c0 + 1:c0 + 2], func=AF.Ln), r=[tag + 'st%d' % i], w=[tag + 'st%d' % i])
                        k.op('act', lambda e: e.activation(out=st[:, c0 + 3:c0 + 4], in_=st[:, c0 + 2:c0 + 3], func=AF.Exp, scale=-0.5), r=[tag + 'st%d' % i], w=[tag + 'st%d' % i])
                        if dst_f32 is not None:
                            hf = hfs[i]
                            k.op('dve', lambda e: e.scalar_tensor_tensor(out=hf[:], in0=xt[:], scalar=st[:, c0 + 3:c0 + 4], in1=g1[:], op0=ALU.mult, op1=ALU.mult),
                                 r=[tag + 'xt%d' % i, tag + 'st%d' % i, tag + 'g1'], w=[tag + 'hf%d' % i])
                            k.op('pool', lambda e: e.tensor_copy(out=hb[:], in_=hf[:]), r=[tag + 'hf%d' % i], w=[tag + 'hb%d' % i])
                            k.dma('act', tag + 'hfo%d' % i, dst_f32[b], hf[:], r=[tag + 'hf%d' % i], w=[tag + 'dstf'])
                        else:
                            k.op('dve', lambda e: e.scalar_tensor_tensor(out=hb[:], in0=xt[:], scalar=st[:, c0 + 3:c0 + 4], in1=g1[:], op0=ALU.mult, op1=ALU.mult),
                                 r=[tag + 'xt%d' % i, tag + 'st%d' % i, tag + 'g1'], w=[tag + 'hb%d' % i])
                            if dst_b16 is not None:
                                k.dma('act', tag + 'hbo%d' % i, dst_b16[b], hb[:], r=[tag + 'hb%d' % i], w=[tag + 'dstb'])
                    hkey = (tag + 'hb%d' % i) if not src_bf16 else (tag + 'xt%d' % i)
                    for q in range(4):
                        for j in range(8):
                            kc = q * 8 + j
                            k.op('pe', lambda e: e.transpose(pts[q][:, j * 128:(j + 1) * 128], hb[:, kc * 128:(kc + 1) * 128], ident_b[:]),
                                 r=[hkey, 'ident_b'], w=[tag + 'pt%d' % q])
                        eng = 'act' if q % 2 == 0 else 'dve'
                        if eng == 'act':
                            k.op('act', lambda e: e.copy(out=hT[:, q * 1024:(q + 1) * 1024], in_=pts[q][:, :]), r=[], w=[tag + 'pt%d' % q, tag + 'hT%d' % i])
                        else:
                            k.op('dve', lambda e: e.tensor_copy(out=hT[:, q * 1024:(q + 1) * 1024], in_=pts[q][:, :]), r=[], w=[tag + 'pt%d' % q, tag + 'hT%d' % i])
                    k.dma('pool', tag + 'ho%d' % i, dstT[b], hT[:], r=[tag + 'hT%d' % i], w=[tag + 'dstT'])
                k.barrier()
            k.mem = es

        norm_transpose_phase("na", [x_all[b * 128:(b + 1) * 128, :] for b in range(NTB)], NTB, ln1_g, [hT_all[b] for b in range(NTB)])
        norm_transpose_phase("no", [x_own[b * 128:(b + 1) * 128, :] for b in range(NSLOT)], NSLOT, ln1_g, [hT_own[b] for b in range(NSLOT)])

        def project(tag, hT_src, nblk, wdram, chunks, post, pre=None, nht_buf=3, nps_buf=3, resident=False):
            wfs = [k.sb(tag + "wf%d" % i, [128, 2, 512], F32) for i in range(2)]
            wbs = [k.sb(tag + "wb%d" % i, [128, 32, 512], BF16) for i in range(2)]
            hts = [k.sb(tag + "ht%d" % i, [128, D], BF16) for i in range(nblk if resident else nht_buf)]
            pss = [k.ps(tag + "ps%d" % i, [128, 512], F32) for i in range(nps_buf)]
            wview = wdram.rearrange("(kc p) c -> p kc c", p=128)
            nwf = 0; nps = 0; nht = 0; pending = None
            for ci, (name, pieces, ncols) in enumerate(chunks):
                wb = wbs[ci % 2]; wbk = tag + 'wb%d' % (ci % 2)
                for j in range(16):
                    wf = wfs[nwf % 2]; wfk = tag + 'wf%d' % (nwf % 2); nwf += 1
                    for pi, (doff, scol, n) in enumerate(pieces):
                        k.dma('sp', wfk + 'p%d' % pi, wf[:, :, doff:doff + n], wview[:, j * 2:(j + 1) * 2, scol:scol + n], w=[wfk])
                    eng = ('dve', 'pool', 'act')[j % 3]
                    if eng == 'act':
                        k.op('act', lambda e: e.copy(out=wb[:, j * 2:(j + 1) * 2, 0:ncols], in_=wf[:, :, 0:ncols]), r=[wfk], w=[wbk])
                    else:
                        k.op(eng, lambda e: e.tensor_copy(out=wb[:, j * 2:(j + 1) * 2, 0:ncols], in_=wf[:, :, 0:ncols]), r=[wfk], w=[wbk])
                if pre is not None:
                    pre(name)
                for b in range(nblk):
                    if resident:
                        ht = hts[b]; htk = tag + 'ht%d' % b
                        if ci == 0:
                            k.dma('pool', htk, ht[:], hT_src[b], w=[htk])
                    else:
                        ht = hts[nht % nht_buf]; htk = tag + 'ht%d' % (nht % nht_buf); nht += 1
                        k.dma('sp', htk, ht[:], hT_src[b], w=[htk])
                    ps = pss[nps % nps_buf]; psk = tag + 'ps%d' % (nps % nps_buf); nps += 1
                    for kc in range(32):
                        k.op('pe', lambda e: e.matmul(ps[:, 0:ncols], lhsT=ht[:, kc * 128:(kc + 1) * 128], rhs=wb[:, kc, 0:ncols],
                                                      start=(kc == 0), stop=(kc == 31)), r=[htk, wbk], w=[psk])
                    if pending is not None:
                        post(*pending)
                    pending = (name, b, ps, psk)
            if pending is not None:
                post(*pending)

        def make_post_env(tag, nblk, posf_tile, pos_key, light=False, stg=True, share=None):
            env = {}
            env['ssq'] = k.sb(tag + "ssq", [128, 16], F32)
            env['yb'] = k.sb(tag + "yb", [128, 4, 128], BF16)
            if not light:
                env['yf'] = k.sb(tag + "yf", [128, 4, 128], F32)
                env['yr'] = k.sb(tag + "yr", [128, 4, 128], F32)
                env['t1'] = k.sb(tag + "t1", [128, 8, 64], F32); env['t2'] = k.sb(tag + "t2", [128, 8, 64], F32)
            env['junk'] = k.sb(tag + "pj", [128, 128], F32)
            if stg:
                env['stgT'] = [k.sb(tag + "stgT%d" % i, [128, 4, 512], BF16) for i in range(2)]
                env['stgV'] = [k.sb(tag + "stgV%d" % i, [128, 4, 4, 128], BF16) for i in range(2)]
            env['ptT'] = k.ps(tag + "ptT", [128, 1024], BF16)
            env['gt'] = {}
            if share is not None:
                for nm_ in ('cos', 'sin', 'cos2', 'sin2'):
                    env[nm_] = share[nm_]
                k.lastw[tag + 'tab'] = k.lastw.get(share['tag'] + 'tab', (None, 0))
                return env
            env['tag'] = tag
            env['cos'] = k.sb(tag + "cos", [128, nblk * 64], F32); env['sin'] = k.sb(tag + "sin", [128, nblk * 64], F32)
            env['cos2'] = k.sb(tag + "cos2", [128, nblk * 32], F32); env['sin2'] = k.sb(tag + "sin2", [128, nblk * 32], F32)
            env['rtmp'] = k.sb(tag + "rtmp", [128, 2048], F32)
            def gen_tables():
                for b in range(0, nblk, 8):
                    rope_tables(tag, env['cos'][:, b * 64:(b + 8) * 64], env['sin'][:, b * 64:(b + 8) * 64], posf_tile[:, b:b + 8], 8, None, 0, 64, env['rtmp'])
                    rope_tables(tag, env['cos2'][:, b * 32:(b + 8) * 32], env['sin2'][:, b * 32:(b + 8) * 32], posf_tile[:, b:b + 8], 8, None, 64, 32, env['rtmp'])
            if light:
                env['gen_tables'] = gen_tables
            else:
                gen_tables()
            return env

        def load_gain(tag, env, name, gdram, mul):
            gt = k.sb(tag + "g_" + name, [128, 128], F32)
            k.dma('sp', tag + 'g_' + name, gt[:], gdram[0:1, :].partition_broadcast(128), w=[tag + 'g_' + name])
            if mul != 1.0:
                k.op('dve', lambda e: e.tensor_scalar(out=gt[:], in0=gt[:], scalar1=float(mul), scalar2=None, op0=ALU.mult), r=[], w=[tag + 'g_' + name])
            env['gt'][name] = (gt, tag + 'g_' + name)

        def post_norm_heads(tag, env, ps, psk, nh, gname, rope_b):
            ssq = env['ssq']; gt, gk = env['gt'][gname]
            for h in range(nh):
                k.op('act', lambda e: e.activation(out=env['junk'][:], in_=ps[:, h * 128:(h + 1) * 128], func=AF.Square, accum_out=ssq[:, h:h + 1]),
                     r=[], w=[psk, tag + 'pj', tag + 'ssq'])
            k.op('dve', lambda e: e.tensor_scalar(out=ssq[:, 4:4 + nh], in0=ssq[:, 0:nh], scalar1=1.0 / HD, scalar2=EPS, op0=ALU.mult, op1=ALU.add), r=[], w=[tag + 'ssq'])
            k.op('act', lambda e: e.activation(out=ssq[:, 8:8 + nh], in_=ssq[:, 4:4 + nh], func=AF.Ln), r=[], w=[tag + 'ssq'])
            k.op('act', lambda e: e.activation(out=ssq[:, 12:12 + nh], in_=ssq[:, 8:8 + nh], func=AF.Exp, scale=-0.5), r=[], w=[tag + 'ssq'])
            dst = env['yf'] if rope_b is not None else env['yb']
            dk_ = tag + ('ropein' if rope_b is not None else 'yb')
            for h in range(nh):
                k.op('dve', lambda e: e.scalar_tensor_tensor(out=dst[:, h, :], in0=ps[:, h * 128:(h + 1) * 128], scalar=ssq[:, 12 + h:13 + h], in1=gt[:],
                                                            op0=ALU.mult, op1=ALU.mult), r=[tag + 'ssq', gk], w=[psk, dk_])
            if rope_b is not None:
                b = rope_b
                rope_apply('pool', tag, env['yr'][:, 0:nh, :], env['yf'][:, 0:nh, :], env['cos'][:, b * 64:(b + 1) * 64], env['sin'][:, b * 64:(b + 1) * 64],
                           nh, 64, env['t1'][:, 0:nh, :], env['t2'][:, 0:nh, :])
                k.op('pool', lambda e: e.tensor_copy(out=env['yb'][:, 0:nh, :], in_=env['yr'][:, 0:nh, :]), r=[tag + 'ropeout'], w=[tag + 'yb'])

        def transpose_heads_to_stg(tag, env, nh, b4, stg, stgk, width=128, src=None, srck=None):
            src = env['yb'] if src is None else src; srck = (tag + 'yb') if srck is None else srck
            pt = env['ptT']
            for h in range(nh):
                k.op('pe', lambda e: e.transpose(pt[0:width, h * 128:(h + 1) * 128], src[:, h, 0:width], ident_b[:]), r=[srck, 'ident_b'], w=[tag + 'ptT'])
            k.op('act', lambda e: e.copy(out=stg[0:width, 0:nh, b4 * 128:(b4 + 1) * 128],
                                         in_=pt[0:width, 0:nh * 128].rearrange("p (h t) -> p h t", h=nh)), r=[], w=[tag + 'ptT', stgk])

        if STOP >= 2:
          with ExitStack() as ph:
            k.mem = ph
            tag = "kp"
            posi = k.sb("kp_posi", [128, NTB], I32); posf = k.sb("kp_posf", [128, NTB], F32)
            k.dma('sp', 'kp_pos', posi[:], pos_all[:, :], w=['kp_posi'])
            k.op('dve', lambda e: e.tensor_copy(out=posf[:], in_=posi[:]), r=['kp_posi'], w=['kpposf'])
            env = make_post_env(tag, NTB, posf, 'kpposf', light=True)
            load_gain(tag, env, 'gk_f', gk_f, 1.0); load_gain(tag, env, 'gk_d', gk_d, 1.0)
            fbt = k.sb("kp_fb", [128, NH], F32)
            k.dma('sp', 'kp_fb', fbt[:], ffb[0:1, :].partition_broadcast(128), w=['kp_fb'])
            prefix = k.sb("kp_prefix", [128, NH], F32); nlf = k.sb("kp_nlf", [128, NH], F32)
            k.op('dve', lambda e: e.memset(prefix[:], 0.0), w=['kp_prefix'])
            psc = k.ps("kp_psc", [128, 512], F32)
            stgI = [k.sb("kp_stgI%d" % i, [64, 1, 512], BF16) for i in range(2)]
            stgD = [k.sb("kp_stgD%d" % i, [128, 1, 512], BF16) for i in range(2)]
            stgDV = [k.sb("kp_stgDV%d" % i, [128, 4, 128], BF16) for i in range(2)]
            xms = [k.sb("kp_xm%d" % i, [128, 4, 336], F32) for i in range(2)]
            nlf4 = k.sb("kp_nlf4", [128, 4, NH], F32); cs4 = k.sb("kp_cs4", [128, 128], F32)
            sq4 = k.sb("kp_sq4", [128, 4, 128], F32); ss4 = k.sb("kp_ss4", [128, 16], F32)
            dkb4 = k.sb("kp_dkb4", [128, 4, 128], BF16); ikb4 = k.sb("kp_ikb4", [128, 4, 64], BF16)
            r4a = k.sb("kp_r4a", [128, 4, 64], F32); r4b = k.sb("kp_r4b", [128, 4, 64], F32)

            def rope4(m, outb, xin, xin_key, cosv, sinv, half):
                x1v = xin[:, :, 0:half]; x2v = xin[:, :, half:2 * half]
                ta = r4a[:, :, 0:half]; tb = r4b[:, :, 0:half]
                rk = [xin_key, tag + 'tab']
                k.op('dve', lambda e: e.tensor_tensor(out=ta, in0=x1v, in1=cosv, op=ALU.mult), r=rk, w=['kp_r4a'])
                k.op('pool', lambda e: e.tensor_tensor(out=tb, in0=x2v, in1=sinv, op=ALU.mult), r=rk, w=['kp_r4b'])
                k.op('dve', lambda e: e.tensor_tensor(out=outb[:, :, 0:half], in0=ta, in1=tb, op=ALU.subtract), r=['kp_r4a', 'kp_r4b'], w=[m + '_out'])
                k.op('dve', lambda e: e.tensor_tensor(out=ta, in0=x2v, in1=cosv, op=ALU.mult), r=rk, w=['kp_r4a'])
                k.op('pool', lambda e: e.tensor_tensor(out=tb, in0=x1v, in1=sinv, op=ALU.mult), r=rk, w=['kp_r4b'])
                k.op('dve', lambda e: e.tensor_tensor(out=outb[:, :, half:2 * half], in0=ta, in1=tb, op=ALU.add), r=['kp_r4a', 'kp_r4b'], w=[m + '_out'])

            def post_k(name, b, ps, psk):
                b4 = b % 4; g4 = b // 4; i2 = g4 % 2
                if name.startswith('fk'):
                    h0 = int(name[2:]) * 4
                    post_norm_heads(tag, env, ps, psk, 4, 'gk_f', None)
                    stg = env['stgT'][i2]; stgk = tag + 'stgT%d' % i2
                    transpose_heads_to_stg(tag, env, 4, b4, stg, stgk)
                    if b4 == 3:
                        k.dma('pool', stgk, fkT[h0:h0 + 4, :, g4 * 512:(g4 + 1) * 512].rearrange("h d t -> d h t"), stg[:, :, :], r=[stgk], w=['fkT'])
                elif name.startswith('fv'):
                    h0 = int(name[2:]) * 4
                    stg = env['stgV'][i2]; stgk = tag + 'stgV%d' % i2
                    k.op('act', lambda e: e.copy(out=stg[:, :, b4, :], in_=ps[:, 0:512].rearrange("p (h d) -> p h d", h=4)), r=[], w=[psk, stgk])
                    if b4 == 3:
                        k.dma('pool', stgk, fv[h0:h0 + 4, :, g4 * 4:(g4 + 1) * 4, :].rearrange("h j b d -> j h b d"), stg[:, :, :, :], r=[stgk], w=['fv'])
                else:
                    xm = xms[i2]; xk = 'kp_xm%d' % i2
                    k.op('act', lambda e: e.copy(out=xm[:, b4, :], in_=ps[:, 0:336]), r=[], w=[psk, xk])
                    if b4 != 3:
                        return
                    b0 = g4 * 4
                    k.op('dve', lambda e: e.tensor_tensor(out=nlf4[:, :, :], in0=xm[:, :, 0:16], in1=fbt[:, :].unsqueeze(1).to_broadcast([128, 4, NH]), op=ALU.add), r=[xk, 'kp_fb'], w=['kp_nlf'])
                    k.op('act', lambda e: e.activation(out=nlf4[:, :, :], in_=nlf4[:, :, :], func=AF.Exp, scale=-1.0), r=[], w=['kp_nlf'])
                    k.op('act', lambda e: e.activation(out=nlf4[:, :, :], in_=nlf4[:, :, :], func=AF.Ln, bias=1.0), r=[], w=['kp_nlf'])
                    nl2 = nlf4[:, :, :].rearrange("p b h -> p (b h)")
                    k.op('pe', lambda e: e.matmul(psc[:, 0:64], lhsT=tri_f[:], rhs=nl2, start=True, stop=True), r=['tri_f', 'kp_nlf'], w=['kp_psc'])
                    k.op('pe', lambda e: e.matmul(psc[:, 64:128], lhsT=ones_f[:], rhs=nl2, start=True, stop=True), r=['ones_f', 'kp_nlf'], w=['kp_psc'])
                    k.op('dve', lambda e: e.tensor_copy(out=cs4[:, :], in_=psc[:, 0:128]), r=[], w=['kp_psc', 'kp_cs4'])
                    for j in range(4):
                        bb = b0 + j
                        k.op('dve', lambda e: e.tensor_tensor(out=nc_all[:, bb * NH:(bb + 1) * NH], in0=cs4[:, j * NH:(j + 1) * NH], in1=prefix[:], op=ALU.add), r=['kp_cs4', 'kp_prefix'], w=['nc_all'])
                        k.op('dve', lambda e: e.tensor_tensor(out=prefix[:], in0=cs4[:, 64 + j * NH:64 + (j + 1) * NH], in1=prefix[:], op=ALU.add), r=['kp_cs4'], w=['kp_prefix'])
                    xk_ = xm[:, :, 16:144]
                    k.op('pool', lambda e: e.tensor_tensor(out=sq4[:, :, :], in0=xk_, in1=xk_, op=ALU.mult), r=[xk], w=['kp_sq4'])
                    k.op('dve', lambda e: e.tensor_reduce(out=ss4[:, 0:4], in_=sq4[:, :, :], axis=AX.X, op=ALU.add), r=['kp_sq4'], w=['kp_ss4'])
                    k.op('dve', lambda e: e.tensor_scalar(out=ss4[:, 4:8], in0=ss4[:, 0:4], scalar1=1.0 / HD, scalar2=EPS, op0=ALU.mult, op1=ALU.add), r=[], w=['kp_ss4'])
                    k.op('act', lambda e: e.activation(out=ss4[:, 8:12], in_=ss4[:, 4:8], func=AF.Ln), r=[], w=['kp_ss4'])
                    k.op('act', lambda e: e.activation(out=ss4[:, 12:16], in_=ss4[:, 8:12], func=AF.Exp, scale=-0.5), r=[], w=['kp_ss4'])
                    gt_, gk_ = env['gt']['gk_d']
                    k.op('dve', lambda e: e.tensor_tensor(out=sq4[:, :, :], in0=xk_, in1=ss4[:, 12:16].unsqueeze(2).to_broadcast([128, 4, 128]), op=ALU.mult), r=[xk, 'kp_ss4'], w=['kp_sq4'])
                    k.op('dve', lambda e: e.tensor_tensor(out=sq4[:, :, :], in0=sq4[:, :, :], in1=gt_[:, :].unsqueeze(1).to_broadcast([128, 4, 128]), op=ALU.mult), r=[gk_], w=['kp_sq4'])
                    cos4 = env['cos'][:, b0 * 64:(b0 + 4) * 64].rearrange("p (b d) -> p b d", b=4); sin4 = env['sin'][:, b0 * 64:(b0 + 4) * 64].rearrange("p (b d) -> p b d", b=4)
                    rope4('kpd', dkb4, sq4, 'kp_sq4', cos4, sin4, 64)
                    sd = stgD[i2]; sdk = 'kp_stgD%d' % i2
                    pt = env['ptT']
                    for j in range(4):
                        k.op('pe', lambda e: e.transpose(pt[:, j * 128:(j + 1) * 128], dkb4[:, j, :], ident_b[:]), r=['kpd_out', 'ident_b'], w=[tag + 'ptT'])
                    k.op('act', lambda e: e.copy(out=sd[:, 0, :], in_=pt[:, 0:512]), r=[], w=[tag + 'ptT', sdk])
                    k.dma('pool', sdk, dkT[:, g4 * 512:(g4 + 1) * 512], sd[:, 0, :], r=[sdk], w=['dkT'])
                    sv = stgDV[i2]; svk = 'kp_stgDV%d' % i2
                    k.op('act', lambda e: e.copy(out=sv[:, :, :], in_=xm[:, :, 144:272]), r=[xk], w=[svk])
                    k.dma('pool', svk, dv[:, g4 * 4:(g4 + 1) * 4, :], sv[:, :, :], r=[svk], w=['dv'])
                    cos24 = env['cos2'][:, b0 * 32:(b0 + 4) * 32].rearrange("p (b d) -> p b d", b=4); sin24 = env['sin2'][:, b0 * 32:(b0 + 4) * 32].rearrange("p (b d) -> p b d", b=4)
                    rope4('kpi', ikb4, xm[:, :, 272:336], xk, cos24, sin24, 32)
                    si = stgI[i2]; sik = 'kp_stgI%d' % i2
                    for j in range(4):
                        k.op('pe', lambda e: e.transpose(pt[0:64, j * 128:(j + 1) * 128], ikb4[:, j, :], ident_b[:]), r=['kpi_out', 'ident_b'], w=[tag + 'ptT'])
                    k.op('act', lambda e: e.copy(out=si[:, 0, :], in_=pt[0:64, 0:512]), r=[], w=[tag + 'ptT', sik])
                    k.dma('pool', sik, ikT[:, g4 * 512:(g4 + 1) * 512], si[:, 0, :], r=[sik], w=['ikT'])

            k.lastw['kpitab'] = k.lastw.get(tag + 'tab', (None, 0))
            chunks = [('fk%d' % i, [(0, C_FK + i * 512, 512)], 512) for i in range(4)]
            chunks += [('fv%d' % i, [(0, C_FV + i * 512, 512)], 512) for i in range(4)]
            chunks += [('mixed', [(0, C_FF, 16), (16, C_DK, 128), (144, C_DV, 128), (272, C_IK, 64)], 336)]
            if 'kchunks' in dbg:
                chunks = [c for c in chunks if c[0] in dbg['kchunks']]
            project(tag, [hT_all[b] for b in range(dbg.get('nkb', NTB))], dbg.get('nkb', NTB), w_in, chunks, post_k, nht_buf=2, nps_buf=4,
                    pre=lambda nm: env['gen_tables']() if nm == chunks[min(1, len(chunks) - 1)][0] else None)
            if 'dbg_nc' in dbg.get('dump', ()):
                k.dma('sp', 'dbgnc', dbg_nc[:, :], nc_all[:], r=['nc_all'], w=['dbg_nc'])
            k.barrier()
          k.mem = es


        if STOP >= 3:
          with ExitStack() as ph:
            k.mem = ph
            tag = "qp"
            posi = k.sb("qp_posi", [128, NSLOT], I32); posf = k.sb("qp_posf", [128, NSLOT], F32)
            k.dma('sp', 'qp_pos', posi[:], pos_own[:, :], w=['qp_posi'])
            k.op('dve', lambda e: e.tensor_copy(out=posf[:], in_=posi[:]), r=['qp_posi'], w=['qpposf'])
            k.lastw['qqposf'] = k.lastw['qpposf']
            env = make_post_env(tag, NSLOT, posf, 'qpposf', stg=False)
            load_gain(tag, env, 'gq_f', gq_f, HD ** -0.5); load_gain(tag, env, 'gq_d', gq_d, HD ** -0.5)
            envB = make_post_env("qq", NSLOT, posf, 'qqposf', stg=False, share=env)
            load_gain("qq", envB, 'gq_f', gq_f, HD ** -0.5); load_gain("qq", envB, 'gq_d', gq_d, HD ** -0.5)
            envs2 = ((tag, env), ("qq", envB))
            iqf = k.sb("qp_iqf", [128, 8, 64], F32); iqr = k.sb("qp_iqr", [128, 8, 64], F32); iqb = k.sb("qp_iqb", [128, 8, 64], BF16)
            stgQ = [k.sb("qp_stgQ%d" % i, [128, 4, 128], BF16) for i in range(2)]
            stgIQ = [k.sb("qp_stgIQ%d" % i, [64, 8, 128], BF16) for i in range(2)]
            k.lastw['qpitab'] = k.lastw.get(tag + 'tab', (None, 0))
            cnt = {'n': 0}

            def post_q(name, b, ps, psk):
                i2 = cnt['n'] % 2; cnt['n'] += 1
                if name == 'iw':
                    k.op('dve', lambda e: e.tensor_scalar(out=aw[:, b * NH:(b + 1) * NH], in0=ps[:, 0:16], scalar1=0.03125, scalar2=None, op0=ALU.mult), r=[], w=[psk, 'aw'])
                elif name.startswith('fq') or name.startswith('dq'):
                    h0 = int(name[2:]) * 4
                    isd = name.startswith('dq')
                    tg_, en_ = envs2[b % 2]
                    post_norm_heads(tg_, en_, ps, psk, 4, 'gq_d' if isd else 'gq_f', b if isd else None)
                    stg = stgQ[i2]; stgk = 'qp_stgQ%d' % i2
                    pt = en_['ptT']
                    for h in range(4):
                        k.op('pe', lambda e: e.transpose(pt[:, h * 128:(h + 1) * 128], en_['yb'][:, h, :], ident_b[:]), r=[tg_ + 'yb', 'ident_b'], w=[tg_ + 'ptT'])
                    k.op('act', lambda e: e.copy(out=stg[:, :, :], in_=pt[:, 0:512].rearrange("p (h t) -> p h t", h=4)), r=[], w=[tg_ + 'ptT', stgk])
                    if isd:
                        k.dma('pool', stgk, dqT[:, b, h0:h0 + 4, :], stg[:, :, :], r=[stgk], w=['dqT'])
                    else:
                        k.dma('pool', stgk, fqT[h0:h0 + 4, :, b * 128:(b + 1) * 128].rearrange("h d t -> d h t"), stg[:, :, :], r=[stgk], w=['fqT'])
                else:
                    h0 = int(name[2:]) * 8
                    k.op('act', lambda e: e.copy(out=iqf[:, :, :], in_=ps[:, 0:512].rearrange("p (h d) -> p h d", h=8)), r=[], w=[psk, 'qpiropein'])
                    rope_apply('pool', 'qpi', iqr[:, :, :], iqf[:, :, :], env['cos2'][:, b * 32:(b + 1) * 32], env['sin2'][:, b * 32:(b + 1) * 32], 8, 32,
                               env['t1'][:, :, 0:32], env['t2'][:, :, 0:32])
                    k.op('pool', lambda e: e.tensor_tensor(out=iqb[:, :, :], in0=iqr[:, :, :],
                                                          in1=aw[:, b * NH + h0:b * NH + h0 + 8].unsqueeze(2).to_broadcast([128, 8, 64]), op=ALU.mult),
                         r=['qpiropeout', 'aw'], w=['qp_iqb'])
                    pt = env['ptT']
                    for h in range(8):
                        k.op('pe', lambda e: e.transpose(pt[0:64, h * 128:(h + 1) * 128], iqb[:, h, :], ident_b[:]), r=['qp_iqb', 'ident_b'], w=[tag + 'ptT'])
                    stg = stgIQ[i2]; stgk = 'qp_stgIQ%d' % i2
                    k.op('act', lambda e: e.copy(out=stg[:, :, :], in_=pt[0:64, 0:1024].rearrange("p (h t) -> p h t", h=8)), r=[], w=[tag + 'ptT', stgk])
                    k.dma('pool', stgk, iqT[:, b, h0:h0 + 8, :], stg[:, :, :], r=[stgk], w=['iqT'])

            chunks = [('iw', [(0, C_IW, 16)], 16)]
            chunks += [('iq%d' % i, [(0, C_IQ + i * 512, 512)], 512) for i in range(2)]
            chunks += [('fq%d' % i, [(0, C_FQ + i * 512, 512)], 512) for i in range(4)]
            chunks += [('dq%d' % i, [(0, C_DQ + i * 512, 512)], 512) for i in range(4)]
            project(tag, [hT_own[b] for b in range(NSLOT)], NSLOT, w_in, chunks, post_q, resident=True)
            k.barrier()
          k.mem = es

        if STOP >= 4:
            kio = k.sb("kio", [128, 1024], F32)
            k.dma('sp', 'kio', kio[:], c_kiota[0:1, :].partition_broadcast(128), w=['kio'])
            qrel = k.sb("qrel", [128, NSLOT], F32)
            for s in range(NSLOT):
                k.op('dve', lambda e: e.tensor_scalar(out=qrel[:, s:s + 1], in0=qidx_t[:, s:s + 1], scalar1=float(-1024 * s), scalar2=None, op0=ALU.add), r=['qidx_t'], w=['qrel'])
            NMf = k.sb("NMf", [128, NSLOT, 1024], BF16)
            for s in range(NSLOT):
                k.op('dve', lambda e: e.tensor_scalar(out=NMf[:, s, :], in0=kio[:], scalar1=qrel[:, s:s + 1], scalar2=NEG, op0=ALU.is_gt, op1=ALU.mult), r=['kio', 'qrel'], w=['NMf'])
            cT = k.sb("cT", [16, 1024], BF16)
            with ExitStack() as ph:
                k.mem = ph
                selb = k.sb("selb", [128, NSLOT * NTB], F32); tmpc = k.sb("tmpc", [128, NH, NTB], F32)
                nco = k.sb("nco", [128, NH], F32); ncb = k.sb("ncb", [128, NH], BF16)
                ptc = k.ps("ptc", [128, 1024], BF16)
                k.dma('sp', 'selb', selb[:], selblk[0:1, :].partition_broadcast(128), w=['selb'])
                for s in range(NSLOT):
                    k.op('dve', lambda e: e.tensor_tensor(out=tmpc[:, :, :], in0=nc_all[:, :].rearrange("p (tb h) -> p h tb", h=NH),
                                                         in1=selb[:, s * NTB:(s + 1) * NTB].unsqueeze(1).to_broadcast([128, NH, NTB]), op=ALU.mult), r=['nc_all', 'selb'], w=['tmpc'])
                    k.op('dve', lambda e: e.tensor_reduce(out=nco[:, :], in_=tmpc[:, :, :], axis=AX.X, op=ALU.add), r=['tmpc'], w=['nco'])
                    k.op('dve', lambda e: e.tensor_scalar(out=ncb[:, :], in0=nco[:, :], scalar1=-1.0, scalar2=None, op0=ALU.mult), r=['nco'], w=['ncb'])
                    k.op('pe', lambda e: e.transpose(ptc[0:16, s * 128:(s + 1) * 128], ncb[:, :], ident_b[:]), r=['ncb', 'ident_b'], w=['ptc'])
                k.op('act', lambda e: e.copy(out=cT[:, :], in_=ptc[0:16, :]), r=[], w=['ptc', 'cT'])
                k.barrier()
            k.mem = es

        def attn_evacuate(tagp, accs, acckeys, stg, stgk, rden):
            for s_ in range(len(accs)):
                acc = accs[s_]
                k.op('dve', lambda e: e.reciprocal(out=rden[:, s_:s_ + 1], in_=acc[:, 128:129]), r=[], w=[acckeys[s_], tagp + 'rden'])
                k.op('dve', lambda e: e.tensor_scalar(out=stg[:, s_, :], in0=acc[:, 0:128], scalar1=rden[:, s_:s_ + 1], scalar2=None, op0=ALU.mult),
                     r=[tagp + 'rden'], w=[acckeys[s_], stgk])

        bg = {'gen': None}
        if STOP >= 4:
            cast_scope = ExitStack()
            k.mem = cast_scope
            cfs = [k.sb("cst_f%d" % i, [128, 2048], F32) for i in range(2)]
            cbs = [k.sb("cst_b%d" % i, [128, 2048], BF16) for i in range(2)]
            k.mem = es

            def cast_gen():
                items = []
                for wi, srcw in enumerate((pu, pv)):
                    for rb in range(128):
                        for hf_ in range(2):
                            items.append((srcw[rb * 128:(rb + 1) * 128, hf_ * 2048:(hf_ + 1) * 2048],
                                          uvb[rb * 128:(rb + 1) * 128, wi * D + hf_ * 2048:wi * D + (hf_ + 1) * 2048]))
                n = len(items)
                for i in range(n + 1):
                    if i < n:
                        fi = i % 2
                        k.dma('pool', 'cst_f%d' % fi, cfs[fi][:], items[i][0], w=['cst_f%d' % fi])
                    if i >= 1:
                        j = i - 1; fi = j % 2; bi = j % 2
                        eng = 'pool' if j % 3 == 0 else 'dve'
                        k.op(eng, lambda e: e.tensor_copy(out=cbs[bi][:], in_=cfs[fi][:]), r=['cst_f%d' % fi], w=['cst_b%d' % bi])
                        k.dma('pool', 'cst_b%d' % bi, items[j][1], cbs[bi][:], r=['cst_b%d' % bi], w=['ubvb'])
                    yield
            bg['gen'] = cast_gen()

        def bg_pull(n=1):
            g_ = bg['gen']
            if g_ is None:
                return
            for _ in range(n):
                try:
                    next(g_)
                except StopIteration:
                    bg['gen'] = None
                    return

        if STOP >= 4:
          with ExitStack() as ph:
            k.mem = ph
            kTs = [k.sb("fx_kT%d" % i, [128, S], BF16) for i in range(2)]
            Vs = [k.sb("fx_V%d" % i, [128, NTB, 129], BF16) for i in range(2)]
            qTt = [k.sb("fx_qT%d" % i, [128, 1024], BF16) for i in range(2)]
            Ps = [k.sb("fx_P%d" % i, [128, 512], BF16) for i in range(4)]
            stgs = [k.sb("fx_stg%d" % i, [128, NSLOT, 128], BF16) for i in range(2)]
            rden = k.sb("fx_rden", [128, 8], F32)
            Sb = [k.ps("fx_S%d" % i, [128, 512], F32) for i in range(4)]
            Ab1 = [k.ps("fx_A0_%d" % j, [128, 512], F32) for j in range(3)]
            Ab = [Ab1, Ab1]
            for i in range(2):
                k.op('pool', lambda e: e.memset(Vs[i][:, :, 128:129], 1.0), w=['fx_V%d' % i])
            nS = 0; nP = 0
            for h in range(NH):
                hi = h % 2
                kT = kTs[hi]; V = Vs[hi]; qT = qTt[hi]
                k.dma('sp', 'fx_kT%d' % hi, kT[:], fkT[h], w=['fx_kT%d' % hi])
                k.dma('sp', 'fx_V%d' % hi, V[:, :, 0:128], fv[h], w=['fx_V%d' % hi])
                k.dma('sp', 'fx_qT%d' % hi, qT[:], fqT[h], w=['fx_qT%d' % hi])
                accs = []; acckeys = []
                for s in range(NSLOT):
                    accs.append(Ab[0][s // 3][:, (s % 3) * 160:(s % 3) * 160 + 129]); acckeys.append('fx_A0_%d' % (s // 3))
                for j in range(3):
                    k.op('dve', lambda e: e.memset(Ab[0][j][:, :], 0.0), w=['fx_A0_%d' % j])
                tiles = [(g, kb) for g in range(2) for kb in range(32 * (g + 1))]

                def stageA(n):
                    g, kb = tiles[n]
                    smin = max(4 * g, kb // 8)
                    c0 = smin * 128; c1 = (4 * g + 4) * 128; nn = c1 - c0
                    Sp = Sb[n % 4]; Sk = 'fx_S%d' % (n % 4)
                    P = Ps[n % 4]; Pk = 'fx_P%d' % (n % 4)
                    zone = (kb // 8) >= 4 * g
                    k.op('pe', lambda e: e.matmul(Sp[:, 0:nn], lhsT=kT[:, kb * 128:(kb + 1) * 128], rhs=qT[:, c0:c1], start=True, stop=False),
                         r=['fx_kT%d' % hi, 'fx_qT%d' % hi], w=[Sk])
                    k.op('pe', lambda e: e.matmul(Sp[:, 0:nn], lhsT=sel16[:, h * 128:(h + 1) * 128], rhs=cT[:, c0:c1], start=False, stop=not zone),
                         r=['sel16', 'cT'], w=[Sk])
                    if zone:
                        sz = kb // 8; z = kb - 8 * sz
                        k.op('pe', lambda e: e.matmul(Sp[:, 0:128], lhsT=NMf[:, sz, z * 128:(z + 1) * 128], rhs=ident_b[:], start=False, stop=True),
                             r=['NMf', 'ident_b'], w=[Sk])
                    k.op('act', lambda e: e.activation(out=P[:, 0:nn], in_=Sp[:, 0:nn], func=AF.Exp, bias=nc_all[:, kb * NH + h:kb * NH + h + 1], scale=1.0),
                         r=['nc_all'], w=[Sk, Pk])

                def stageB(n):
                    g, kb = tiles[n]
                    smin = max(4 * g, kb // 8)
                    P = Ps[n % 4]; Pk = 'fx_P%d' % (n % 4)
                    for s in range(smin, 4 * g + 4):
                        k.op('pe', lambda e: e.matmul(accs[s], lhsT=P[:, (s - smin) * 128:(s - smin + 1) * 128], rhs=V[:, kb, :], start=False, stop=False, skip_group_check=True),
                             r=[Pk, 'fx_V%d' % hi], w=[acckeys[s]])
                LA = 2
                for n in range(len(tiles) + LA):
                    if n < len(tiles):
                        stageA(n)
                    if n >= LA:
                        stageB(n - LA)
                    if n % 6 == 5:
                        bg_pull()
                stg = stgs[hi]; stgk = 'fx_stg%d' % hi
                attn_evacuate('fx', accs, acckeys, stg, stgk, rden)
                k.dma('pool', stgk, mix[:, :, h * 128:(h + 1) * 128].rearrange("s i d -> i s d"), stg[:, :, :], r=[stgk], w=['mix'])
            k.barrier()
          k.mem = es

        if STOP >= 5:
          with ExitStack() as ph:
            k.mem = ph
            dkt = k.sb("ds_kT", [128, S], BF16); dvt = k.sb("ds_V", [128, NTB, 129], BF16); ikt = k.sb("ds_ik", [64, S], BF16)
            dqs = [k.sb("ds_q%d" % i, [128, NH, 128], BF16) for i in range(2)]
            iqs = [k.sb("ds_iq%d" % i, [64, NH, 128], BF16) for i in range(2)]
            score = k.sb("ds_score", [128, S], F32); NMd = k.sb("ds_NM0", [128, S], BF16)
            rl = [k.sb("ds_rl%d" % i, [128, 512], F32) for i in range(2)]
            Ps = [k.sb("ds_P%d" % i, [128, 512], BF16) for i in range(4)]
            stgs = [k.sb("ds_stg%d" % i, [128, 4, 128], BF16) for i in range(2)]
            bs = k.sb("ds_bs", [128, 16], F32)
            tau_all = k.sb("ds_tau", [128, NSLOT], F32)
            rden = k.sb("ds_rden", [128, 8], F32)
            Lb = [k.ps("ds_L%d" % i, [128, 512], F32) for i in range(2)]
            Sb = [k.ps("ds_S%d" % i, [128, 512], F32) for i in range(4)]
            Ab1 = [k.ps("ds_A0_%d" % j, [128, 512], F32) for j in range(2)]
            Ab = [Ab1, Ab1]
            k.dma('sp', 'ds_kT', dkt[:], dkT[:, :], w=['ds_kT'])
            k.op('pool', lambda e: e.memset(dvt[:, :, 128:129], 1.0), w=['ds_V'])
            k.dma('sp', 'ds_V', dvt[:, :, 0:128], dv[:, :, :], w=['ds_V'])
            k.dma('sp', 'ds_ik', ikt[:], ikT[:, :], w=['ds_ik'])
            sgv = k.sb("sgn", [128, NSLOT * NH], F32)
            k.op('dve', lambda e: e.tensor_scalar(out=sgv[:], in0=aw[:], scalar1=0.0, scalar2=2.0, op0=ALU.is_ge, op1=ALU.mult), r=['aw'], w=['sgn'])
            k.op('dve', lambda e: e.tensor_scalar(out=sgv[:], in0=sgv[:], scalar1=-1.0, scalar2=None, op0=ALU.add), r=[], w=['sgn'])
            NMds = [NMd, k.sb("ds_NM1", [128, S], BF16)]
            cnt_ = {'L': 0, 'R': 0, 'S': 0, 'A': 0}

            def prep_slot(s):
                si = s % 2; L = 1024 * (s + 1)
                dq = dqs[si]; iq = iqs[si]; NMs = NMds[si]; NMk = 'ds_NM%d' % si
                k.dma('sp', 'ds_q%d' % si, dq[:], dqT[:, s, :, :], w=['ds_q%d' % si])
                k.dma('sp', 'ds_iq%d' % si, iq[:], iqT[:, s, :, :], w=['ds_iq%d' % si])
                for kc in range(L // 512):
                    for h in range(NH):
                        nL = cnt_['L']; cnt_['L'] += 1; nR = cnt_['R']; cnt_['R'] += 1
                        Lp = Lb[nL % 2]; Lk = 'ds_L%d' % (nL % 2)
                        R = rl[nR % 2]; Rk = 'ds_rl%d' % (nR % 2)
                        k.op('pe', lambda e: e.matmul(Lp[:, :], lhsT=iq[:, h, :], rhs=ikt[:, kc * 512:(kc + 1) * 512], start=True, stop=True),
                             r=['ds_iq%d' % si, 'ds_ik'], w=[Lk])
                        k.op('act', lambda e: e.activation(out=R[:, :], in_=Lp[:, :], func=AF.Relu, scale=sgv[:, s * NH + h:s * NH + h + 1]), r=['sgn'], w=[Lk, Rk])
                        if h == 0:
                            k.op('dve', lambda e: e.tensor_scalar(out=score[:, kc * 512:(kc + 1) * 512], in0=R[:, :], scalar1=sgv[:, s * NH + h:s * NH + h + 1], scalar2=None, op0=ALU.mult),
                                 r=[Rk, 'sgn'], w=['ds_score'])
                        else:
                            k.op('dve', lambda e: e.scalar_tensor_tensor(out=score[:, kc * 512:(kc + 1) * 512], in0=R[:, :], scalar=sgv[:, s * NH + h:s * NH + h + 1],
                                                                        in1=score[:, kc * 512:(kc + 1) * 512], op0=ALU.mult, op1=ALU.add), r=[Rk, 'sgn'], w=['ds_score'])
                        yield
                k.op('dve', lambda e: e.tensor_reduce(out=bs[:, 0:1], in_=score[:, 0:L], axis=AX.X, op=ALU.max), r=['ds_score'], w=['ds_bs'])
                k.op('dve', lambda e: e.tensor_reduce(out=bs[:, 1:2], in_=score[:, 0:L], axis=AX.X, op=ALU.min), r=['ds_score'], w=['ds_bs'])
                k.op('dve', lambda e: e.tensor_tensor(out=bs[:, 2:3], in0=bs[:, 0:1], in1=bs[:, 1:2], op=ALU.subtract), r=[], w=['ds_bs'])
                k.op('dve', lambda e: e.tensor_scalar(out=bs[:, 2:3], in0=bs[:, 2:3], scalar1=1.0001, scalar2=2e-3, op0=ALU.mult, op1=ALU.add), r=[], w=['ds_bs'])
                k.op('dve', lambda e: e.tensor_scalar(out=bs[:, 3:4], in0=bs[:, 1:2], scalar1=-1.0, scalar2=1e-3, op0=ALU.mult, op1=ALU.add), r=[], w=['ds_bs'])
                yield
                k.op('pool', lambda e: e.tensor_scalar(out=rl[0][:, :], in0=kio[:, 0:512], scalar1=qrel[:, s:s + 1], scalar2=-1e30, op0=ALU.is_gt, op1=ALU.mult), r=['kio', 'qrel'], w=['ds_rl0'])
                k.op('pool', lambda e: e.tensor_scalar(out=rl[1][:, :], in0=kio[:, 512:1024], scalar1=qrel[:, s:s + 1], scalar2=-1e30, op0=ALU.is_gt, op1=ALU.mult), r=['kio', 'qrel'], w=['ds_rl1'])
                k.op('dve', lambda e: e.tensor_tensor(out=score[:, L - 1024:L - 512], in0=score[:, L - 1024:L - 512], in1=rl[0][:, :], op=ALU.add), r=['ds_rl0'], w=['ds_score'])
                k.op('dve', lambda e: e.tensor_tensor(out=score[:, L - 512:L], in0=score[:, L - 512:L], in1=rl[1][:, :], op=ALU.add), r=['ds_rl1'], w=['ds_score'])
                yield
                thr = float(2 * TOPK - L) - 0.5
                for it in range(NBIS):
                    f = -(2.0 ** -(it + 1))
                    k.op('dve', lambda e: e.tensor_scalar(out=bs[:, 4:5], in0=bs[:, 2:3], scalar1=float(f), scalar2=None, op0=ALU.mult), r=[], w=['ds_bs'])
                    k.op('dve', lambda e: e.tensor_tensor(out=bs[:, 5:6], in0=bs[:, 4:5], in1=bs[:, 3:4], op=ALU.add), r=[], w=['ds_bs'])
                    k.op('act', lambda e: e.activation(out=NMs[:, 0:L], in_=score[:, 0:L], func=AF.Sign, bias=bs[:, 5:6], scale=1.0, accum_out=bs[:, 6:7]),
                         r=['ds_score', 'ds_bs'], w=[NMk, 'ds_bs2'])
                    k.op('dve', lambda e: e.scalar_tensor_tensor(out=bs[:, 7:8], in0=bs[:, 6:7], scalar=thr, in1=bs[:, 4:5], op0=ALU.is_ge, op1=ALU.mult), r=['ds_bs2'], w=['ds_bs'])
                    k.op('dve', lambda e: e.tensor_tensor(out=bs[:, 3:4], in0=bs[:, 3:4], in1=bs[:, 7:8], op=ALU.add), r=[], w=['ds_bs'])
                    yield
                k.op('dve', lambda e: e.tensor_scalar(out=tau_all[:, s:s + 1], in0=bs[:, 3:4], scalar1=-1.0, scalar2=None, op0=ALU.mult), r=['ds_bs'], w=['ds_tau'])
                k.op('dve', lambda e: e.tensor_scalar(out=NMs[:, 0:L], in0=score[:, 0:L], scalar1=tau_all[:, s:s + 1], scalar2=NEG, op0=ALU.is_lt, op1=ALU.mult),
                     r=['ds_score', 'ds_tau'], w=[NMk])
                yield

            def prep_steps(s):
                return 16 * 2 * (s + 1) + 3 + NBIS + 1

            def attend_slot(s, gnext, nsteps_next):
                si = s % 2; nkb = 8 * (s + 1)
                dq = dqs[si]; NMs = NMds[si]; NMk = 'ds_NM%d' % si
                ntiles = 4 * (nkb + 2); done_t = 0; done_bg = 0
                for hg in range(4):
                    ai = cnt_['A'] % 2; cnt_['A'] += 1
                    accs = [Ab[0][j // 2][:, (j % 2) * 160:(j % 2) * 160 + 129] for j in range(4)]
                    acckeys = ['ds_A0_%d' % (j // 2) for j in range(4)]
                    for j in range(2):
                        k.op('dve', lambda e: e.memset(Ab[0][j][:, :], 0.0), w=['ds_A0_%d' % j])
                    nS = cnt_['S']

                    def stageA(kb):
                        n = nS + kb
                        Sp = Sb[n % 4]; Sk = 'ds_S%d' % (n % 4)
                        P = Ps[n % 4]; Pk = 'ds_P%d' % (n % 4)
                        k.op('pe', lambda e: e.matmul(Sp[:, :], lhsT=dkt[:, kb * 128:(kb + 1) * 128], rhs=dq[:, hg * 4:(hg + 1) * 4, :], start=True, stop=False),
                             r=['ds_kT', 'ds_q%d' % si], w=[Sk])
                        for hh in range(4):
                            k.op('pe', lambda e: e.matmul(Sp[:, hh * 128:(hh + 1) * 128], lhsT=NMs[:, kb * 128:(kb + 1) * 128], rhs=ident_b[:], start=False, stop=(hh == 3)),
                                 r=[NMk, 'ident_b'], w=[Sk])
                        k.op('act', lambda e: e.activation(out=P[:, :], in_=Sp[:, :], func=AF.Exp), r=[], w=[Sk, Pk])

                    def stageB(kb):
                        n = nS + kb
                        P = Ps[n % 4]; Pk = 'ds_P%d' % (n % 4)
                        for hh in range(4):
                            k.op('pe', lambda e: e.matmul(accs[hh], lhsT=P[:, hh * 128:(hh + 1) * 128], rhs=dvt[:, kb, :], start=False, stop=False, skip_group_check=True),
                                 r=[Pk, 'ds_V'], w=[acckeys[hh]])
                    LA = 2
                    for kb in range(nkb + LA):
                        if kb < nkb:
                            stageA(kb)
                        if kb >= LA:
                            stageB(kb - LA)
                        if kb % 4 == 3:
                            bg_pull()
                        done_t += 1
                        if gnext is not None:
                            want = (done_t * nsteps_next) // ntiles
                            while done_bg < want:
                                done_bg += 1
                                try:
                                    next(gnext)
                                except StopIteration:
                                    gnext = None
                                    break
                    cnt_['S'] += nkb
                    stg = stgs[ai]; stgk = 'ds_stg%d' % ai
                    attn_evacuate('ds', accs, acckeys, stg, stgk, rden)
                    k.dma('pool', stgk, mix[s, :, 2048 + hg * 512:2048 + (hg + 1) * 512], stg[:, :, :], r=[stgk], w=['mix'])
                if gnext is not None:
                    for _ in gnext:
                        pass

            for _ in prep_slot(0):
                pass
            for s in range(NSLOT):
                gnext = prep_slot(s + 1) if s + 1 < NSLOT else None
                attend_slot(s, gnext, prep_steps(s + 1) if s + 1 < NSLOT else 0)
            if 'dbg_tau' in dbg.get('dump', ()):
                k.dma('sp', 'dbgtau', dbg_tau[:, :], tau_all[:], r=['ds_tau'], w=['dbg_tau'])
            k.barrier()
          k.mem = es

        bg_pull(10000)
        if STOP >= 4:
            k.barrier()
            cast_scope.close()
        if STOP >= 6:
            norm_transpose_phase("mt", [mix[b] for b in range(NSLOT)], NSLOT, None, [mixT[b] for b in range(NSLOT)], do_norm=False, src_bf16=True)
            with ExitStack() as ph:
                k.mem = ph
                xos = [k.sb("wo_xo%d" % i, [128, 512], F32) for i in range(2)]
                cw = {'n': 0}

                def post_wo(name, b, ps, psk):
                    c = int(name[2:]); i2 = cw['n'] % 2; cw['n'] += 1
                    xo = xos[i2]; xk = 'wo_xo%d' % i2
                    k.dma('sp', xk, xo[:], x_own[b * 128:(b + 1) * 128, c * 512:(c + 1) * 512], w=[xk])
                    k.op('dve', lambda e: e.tensor_tensor(out=xo[:], in0=ps[:, :], in1=xo[:], op=ALU.add), r=[], w=[psk, xk])
                    k.dma('pool', xk + 'o', x1[b * 128:(b + 1) * 128, c * 512:(c + 1) * 512], xo[:], r=[xk], w=['x1'])
                project("wo", [mixT[b] for b in range(NSLOT)], NSLOT, w_o, [('wo%d' % i, [(0, i * 512, 512)], 512) for i in range(8)], post_wo, resident=True)
                k.barrier()
            k.mem = es

        if STOP >= 7:
            norm_transpose_phase("n2", [x1[b * 128:(b + 1) * 128, :] for b in range(NSLOT)], NSLOT, ln2_g, [h2T[b] for b in range(NSLOT)],
                                 dst_b16=[h2b[b * 128:(b + 1) * 128, :] for b in range(NSLOT)])
            with ExitStack() as ph:
                k.mem = ph
                yb = k.sb("pq_yb", [128, 4, 128], BF16); stgq = [k.sb("pq_stg%d" % i, [128, 4, 128], BF16) for i in range(2)]
                ptq = k.ps("pq_pt", [128, 1024], BF16)
                cq = {'n': 0}

                def post_pq(name, b, ps, psk):
                    c = int(name[2:]); i2 = cq['n'] % 2; cq['n'] += 1
                    k.op('act', lambda e: e.copy(out=yb[:, :, :], in_=ps[:, 0:512].rearrange("p (h d) -> p h d", h=4)), r=[], w=[psk, 'pq_yb'])
                    for h in range(4):
                        k.op('pe', lambda e: e.transpose(ptq[:, h * 128:(h + 1) * 128], yb[:, h, :], ident_b[:]), r=['pq_yb', 'ident_b'], w=['pq_pt'])
                    stg = stgq[i2]; stgk = 'pq_stg%d' % i2
                    k.op('act', lambda e: e.copy(out=stg[:, :, :], in_=ptq[:, 0:512].rearrange("p (h t) -> p h t", h=4)), r=[], w=['pq_pt', stgk])
                    k.dma('pool', stgk, qTs[b][:, c * 512:(c + 1) * 512], stg[:, :, :], r=[stgk], w=['qTs'])
                project("pq", [h2T[b] for b in range(NSLOT)], NSLOT, wq, [('pq%d' % i, [(0, i * 512, 512)], 512) for i in range(2)], post_pq, resident=True)
                k.barrier()
            k.mem = es
            with ExitStack() as ph:
                k.mem = ph
                kkb = k.sb("pe_kkb", [128, 8, 256], BF16)
                kkf = hbt_early = k.sb("pe_h2b", [128, D], BF16)
                kkf32 = kkf[:, 0:D].bitcast(F32).rearrange("p (h n) -> p h n", h=8)
                k.dma('sp', 'pe_kk', kkf32, kk.rearrange("h p n -> p h n"), w=['pe_h2b0'])
                k.op('dve', lambda e: e.tensor_copy(out=kkb[:], in_=kkf32), r=['pe_h2b0'], w=['pe_kkb'])
                qT = k.sb("pe_qT", [128, 1024], BF16)
                s12 = k.sb("pe_s12", [128, 8, 256], F32); tmpS = k.sb("pe_tmpS", [128, 256], F32)
                V12 = k.sb("pe_V12", [128, 8, 2, 16], F32); I12 = k.sb("pe_I12", [128, 8, 2, 16], U32); If = k.sb("pe_If", [128, 8, 2, 16], F32)
                cand = k.sb("pe_cand", [128, 8, 256], F32); cidx = k.sb("pe_cidx", [128, 8, 256], F32)
                Bt = k.sb("pe_B", [128, 8, 16], F32); Et = k.sb("pe_E", [128, 8, 16], F32); Ei = k.sb("pe_Ei", [128, 128], I32)
                gt = k.sb("pe_gate", [128, 8, 16], F32); gs = k.sb("pe_gs", [128, 16], F32)
                hd = k.sb("pe_hd", [128, 128], F32); at = k.sb("pe_a", [128, 128], F32)
                hbt = hbt_early; x1t = k.sb("pe_x1", [128, D], F32)
                NROW = 5
                rows = [k.sb("pe_row%d" % i, [128, 2 * D], BF16) for i in range(NROW)]
                aj = k.sb("pe_aj", [128, 8], F32)
                accs_ = [k.sb("pe_acc%d" % i, [128, 512], F32) for i in range(2)]; junk = k.sb("pe_junk", [128, D], BF16)
                diags = [k.sb("pe_dg%d" % i, [128, 128], BF16) for i in range(4)]
                ps12 = [k.ps("pe_ps%d" % i, [128, 512], F32) for i in range(8)]
                nrow = 0
                Ei2 = [Ei, k.sb("pe_Ei1", [128, 128], I32)]; gt2 = [gt, k.sb("pe_gate1", [128, 8, 16], F32)]
                hbt2 = [hbt, k.sb("pe_h2b1", [128, D], BF16)]

                def sel_matmul(b):
                    k.dma('sp', 'pe_qT', qT[:], qTs[b], w=['pe_qT'])
                    for h in range(8):
                        pb = ps12[h // 2]
                        k.op('pe', lambda e: e.matmul(pb[:, (h % 2) * 256:(h % 2) * 256 + 256], lhsT=qT[:, h * 128:(h + 1) * 128], rhs=kkb[:, h, :], start=True, stop=True),
                             r=['pe_qT', 'pe_kkb'], w=['pe_ps%d' % (h // 2)])
                    for j in range(4):
                        k.op('act', lambda e: e.copy(out=s12[:, 2 * j:2 * j + 2, :], in_=ps12[j][:, :].rearrange("p (h n) -> p h n", h=2)), r=[], w=['pe_ps%d' % j, 'pe_s12'])

                def sel_gen(b):
                    Eib = Ei2[b % 2]; gtb = gt2[b % 2]; eik = 'pe_Ei%d' % (b % 2); gk_ = 'pe_gate%d' % (b % 2)
                    for h in range(8):
                        for z in range(2):
                            vals = s12[:, h, z * 128:(z + 1) * 128]
                            k.op('dve', lambda e: e.max(out=V12[:, h, z, 0:8], in_=vals), r=['pe_s12'], w=['pe_V12'])
                            k.op('dve', lambda e: e.max_index(out=I12[:, h, z, 0:8], in_max=V12[:, h, z, 0:8], in_values=vals), r=['pe_s12', 'pe_V12'], w=['pe_I12'])
                            k.op('dve', lambda e: e.match_replace(out=tmpS[:, 0:128], in_to_replace=V12[:, h, z, 0:8], in_values=vals, imm_value=-1e30), r=['pe_s12', 'pe_V12'], w=['pe_tmpS'])
                            yield
                            k.op('dve', lambda e: e.max(out=V12[:, h, z, 8:16], in_=tmpS[:, 0:128]), r=['pe_tmpS'], w=['pe_V12'])
                            k.op('dve', lambda e: e.max_index(out=I12[:, h, z, 8:16], in_max=V12[:, h, z, 8:16], in_values=tmpS[:, 0:128]), r=['pe_tmpS', 'pe_V12'], w=['pe_I12'])
                            yield
                    k.op('dve', lambda e: e.tensor_copy(out=If[:], in_=I12[:]), r=['pe_I12'], w=['pe_If'])
                    for h in range(8):
                        ch = cand[:, h, :].rearrange("p (a b) -> p a b", a=16); ci = cidx[:, h, :].rearrange("p (a b) -> p a b", a=16)
                        k.op('dve', lambda e: e.tensor_tensor(out=ch, in0=V12[:, h, 0, :].unsqueeze(2).to_broadcast([128, 16, 16]),
                                                             in1=V12[:, h, 1, :].unsqueeze(1).to_broadcast([128, 16, 16]), op=ALU.add), r=['pe_V12'], w=['pe_cand'])
                        k.op('dve', lambda e: e.scalar_tensor_tensor(out=ci, in0=If[:, h, 0, :].unsqueeze(2).to_broadcast([128, 16, 16]), scalar=128.0,
                                                                    in1=If[:, h, 1, :].unsqueeze(1).to_broadcast([128, 16, 16]), op0=ALU.mult, op1=ALU.add), r=['pe_If'], w=['pe_cidx'])
                        yield
                        k.op('dve', lambda e: e.max(out=Bt[:, h, 0:8], in_=cand[:, h, :]), r=['pe_cand'], w=['pe_B'])
                        k.op('dve', lambda e: e.match_replace(out=tmpS[:, :], in_to_replace=Bt[:, h, 0:8], in_values=cand[:, h, :], imm_value=-1e30), r=['pe_cand', 'pe_B'], w=['pe_tmpS'])
                        k.op('dve', lambda e: e.max(out=Bt[:, h, 8:16], in_=tmpS[:, :]), r=['pe_tmpS'], w=['pe_B'])
                        yield
                        for kk_ in range(16):
                            k.op('dve', lambda e: e.scalar_tensor_tensor(out=tmpS[:, :], in0=cand[:, h, :], scalar=Bt[:, h, kk_:kk_ + 1], in1=cidx[:, h, :],
                                                                        op0=ALU.is_equal, op1=ALU.mult, accum_out=Et[:, h, kk_:kk_ + 1]), r=['pe_cand', 'pe_cidx', 'pe_B'], w=['pe_tmpS', 'pe_E'])
                            if kk_ % 2 == 1:
                                yield
                        k.op('dve', lambda e: e.tensor_scalar(out=gs[:, h:h + 1], in0=Bt[:, h, 0:1], scalar1=-1.0, scalar2=None, op0=ALU.mult), r=['pe_B'], w=['pe_gs'])
                        k.op('act', lambda e: e.activation(out=gtb[:, h, :], in_=Bt[:, h, :], func=AF.Exp, bias=gs[:, h:h + 1], scale=1.0, accum_out=gs[:, 8 + h:9 + h]),
                             r=['pe_B', 'pe_gs'], w=[gk_, 'pe_gs2'])
                        yield
                    k.op('dve', lambda e: e.reciprocal(out=gs[:, 8:16], in_=gs[:, 8:16]), r=['pe_gs2'], w=['pe_gs2'])
                    k.op('dve', lambda e: e.tensor_tensor(out=gtb[:, :, :], in0=gtb[:, :, :], in1=gs[:, 8:16].unsqueeze(2).to_broadcast([128, 8, 16]), op=ALU.mult), r=['pe_gs2'], w=[gk_])
                    k.op('dve', lambda e: e.tensor_scalar(out=Et[:, :, :], in0=Et[:, :, :], scalar1=0.0, scalar2=16383.0, op0=ALU.max, op1=ALU.min), r=[], w=['pe_E'])
                    k.op('dve', lambda e: e.tensor_copy(out=Eib[:, :], in_=Et[:, :, :].rearrange("p h k -> p (h k)")), r=['pe_E'], w=[eik])
                    yield

                SEL_STEPS = 8 * 2 * 2 + 8 * (2 + 8 + 1) + 1

                def expert_loop(b, gnext):
                    nonlocal_n = nrow_box[0]
                    Eib = Ei2[b % 2]; gtb = gt2[b % 2]; eik = 'pe_Ei%d' % (b % 2); gk_ = 'pe_gate%d' % (b % 2)
                    hb_ = hbt2[b % 2]; hbk = 'pe_h2b%d' % (b % 2)
                    k.dma('sp', hbk, hb_[:], h2b[b * 128:(b + 1) * 128, :], w=[hbk])
                    k.dma('sp', 'pe_x1', x1t[:], x1[b * 128:(b + 1) * 128, :], w=['pe_x1'])
                    gflat = gtb[:, :, :].rearrange("p h k -> p (h k)")
                    LOOK = 3

                    def emit_gather(jj):
                        n_ = nonlocal_n + jj
                        k.gather('pe_row%d' % (n_ % NROW), rows[n_ % NROW][:], uvb[:, :], Eib[:, jj:jj + 1], r=[eik, 'ubvb'], w=['pe_row%d' % (n_ % NROW)])

                    def stage1(j):
                        rw = rows[(nonlocal_n + j) % NROW]; rk = 'pe_row%d' % ((nonlocal_n + j) % NROW)
                        a0_ = (j % 4) * 2
                        k.op('dve', lambda e: e.scalar_tensor_tensor(out=junk[:], in0=rw[:, 0:D], scalar=1.0, in1=hb_[:], op0=ALU.mult, op1=ALU.mult, accum_out=hd[:, j:j + 1]),
                             r=[rk, hbk], w=['pe_junk', 'pe_hd%d' % (j % 4)])
                        k.op('act', lambda e: e.activation(out=aj[:, a0_:a0_ + 1], in_=hd[:, j:j + 1], func=AF.Gelu), r=['pe_hd%d' % (j % 4)], w=['pe_aj%d' % (j % 4)])
                        k.op('act', lambda e: e.activation(out=aj[:, a0_ + 1:a0_ + 2], in_=aj[:, a0_:a0_ + 1], func=AF.Copy, scale=gflat[:, j:j + 1]), r=[gk_], w=['pe_aj%d' % (j % 4)])

                    def stage2(j):
                        rw = rows[(nonlocal_n + j) % NROW]; rk = 'pe_row%d' % ((nonlocal_n + j) % NROW)
                        dg = diags[j % 4]; dgk = 'pe_dg%d' % (j % 4); a0_ = (j % 4) * 2
                        k.op('act', lambda e: e.activation(out=dg[:], in_=ident_f[:], func=AF.Copy, scale=aj[:, a0_ + 1:a0_ + 2]), r=['ident_f', 'pe_aj%d' % (j % 4)], w=[dgk])
                        for c in range(8):
                            k.op('pe', lambda e: e.matmul(ps12[c][:, :], lhsT=dg[:], rhs=rw[:, D + c * 512:D + (c + 1) * 512], start=(j == 0), stop=(j == 127)),
                                 r=[dgk, rk], w=['pe_ps%d' % c])
                    for jj in range(LOOK):
                        emit_gather(jj)
                    pulled = 0
                    for j in range(129):
                        if j + LOOK < 128:
                            emit_gather(j + LOOK)
                        if j < 128:
                            stage1(j)
                        if j >= 1:
                            stage2(j - 1)
                        if gnext is not None:
                            want = ((j + 1) * SEL_STEPS) // 120
                            while pulled < want:
                                pulled += 1
                                try:
                                    next(gnext)
                                except StopIteration:
                                    gnext = None
                                    break
                    if gnext is not None:
                        for _ in gnext:
                            pass
                    nrow_box[0] += 128
                    for c in range(8):
                        ac = accs_[c % 2]; ack = 'pe_acc%d' % (c % 2)
                        k.op('dve', lambda e: e.tensor_tensor(out=ac[:, :], in0=ps12[c][:, :], in1=x1t[:, c * 512:(c + 1) * 512], op=ALU.add),
                             r=['pe_x1'], w=['pe_ps%d' % c, ack])
                        k.dma('sp', ack, out_own[b * 128:(b + 1) * 128, c * 512:(c + 1) * 512], ac[:, :], r=[ack], w=['out'])

                nrow_box = [0]
                sel_matmul(0)
                for _ in sel_gen(0):
                    pass
                for b in range(NSLOT):
                    gnext = None
                    if b + 1 < NSLOT:
                        sel_matmul(b + 1)
                        gnext = sel_gen(b + 1)
                    expert_loop(b, gnext)
                k.barrier()
            k.mem = es

        k.barrier()
        print("instructions:", k.ninst)
    return nc


def _blocks(c):
    return [c, 15 - c, 16 + c, 31 - c, 32 + c, 47 - c, 48 + c, 63 - c]


def make_in_maps(inp):
    f = lambda a: np.ascontiguousarray(np.asarray(a), dtype=np.float32)
    x = f(inp["x"])[0]; pos = np.asarray(inp["positions"])[0].astype(np.int32)
    common = {
        "x_all": x, "pos_all": np.ascontiguousarray(pos.reshape(NTB, 128).T),
        "ln1_g": f(inp["ln1_g"]), "ln2_g": f(inp["ln2_g"]), "w_in": f(inp["w_in"])[0], "w_o": f(inp["w_o"])[0],
        "peer_wq": f(inp["peer_wq"])[0], "ffb": f(inp["fox_forget_b"]), "gq_f": f(inp["fox_qn_g"]), "gk_f": f(inp["fox_kn_g"]),
        "gq_d": f(inp["dsa_qn_g"]), "gk_d": f(inp["dsa_kn_g"]), "peer_u": f(inp["peer_u"])[0], "peer_v": f(inp["peer_v"])[0],
    }
    k1 = f(inp["peer_keys1"])[0]; k2 = f(inp["peer_keys2"])[0]
    kkm = np.zeros((8, 128, 256), np.float32)
    for h in range(8):
        kkm[h, 0:64, 0:128] = k1[h].T; kkm[h, 64:128, 128:256] = k2[h].T
    common["kk"] = kkm
    common["c_ident"] = np.eye(128, dtype=np.float32)
    common["c_tri"] = np.triu(np.ones((128, 128), np.float32))
    sel = np.zeros((16, NH * 128), np.float32)
    for h in range(NH):
        sel[h, h * 128:(h + 1) * 128] = 1.0
    common["c_sel16"] = sel
    inv128 = (10000.0 ** (-np.arange(0, 128, 2, dtype=np.float32) / np.float32(128))).astype(np.float32)
    inv64 = (10000.0 ** (-np.arange(0, 64, 2, dtype=np.float32) / np.float32(64))).astype(np.float32)
    common["c_inv"] = np.concatenate([inv128, inv64])[None, :].astype(np.float32)
    common["c_kiota"] = np.arange(1024, dtype=np.float32)[None, :]
    maps = []
    for c in range(8):
        blks = _blocks(c)
        rows = np.concatenate([np.arange(b * 128, (b + 1) * 128) for b in blks])
        m = dict(common)
        m["x_own"] = np.ascontiguousarray(x[rows])
        m["pos_own"] = np.ascontiguousarray(pos[rows].reshape(NSLOT, 128).T)
        m["qidx_p"] = np.ascontiguousarray(rows.astype(np.float32).reshape(NSLOT, 128).T)
        sb = np.zeros((NSLOT, NTB), np.float32)
        for s, b in enumerate(blks):
            sb[s, b] = 1.0
        m["selblk"] = sb.reshape(1, -1)
        maps.append(m)
    return maps


def kernel(**inputs):
    maps = make_in_maps(inputs)
    nc = build()
    res = run_bass_kernel_spmd(nc, maps, core_ids=list(range(8)))
    out = np.zeros((1, S, D), np.float32)
    for c in range(8):
        o = res.results[c]["out_own"]
        for s, b in enumerate(_blocks(c)):
            out[0, b * 128:(b + 1) * 128] = o[s * 128:(s + 1) * 128]
    return out
```

```python
import numpy as np
from contextlib import ExitStack
import concourse.bass as bass
import concourse.mybir as mybir
from concourse.bass_utils import run_bass_kernel_spmd

F32 = mybir.dt.float32; BF16 = mybir.dt.bfloat16; I32 = mybir.dt.int32; U32 = mybir.dt.uint32
AF = mybir.ActivationFunctionType; ALU = mybir.AluOpType; AX = mybir.AxisListType

D = 4096; S = 8192; NTB = 64; NSLOT = 8; HD = 128; NH = 16
INW = 9568
C_FQ, C_FK, C_FV, C_FF, C_DQ, C_DK, C_DV, C_IQ, C_IK, C_IW = 0, 2048, 4096, 6144, 6160, 8208, 8336, 8464, 9488, 9552
EPS = 1e-6
NEG = -30000.0
TOPK = 256
NBIS = 18


class KB:
    def __init__(self, nc, es):
        self.nc = nc; self.es = es; self.mem = es
        self.eng = {'pe': nc.tensor, 'act': nc.scalar, 'dve': nc.vector, 'pool': nc.gpsimd, 'sp': nc.sync}
        self.sem = {}; self.cnt = {}
        for e in ('pe', 'act', 'dve', 'pool'):
            self.sem[e] = es.enter_context(nc.semaphore("sem_" + e)); self.cnt[e] = 0
        self.waited = {e: {} for e in self.eng}
        self.lastw = {}; self.readers = {}
        self.ninst = 0

    def sb(self, name, shape, dt):
        return self.mem.enter_context(self.nc.sbuf_tensor(name, shape, dt))

    def ps(self, name, shape, dt):
        return self.mem.enter_context(self.nc.psum_tensor(name, shape, dt))

    def _wait(self, e, deps):
        best = {}
        for (src, n) in deps:
            if src is None:
                continue
            if src == 'pe' and e == 'pe':
                continue
            if n > best.get(src, 0):
                best[src] = n
        for src, n in best.items():
            if self.waited[e].get(src, 0) >= n:
                continue
            self.waited[e][src] = n
            val = n * 16 if src.startswith('dma:') else n
            self.eng[e].wait_ge(self.sem[src], val); self.ninst += 1

    def _deps(self, r, w):
        deps = []
        for k in r:
            deps.append(self.lastw.get(k, (None, 0)))
        for k in w:
            deps.append(self.lastw.get(k, (None, 0)))
            deps.extend(self.readers.get(k, []))
        return deps

    def _commit(self, tag, r, w):
        for k in r:
            self.readers.setdefault(k, []).append(tag)
        for k in w:
            self.lastw[k] = tag; self.readers[k] = []

    def op(self, e, fn, r=(), w=()):
        self._wait(e, self._deps(r, w))
        inst = fn(self.eng[e]); self.ninst += 1
        self.cnt[e] += 1
        inst.then_inc(self.sem[e], 1)
        self._commit((e, self.cnt[e]), r, w)

    def _dsem(self, key):
        km = self.__dict__.setdefault('keymap', {}); fr = self.__dict__.setdefault('free', [])
        if key in km:
            return km[key]
        if fr:
            src = fr.pop()
        else:
            src = 'dma:%d' % len([s for s in self.sem if s.startswith('dma:')])
            self.sem[src] = self.es.enter_context(self.nc.semaphore("sd_%s" % src[4:])); self.cnt[src] = 0
        km[key] = src
        return src

    def dma(self, q, key, out, in_, r=(), w=(), **kw):
        src = self._dsem(key)
        deps = self._deps(r, w)
        deps.append((src, self.cnt[src]))
        self._wait(q, deps)
        inst = self.eng[q].dma_start(out=out, in_=in_, **kw); self.ninst += 1
        self.cnt[src] += 1
        inst.then_inc(self.sem[src], 16)
        self._commit((src, self.cnt[src]), r, w)

    def gather(self, key, out, table, idx_ap, r=(), w=()):
        src = self._dsem(key)
        deps = self._deps(r, w)
        deps.append((src, self.cnt[src]))
        self._wait('pool', deps)
        inst = self.nc.gpsimd.indirect_dma_start(out=out, out_offset=None, in_=table,
                                                 in_offset=bass.IndirectOffsetOnAxis(ap=idx_ap, axis=0))
        self.ninst += 1
        self.cnt[src] += 1
        inst.then_inc(self.sem[src], 16)
        self._commit((src, self.cnt[src]), r, w)

    def barrier(self):
        srcs = [(s, n) for s, n in self.cnt.items() if n > 0]
        for e in self.eng:
            self._wait(e, srcs)
        self.lastw = {}; self.readers = {}
        self.keymap = {}; self.free = [s for s in self.sem if s.startswith('dma:')]


def build(dbg=None):
    nc = bass.Bass("TRN2", target_bir_lowering=False)
    dbg = dbg or {}
    STOP = dbg.get('stop', 99)

    def din(name, shape, dt=F32):
        return nc.dram_tensor(name, shape, dt, kind="ExternalInput").ap()

    def dscr(name, shape, dt):
        kind = "ExternalOutput" if name in dbg.get('dump', ()) else "Internal"
        return nc.dram_tensor(name, shape, dt, kind=kind).ap()

    x_all = din("x_all", [S, D]); x_own = din("x_own", [1024, D])
    pos_all = din("pos_all", [128, NTB], I32); pos_own = din("pos_own", [128, NSLOT], I32)
    qidx_p = din("qidx_p", [128, NSLOT]); selblk = din("selblk", [1, NSLOT * NTB])
    ln1_g = din("ln1_g", [1, D]); ln2_g = din("ln2_g", [1, D])
    w_in = din("w_in", [D, INW]); w_o = din("w_o", [D, D]); wq = din("peer_wq", [D, 1024])
    ffb = din("ffb", [1, NH]); gq_f = din("gq_f", [1, HD]); gk_f = din("gk_f", [1, HD])
    gq_d = din("gq_d", [1, HD]); gk_d = din("gk_d", [1, HD])
    kk = din("kk", [8, 128, 256]); pu = din("peer_u", [16384, D]); pv = din("peer_v", [16384, D])
    c_ident = din("c_ident", [128, 128]); c_tri = din("c_tri", [128, 128]); c_sel16 = din("c_sel16", [16, NH * 128])
    c_inv = din("c_inv", [1, 96]); c_kiota = din("c_kiota", [1, 1024])
    out_own = nc.dram_tensor("out_own", [1024, D], F32, kind="ExternalOutput").ap()

    hT_all = dscr("hT_all", [NTB, 128, D], BF16); hT_own = dscr("hT_own", [NSLOT, 128, D], BF16)
    fkT = dscr("fkT", [NH, 128, S], BF16); fv = dscr("fv", [NH, 128, NTB, 128], BF16)
    dkT = dscr("dkT", [128, S], BF16); dv = dscr("dv", [128, NTB, 128], BF16); ikT = dscr("ikT", [64, S], BF16)
    fqT = dscr("fqT", [NH, 128, 1024], BF16); dqT = dscr("dqT", [128, NSLOT, NH, 128], BF16)
    iqT = dscr("iqT", [64, NSLOT, NH, 128], BF16)
    mix = dscr("mix", [NSLOT, 128, D], BF16); mixT = dscr("mixT", [NSLOT, 128, D], BF16)
    x1 = dscr("x1", [1024, D], F32); h2b = dscr("h2b", [1024, D], BF16); uvb = dscr("uvb", [16384, 2 * D], BF16); h2T = dscr("h2T", [NSLOT, 128, D], BF16)
    qTs = dscr("qTs", [NSLOT, 128, 1024], BF16)
    dbg_nc = dscr("dbg_nc", [128, NTB * NH], F32)
    dbg_tau = dscr("dbg_tau", [128, NSLOT], F32)

    with ExitStack() as es:
        k = KB(nc, es)
        ident_f = k.sb("ident_f", [128, 128], F32); ident_b = k.sb("ident_b", [128, 128], BF16)
        tri_f = k.sb("tri_f", [128, 128], F32); ones_f = k.sb("ones_f", [128, 128], F32)
        sel16 = k.sb("sel16", [16, NH * 128], BF16); sel16f = k.sb("sel16f", [16, NH * 128], F32)
        inv_t = k.sb("inv_t", [128, 96], F32)
        nc_all = k.sb("nc_all", [128, NTB * NH], F32)
        aw = k.sb("aw", [128, NSLOT * NH], F32)
        qidx_t = k.sb("qidx_t", [128, NSLOT], F32)
        k.dma('sp', 'c0', ident_f[:], c_ident[:, :], w=['ident_f'])
        k.dma('sp', 'c1', tri_f[:], c_tri[:, :], w=['tri_f'])
        k.dma('sp', 'c2', sel16f[:], c_sel16[:, :], w=['sel16f'])
        k.dma('sp', 'c3', inv_t[:], c_inv[0:1, :].partition_broadcast(128), w=['inv_t'])
        k.dma('sp', 'c4', qidx_t[:], qidx_p[:, :], w=['qidx_t'])
        k.op('dve', lambda e: e.tensor_copy(out=ident_b[:], in_=ident_f[:]), r=['ident_f'], w=['ident_b'])
        k.op('dve', lambda e: e.tensor_copy(out=sel16[:], in_=sel16f[:]), r=['sel16f'], w=['sel16'])
        k.op('dve', lambda e: e.memset(ones_f[:], 1.0), w=['ones_f'])

        def rope_tables(m, cos_t, sin_t, posf, ncol, n, inv_off, half, tmp):
            nb = ncol; W = nb * half
            ang = tmp[:, 0:W]; nn = tmp[:, W:2 * W]; ni = tmp[:, 2 * W:3 * W].bitcast(I32); t4 = tmp[:, 3 * W:4 * W]
            for (dst, shift) in ((sin_t, 0.0), (cos_t, np.pi / 2)):
                k.op('dve', lambda e: e.tensor_tensor(out=ang.rearrange("p (b d) -> p b d", b=nb), in0=inv_t[:, inv_off:inv_off + half].unsqueeze(1).to_broadcast([128, nb, half]),
                                                     in1=posf.unsqueeze(2).to_broadcast([128, nb, half]), op=ALU.mult), r=['inv_t', m + 'posf'], w=[m + 'rt'])
                if shift != 0.0:
                    k.op('dve', lambda e: e.tensor_scalar(out=ang, in0=ang, scalar1=float(shift), scalar2=None, op0=ALU.add), r=[m + 'rt'], w=[m + 'rt'])
                k.op('dve', lambda e: e.tensor_scalar(out=nn, in0=ang, scalar1=float(1.0 / (2 * np.pi)), scalar2=None, op0=ALU.mult), r=[m + 'rt'], w=[m + 'rt'])
                k.op('dve', lambda e: e.tensor_copy(out=ni, in_=nn), r=[m + 'rt'], w=[m + 'rt'])
                k.op('dve', lambda e: e.tensor_copy(out=nn, in_=ni), r=[m + 'rt'], w=[m + 'rt'])
                k.op('dve', lambda e: e.scalar_tensor_tensor(out=ang, in0=nn, scalar=-6.28125, in1=ang, op0=ALU.mult, op1=ALU.add), r=[m + 'rt'], w=[m + 'rt'])
                k.op('dve', lambda e: e.scalar_tensor_tensor(out=ang, in0=nn, scalar=float(-(2 * np.pi - 6.28125)), in1=ang, op0=ALU.mult, op1=ALU.add), r=[m + 'rt'], w=[m + 'rt'])
                k.op('dve', lambda e: e.tensor_scalar(out=t4, in0=ang, scalar1=float(np.pi), scalar2=float(-2 * np.pi), op0=ALU.is_gt, op1=ALU.mult), r=[m + 'rt'], w=[m + 'rt'])
                k.op('dve', lambda e: e.tensor_tensor(out=ang, in0=ang, in1=t4, op=ALU.add), r=[m + 'rt'], w=[m + 'rt'])
                k.op('dve', lambda e: e.tensor_scalar(out=t4, in0=ang, scalar1=float(-np.pi), scalar2=float(2 * np.pi), op0=ALU.is_lt, op1=ALU.mult), r=[m + 'rt'], w=[m + 'rt'])
                k.op('dve', lambda e: e.tensor_tensor(out=ang, in0=ang, in1=t4, op=ALU.add), r=[m + 'rt'], w=[m + 'rt'])
                k.op('dve', lambda e: e.tensor_scalar(out=ang, in0=ang, scalar1=3.14159, scalar2=-3.14159, op0=ALU.min, op1=ALU.max), r=[m + 'rt'], w=[m + 'rt'])
                k.op('act', lambda e, dst=dst: e.activation(out=dst, in_=ang, func=AF.Sin), r=[m + 'rt'], w=[m + 'tab'])

        def rope_apply(eng, m, out_ap, in_ap, cos_t, sin_t, nh, half, t1, t2):
            x1v = in_ap[:, :, 0:half]; x2v = in_ap[:, :, half:2 * half]
            cb = cos_t.unsqueeze(1).to_broadcast([128, nh, half]); sbc = sin_t.unsqueeze(1).to_broadcast([128, nh, half])
            rk = [m + 'ropein', m + 'tab']
            k.op(eng, lambda e: e.tensor_tensor(out=t1, in0=x1v, in1=cb, op=ALU.mult), r=rk, w=[m + 't1'])
            k.op(eng, lambda e: e.tensor_tensor(out=t2, in0=x2v, in1=sbc, op=ALU.mult), r=rk, w=[m + 't2'])
            k.op(eng, lambda e: e.tensor_tensor(out=out_ap[:, :, 0:half], in0=t1, in1=t2, op=ALU.subtract), r=[m + 't1', m + 't2'], w=[m + 'ropeout'])
            k.op(eng, lambda e: e.tensor_tensor(out=t1, in0=x2v, in1=cb, op=ALU.mult), r=rk, w=[m + 't1'])
            k.op(eng, lambda e: e.tensor_tensor(out=t2, in0=x1v, in1=sbc, op=ALU.mult), r=rk, w=[m + 't2'])
            k.op(eng, lambda e: e.tensor_tensor(out=out_ap[:, :, half:2 * half], in0=t1, in1=t2, op=ALU.add), r=[m + 't1', m + 't2'], w=[m + 'ropeout'])

        def norm_transpose_phase(tag, src, nblk, gain_dram, dstT, dst_f32=None, do_norm=True, src_bf16=False, dst_b16=None):
            with ExitStack() as ph:
                k.mem = ph
                g1 = k.sb(tag + "g1", [128, D], F32) if do_norm else None
                xts = [k.sb(tag + "xt%d" % i, [128, D], BF16 if src_bf16 else F32) for i in range(2)]
                hbs = [k.sb(tag + "hb%d" % i, [128, D], BF16) for i in range(2)] if not src_bf16 else xts
                hTs = [k.sb(tag + "hT%d" % i, [128, D], BF16) for i in range(2)]
                hfs = [k.sb(tag + "hf%d" % i, [128, D], F32) for i in range(2)] if dst_f32 is not None else None
                junk = k.sb(tag + "junk", [128, D], BF16)
                st = k.sb(tag + "st", [128, 8], F32)
                pts = [k.ps(tag + "pt%d" % i, [128, 1024], BF16) for i in range(4)]
                if do_norm:
                    k.dma('sp', tag + 'g', g1[:], gain_dram[0:1, :].partition_broadcast(128), w=[tag + 'g1'])
                for b in range(nblk):
                    i = b % 2
                    xt = xts[i]; hb = hbs[i]; hT = hTs[i]
                    k.dma('sp', tag + 'x%d' % i, xt[:], src[b], w=[tag + 'xt%d' % i])
                    if do_norm:
                        c0 = (b % 2) * 4
                        k.op('act', lambda e: e.activation(out=junk[:], in_=xt[:], func=AF.Square, accum_out=st[:, c0:c0 + 1]),
                             r=[tag + 'xt%d' % i], w=[tag + 'junk', tag + 'st%d' % i])
                        k.op('dve', lambda e: e.tensor_scalar(out=st[:, c0 + 1:c0 + 2], in0=st[:, c0:c0 + 1], scalar1=1.0 / D, scalar2=EPS,
                                                             op0=ALU.mult, op1=ALU.add), r=[tag + 'st%d' % i], w=[tag + 'st%d' % i])
                        k.op('act', lambda e: e.activation(out=st[:, c0 + 2:c0 + 3], in_=st[:, c0 + 1:c0 + 2], func=AF.Ln), r=[tag + 'st%d' % i], w=[tag + 'st%d' % i])
                        k.op('act', lambda e: e.activation(out=st[:, c0 + 3:c0 + 4], in_=st[:, c0 + 2:c0 + 3], func=AF.Exp, scale=-0.5), r=[tag + 'st%d' % i], w=[tag + 'st%d' % i])
                        if dst_f32 is not None:
                            hf = hfs[i]
                            k.op('dve', lambda e: e.scalar_tensor_tensor(out=hf[:], in0=xt[:], scalar=st[:, c0 + 3:c0 + 4], in1=g1[:], op0=ALU.mult, op1=ALU.mult),
                                 r=[tag + 'xt%d' % i, tag + 'st%d' % i, tag + 'g1'], w=[tag + 'hf%d' % i])
                            k.op('pool', lambda e: e.tensor_copy(out=hb[:], in_=hf[:]), r=[tag + 'hf%d' % i], w=[tag + 'hb%d' % i])
                            k.dma('act', tag + 'hfo%d' % i, dst_f32[b], hf[:], r=[tag + 'hf%d' % i], w=[tag + 'dstf'])
                        else:
                            k.op('dve', lambda e: e.scalar_tensor_tensor(out=hb[:], in0=xt[:], scalar=st[:, c0 + 3:c0 + 4], in1=g1[:], op0=ALU.mult, op1=ALU.mult),
                                 r=[tag + 'xt%d' % i, tag + 'st%d' % i, tag + 'g1'], w=[tag + 'hb%d' % i])
                            if dst_b16 is not None:
                                k.dma('act', tag + 'hbo%d' % i, dst_b16[b], hb[:], r=[tag + 'hb%d' % i], w=[tag + 'dstb'])
                    hkey = (tag + 'hb%d' % i) if not src_bf16 else (tag + 'xt%d' % i)
                    for q in range(4):
                        for j in range(8):
                            kc = q * 8 + j
                            k.op('pe', lambda e: e.transpose(pts[q][:, j * 128:(j + 1) * 128], hb[:, kc * 128:(kc + 1) * 128], ident_b[:]),
                                 r=[hkey, 'ident_b'], w=[tag + 'pt%d' % q])
                        eng = 'act' if q % 2 == 0 else 'dve'
                        if eng == 'act':
                            k.op('act', lambda e: e.copy(out=hT[:, q * 1024:(q + 1) * 1024], in_=pts[q][:, :]), r=[], w=[tag + 'pt%d' % q, tag + 'hT%d' % i])
                        else:
                            k.op('dve', lambda e: e.tensor_copy(out=hT[:, q * 1024:(q + 1) * 1024], in_=pts[q][:, :]), r=[], w=[tag + 'pt%d' % q, tag + 'hT%d' % i])
                    k.dma('pool', tag + 'ho%d' % i, dstT[b], hT[:], r=[tag + 'hT%d' % i], w=[tag + 'dstT'])
                k.barrier()
            k.mem = es

        norm_transpose_phase("na", [x_all[b * 128:(b + 1) * 128, :] for b in range(NTB)], NTB, ln1_g, [hT_all[b] for b in range(NTB)])
        norm_transpose_phase("no", [x_own[b * 128:(b + 1) * 128, :] for b in range(NSLOT)], NSLOT, ln1_g, [hT_own[b] for b in range(NSLOT)])

        def project(tag, hT_src, nblk, wdram, chunks, post, pre=None, nht_buf=3, nps_buf=3, resident=False):
            wfs = [k.sb(tag + "wf%d" % i, [128, 1, 512], F32) for i in range(4)]
            wbs = [k.sb(tag + "wb%d" % i, [128, 32, 512], BF16) for i in range(2)]
            hts = [k.sb(tag + "ht%d" % i, [128, D], BF16) for i in range(nblk if resident else nht_buf)]
            pss = [k.ps(tag + "ps%d" % i, [128, 512], F32) for i in range(nps_buf)]
            wview = wdram.rearrange("(kc p) c -> p kc c", p=128)
            nwf = 0; nps = 0; nht = 0; pending = None
            for ci, (name, pieces, ncols) in enumerate(chunks):
                wb = wbs[ci % 2]; wbk = tag + 'wb%d' % (ci % 2)
                for j in range(32):
                    wf = wfs[nwf % 4]; wfk = tag + 'wf%d' % (nwf % 4); nwf += 1
                    for pi, (doff, scol, n) in enumerate(pieces):
                        k.dma('sp', wfk + 'p%d' % pi, wf[:, :, doff:doff + n], wview[:, j:j + 1, scol:scol + n], w=[wfk])
                    eng = ('dve', 'pool', 'act')[j % 3]
                    if eng == 'act':
                        k.op('act', lambda e: e.copy(out=wb[:, j:j + 1, 0:ncols], in_=wf[:, :, 0:ncols]), r=[wfk], w=[wbk])
                    else:
                        k.op(eng, lambda e: e.tensor_copy(out=wb[:, j:j + 1, 0:ncols], in_=wf[:, :, 0:ncols]), r=[wfk], w=[wbk])
                if pre is not None:
                    pre(name)
                for b in range(nblk):
                    if resident:
                        ht = hts[b]; htk = tag + 'ht%d' % b
                        if ci == 0:
                            k.dma('pool', htk, ht[:], hT_src[b], w=[htk])
                    else:
                        ht = hts[nht % nht_buf]; htk = tag + 'ht%d' % (nht % nht_buf); nht += 1
                        k.dma('sp', htk, ht[:], hT_src[b], w=[htk])
                    ps = pss[nps % nps_buf]; psk = tag + 'ps%d' % (nps % nps_buf); nps += 1
                    for kc in range(32):
                        k.op('pe', lambda e: e.matmul(ps[:, 0:ncols], lhsT=ht[:, kc * 128:(kc + 1) * 128], rhs=wb[:, kc, 0:ncols],
                                                      start=(kc == 0), stop=(kc == 31)), r=[htk, wbk], w=[psk])
                    if pending is not None:
                        post(*pending)
                    pending = (name, b, ps, psk)
            if pending is not None:
                post(*pending)

        def make_post_env(tag, nblk, posf_tile, pos_key, light=False, stg=True, share=None):
            env = {}
            env['ssq'] = k.sb(tag + "ssq", [128, 16], F32)
            env['yb'] = k.sb(tag + "yb", [128, 4, 128], BF16)
            if not light:
                env['yf'] = k.sb(tag + "yf", [128, 4, 128], F32)
                env['yr'] = k.sb(tag + "yr", [128, 4, 128], F32)
                env['t1'] = k.sb(tag + "t1", [128, 8, 64], F32); env['t2'] = k.sb(tag + "t2", [128, 8, 64], F32)
            env['junk'] = k.sb(tag + "pj", [128, 128], F32)
            if stg:
                env['stgT'] = [k.sb(tag + "stgT%d" % i, [128, 4, 512], BF16) for i in range(2)]
                env['stgV'] = [k.sb(tag + "stgV%d" % i, [128, 4, 4, 128], BF16) for i in range(2)]
            env['ptT'] = k.ps(tag + "ptT", [128, 1024], BF16)
            env['gt'] = {}
            if share is not None:
                for nm_ in ('cos', 'sin', 'cos2', 'sin2'):
                    env[nm_] = share[nm_]
                k.lastw[tag + 'tab'] = k.lastw.get(share['tag'] + 'tab', (None, 0))
                return env
            env['tag'] = tag
            env['cos'] = k.sb(tag + "cos", [128, nblk * 64], F32); env['sin'] = k.sb(tag + "sin", [128, nblk * 64], F32)
            env['cos2'] = k.sb(tag + "cos2", [128, nblk * 32], F32); env['sin2'] = k.sb(tag + "sin2", [128, nblk * 32], F32)
            env['rtmp'] = k.sb(tag + "rtmp", [128, 2048], F32)
            def gen_tables():
                for b in range(0, nblk, 8):
                    rope_tables(tag, env['cos'][:, b * 64:(b + 8) * 64], env['sin'][:, b * 64:(b + 8) * 64], posf_tile[:, b:b + 8], 8, None, 0, 64, env['rtmp'])
                    rope_tables(tag, env['cos2'][:, b * 32:(b + 8) * 32], env['sin2'][:, b * 32:(b + 8) * 32], posf_tile[:, b:b + 8], 8, None, 64, 32, env['rtmp'])
            if light:
                env['gen_tables'] = gen_tables
            else:
                gen_tables()
            return env

        def load_gain(tag, env, name, gdram, mul):
            gt = k.sb(tag + "g_" + name, [128, 128], F32)
            k.dma('sp', tag + 'g_' + name, gt[:], gdram[0:1, :].partition_broadcast(128), w=[tag + 'g_' + name])
            if mul != 1.0:
                k.op('dve', lambda e: e.tensor_scalar(out=gt[:], in0=gt[:], scalar1=float(mul), scalar2=None, op0=ALU.mult), r=[], w=[tag + 'g_' + name])
            env['gt'][name] = (gt, tag + 'g_' + name)

        def post_norm_heads(tag, env, ps, psk, nh, gname, rope_b):
            ssq = env['ssq']; gt, gk = env['gt'][gname]
            for h in range(nh):
                k.op('act', lambda e: e.activation(out=env['junk'][:], in_=ps[:, h * 128:(h + 1) * 128], func=AF.Square, accum_out=ssq[:, h:h + 1]),
                     r=[], w=[psk, tag + 'pj', tag + 'ssq'])
            k.op('dve', lambda e: e.tensor_scalar(out=ssq[:, 4:4 + nh], in0=ssq[:, 0:nh], scalar1=1.0 / HD, scalar2=EPS, op0=ALU.mult, op1=ALU.add), r=[], w=[tag + 'ssq'])
            k.op('act', lambda e: e.activation(out=ssq[:, 8:8 + nh], in_=ssq[:, 4:4 + nh], func=AF.Ln), r=[], w=[tag + 'ssq'])
            k.op('act', lambda e: e.activation(out=ssq[:, 12:12 + nh], in_=ssq[:, 8:8 + nh], func=AF.Exp, scale=-0.5), r=[], w=[tag + 'ssq'])
            dst = env['yf'] if rope_b is not None else env['yb']
            dk_ = tag + ('ropein' if rope_b is not None else 'yb')
            for h in range(nh):
                k.op('dve', lambda e: e.scalar_tensor_tensor(out=dst[:, h, :], in0=ps[:, h * 128:(h + 1) * 128], scalar=ssq[:, 12 + h:13 + h], in1=gt[:],
                                                            op0=ALU.mult, op1=ALU.mult), r=[tag + 'ssq', gk], w=[psk, dk_])
            if rope_b is not None:
                b = rope_b
                rope_apply('pool', tag, env['yr'][:, 0:nh, :], env['yf'][:, 0:nh, :], env['cos'][:, b * 64:(b + 1) * 64], env['sin'][:, b * 64:(b + 1) * 64],
                           nh, 64, env['t1'][:, 0:nh, :], env['t2'][:, 0:nh, :])
                k.op('pool', lambda e: e.tensor_copy(out=env['yb'][:, 0:nh, :], in_=env['yr'][:, 0:nh, :]), r=[tag + 'ropeout'], w=[tag + 'yb'])

        def transpose_heads_to_stg(tag, env, nh, b4, stg, stgk, width=128, src=None, srck=None):
            src = env['yb'] if src is None else src; srck = (tag + 'yb') if srck is None else srck
            pt = env['ptT']
            for h in range(nh):
                k.op('pe', lambda e: e.transpose(pt[0:width, h * 128:(h + 1) * 128], src[:, h, 0:width], ident_b[:]), r=[srck, 'ident_b'], w=[tag + 'ptT'])
            k.op('act', lambda e: e.copy(out=stg[0:width, 0:nh, b4 * 128:(b4 + 1) * 128],
                                         in_=pt[0:width, 0:nh * 128].rearrange("p (h t) -> p h t", h=nh)), r=[], w=[tag + 'ptT', stgk])

        if STOP >= 2:
          with ExitStack() as ph:
            k.mem = ph
            tag = "kp"
            posi = k.sb("kp_posi", [128, NTB], I32); posf = k.sb("kp_posf", [128, NTB], F32)
            k.dma('sp', 'kp_pos', posi[:], pos_all[:, :], w=['kp_posi'])
            k.op('dve', lambda e: e.tensor_copy(out=posf[:], in_=posi[:]), r=['kp_posi'], w=['kpposf'])
            env = make_post_env(tag, NTB, posf, 'kpposf', light=True)
            load_gain(tag, env, 'gk_f', gk_f, 1.0); load_gain(tag, env, 'gk_d', gk_d, 1.0)
            fbt = k.sb("kp_fb", [128, NH], F32)
            k.dma('sp', 'kp_fb', fbt[:], ffb[0:1, :].partition_broadcast(128), w=['kp_fb'])
            prefix = k.sb("kp_prefix", [128, NH], F32); nlf = k.sb("kp_nlf", [128, NH], F32)
            k.op('dve', lambda e: e.memset(prefix[:], 0.0), w=['kp_prefix'])
            psc = k.ps("kp_psc", [128, 512], F32)
            stgI = [k.sb("kp_stgI%d" % i, [64, 1, 512], BF16) for i in range(2)]
            stgD = [k.sb("kp_stgD%d" % i, [128, 1, 512], BF16) for i in range(2)]
            stgDV = [k.sb("kp_stgDV%d" % i, [128, 4, 128], BF16) for i in range(2)]
            xms = [k.sb("kp_xm%d" % i, [128, 4, 336], F32) for i in range(2)]
            nlf4 = k.sb("kp_nlf4", [128, 4, NH], F32); cs4 = k.sb("kp_cs4", [128, 128], F32)
            sq4 = k.sb("kp_sq4", [128, 4, 128], F32); ss4 = k.sb("kp_ss4", [128, 16], F32)
            dkb4 = k.sb("kp_dkb4", [128, 4, 128], BF16); ikb4 = k.sb("kp_ikb4", [128, 4, 64], BF16)
            r4a = k.sb("kp_r4a", [128, 4, 64], F32); r4b = k.sb("kp_r4b", [128, 4, 64], F32)

            def rope4(m, outb, xin, xin_key, cosv, sinv, half):
                x1v = xin[:, :, 0:half]; x2v = xin[:, :, half:2 * half]
                ta = r4a[:, :, 0:half]; tb = r4b[:, :, 0:half]
                rk = [xin_key, tag + 'tab']
                k.op('dve', lambda e: e.tensor_tensor(out=ta, in0=x1v, in1=cosv, op=ALU.mult), r=rk, w=['kp_r4a'])
                k.op('pool', lambda e: e.tensor_tensor(out=tb, in0=x2v, in1=sinv, op=ALU.mult), r=rk, w=['kp_r4b'])
                k.op('dve', lambda e: e.tensor_tensor(out=outb[:, :, 0:half], in0=ta, in1=tb, op=ALU.subtract), r=['kp_r4a', 'kp_r4b'], w=[m + '_out'])
                k.op('dve', lambda e: e.tensor_tensor(out=ta, in0=x2v, in1=cosv, op=ALU.mult), r=rk, w=['kp_r4a'])
                k.op('pool', lambda e: e.tensor_tensor(out=tb, in0=x1v, in1=sinv, op=ALU.mult), r=rk, w=['kp_r4b'])
                k.op('dve', lambda e: e.tensor_tensor(out=outb[:, :, half:2 * half], in0=ta, in1=tb, op=ALU.add), r=['kp_r4a', 'kp_r4b'], w=[m + '_out'])

            deferred = []

            def post_k(name, b, ps, psk):
                b4 = b % 4; g4 = b // 4; i2 = g4 % 2
                if name.startswith('fk'):
                    h0 = int(name[2:]) * 4
                    post_norm_heads(tag, env, ps, psk, 4, 'gk_f', None)
                    stg = env['stgT'][i2]; stgk = tag + 'stgT%d' % i2
                    transpose_heads_to_stg(tag, env, 4, b4, stg, stgk)
                    if b4 == 3:
                        k.dma('pool', stgk, fkT[h0:h0 + 4, :, g4 * 512:(g4 + 1) * 512].rearrange("h d t -> d h t"), stg[:, :, :], r=[stgk], w=['fkT'])
                elif name.startswith('fv'):
                    h0 = int(name[2:]) * 4
                    stg = env['stgV'][i2]; stgk = tag + 'stgV%d' % i2
                    k.op('act', lambda e: e.copy(out=stg[:, :, b4, :], in_=ps[:, 0:512].rearrange("p (h d) -> p h d", h=4)), r=[], w=[psk, stgk])
                    if b4 == 3:
                        k.dma('pool', stgk, fv[h0:h0 + 4, :, g4 * 4:(g4 + 1) * 4, :].rearrange("h j b d -> j h b d"), stg[:, :, :, :], r=[stgk], w=['fv'])
                else:
                    xm = xms[i2]; xk = 'kp_xm%d' % i2
                    k.op('act', lambda e: e.copy(out=xm[:, b4, :], in_=ps[:, 0:336]), r=[], w=[psk, xk])
                    if b4 == 1:
                        while deferred:
                            deferred.pop(0)()
                    if b4 != 3:
                        return
                    b0 = g4 * 4
                    k.op('dve', lambda e: e.tensor_tensor(out=nlf4[:, :, :], in0=xm[:, :, 0:16], in1=fbt[:, :].unsqueeze(1).to_broadcast([128, 4, NH]), op=ALU.add), r=[xk, 'kp_fb'], w=['kp_nlf'])
                    k.op('act', lambda e: e.activation(out=nlf4[:, :, :], in_=nlf4[:, :, :], func=AF.Exp, scale=-1.0), r=[], w=['kp_nlf'])
                    k.op('act', lambda e: e.activation(out=nlf4[:, :, :], in_=nlf4[:, :, :], func=AF.Ln, bias=1.0), r=[], w=['kp_nlf'])
                    nl2 = nlf4[:, :, :].rearrange("p b h -> p (b h)")
                    k.op('pe', lambda e: e.matmul(psc[:, 0:64], lhsT=tri_f[:], rhs=nl2, start=True, stop=True), r=['tri_f', 'kp_nlf'], w=['kp_psc'])
                    k.op('pe', lambda e: e.matmul(psc[:, 64:128], lhsT=ones_f[:], rhs=nl2, start=True, stop=True), r=['ones_f', 'kp_nlf'], w=['kp_psc'])
                    k.op('dve', lambda e: e.tensor_copy(out=cs4[:, :], in_=psc[:, 0:128]), r=[], w=['kp_psc', 'kp_cs4'])
                    for j in range(4):
                        bb = b0 + j
                        k.op('dve', lambda e: e.tensor_tensor(out=nc_all[:, bb * NH:(bb + 1) * NH], in0=cs4[:, j * NH:(j + 1) * NH], in1=prefix[:], op=ALU.add), r=['kp_cs4', 'kp_prefix'], w=['nc_all'])
                        k.op('dve', lambda e: e.tensor_tensor(out=prefix[:], in0=cs4[:, 64 + j * NH:64 + (j + 1) * NH], in1=prefix[:], op=ALU.add), r=['kp_cs4'], w=['kp_prefix'])
                    xk_ = xm[:, :, 16:144]
                    k.op('pool', lambda e: e.tensor_tensor(out=sq4[:, :, :], in0=xk_, in1=xk_, op=ALU.mult), r=[xk], w=['kp_sq4'])
                    k.op('dve', lambda e: e.tensor_reduce(out=ss4[:, 0:4], in_=sq4[:, :, :], axis=AX.X, op=ALU.add), r=['kp_sq4'], w=['kp_ss4'])
                    k.op('dve', lambda e: e.tensor_scalar(out=ss4[:, 4:8], in0=ss4[:, 0:4], scalar1=1.0 / HD, scalar2=EPS, op0=ALU.mult, op1=ALU.add), r=[], w=['kp_ss4'])
                    k.op('act', lambda e: e.activation(out=ss4[:, 8:12], in_=ss4[:, 4:8], func=AF.Ln), r=[], w=['kp_ss4'])
                    k.op('act', lambda e: e.activation(out=ss4[:, 12:16], in_=ss4[:, 8:12], func=AF.Exp, scale=-0.5), r=[], w=['kp_ss4'])
                    gt_, gk_ = env['gt']['gk_d']
                    k.op('dve', lambda e: e.tensor_tensor(out=sq4[:, :, :], in0=xk_, in1=ss4[:, 12:16].unsqueeze(2).to_broadcast([128, 4, 128]), op=ALU.mult), r=[xk, 'kp_ss4'], w=['kp_sq4'])
                    k.op('dve', lambda e: e.tensor_tensor(out=sq4[:, :, :], in0=sq4[:, :, :], in1=gt_[:, :].unsqueeze(1).to_broadcast([128, 4, 128]), op=ALU.mult), r=[gk_], w=['kp_sq4'])
                    cos4 = env['cos'][:, b0 * 64:(b0 + 4) * 64].rearrange("p (b d) -> p b d", b=4); sin4 = env['sin'][:, b0 * 64:(b0 + 4) * 64].rearrange("p (b d) -> p b d", b=4)
                    rope4('kpd', dkb4, sq4, 'kp_sq4', cos4, sin4, 64)
                    sd = stgD[i2]; sdk = 'kp_stgD%d' % i2
                    pt = env['ptT']

                    def partB1():
                        for j in range(4):
                            k.op('pe', lambda e: e.transpose(pt[:, j * 128:(j + 1) * 128], dkb4[:, j, :], ident_b[:]), r=['kpd_out', 'ident_b'], w=[tag + 'ptT'])
                        k.op('act', lambda e: e.copy(out=sd[:, 0, :], in_=pt[:, 0:512]), r=[], w=[tag + 'ptT', sdk])
                        k.dma('pool', sdk, dkT[:, g4 * 512:(g4 + 1) * 512], sd[:, 0, :], r=[sdk], w=['dkT'])
                    sv = stgDV[i2]; svk = 'kp_stgDV%d' % i2
                    k.op('act', lambda e: e.copy(out=sv[:, :, :], in_=xm[:, :, 144:272]), r=[xk], w=[svk])
                    k.dma('pool', svk, dv[:, g4 * 4:(g4 + 1) * 4, :], sv[:, :, :], r=[svk], w=['dv'])
                    cos24 = env['cos2'][:, b0 * 32:(b0 + 4) * 32].rearrange("p (b d) -> p b d", b=4); sin24 = env['sin2'][:, b0 * 32:(b0 + 4) * 32].rearrange("p (b d) -> p b d", b=4)
                    rope4('kpi', ikb4, xm[:, :, 272:336], xk, cos24, sin24, 32)
                    si = stgI[i2]; sik = 'kp_stgI%d' % i2

                    def partB2():
                        for j in range(4):
                            k.op('pe', lambda e: e.transpose(pt[0:64, j * 128:(j + 1) * 128], ikb4[:, j, :], ident_b[:]), r=['kpi_out', 'ident_b'], w=[tag + 'ptT'])
                        k.op('act', lambda e: e.copy(out=si[:, 0, :], in_=pt[0:64, 0:512]), r=[], w=[tag + 'ptT', sik])
                        k.dma('pool', sik, ikT[:, g4 * 512:(g4 + 1) * 512], si[:, 0, :], r=[sik], w=['ikT'])
                    deferred.append(lambda: (partB1(), partB2()))

            k.lastw['kpitab'] = k.lastw.get(tag + 'tab', (None, 0))
            chunks = [('fk%d' % i, [(0, C_FK + i * 512, 512)], 512) for i in range(4)]
            chunks += [('fv%d' % i, [(0, C_FV + i * 512, 512)], 512) for i in range(4)]
            chunks += [('mixed', [(0, C_FF, 16), (16, C_DK, 128), (144, C_DV, 128), (272, C_IK, 64)], 336)]
            if 'kchunks' in dbg:
                chunks = [c for c in chunks if c[0] in dbg['kchunks']]
            project(tag, [hT_all[b] for b in range(dbg.get('nkb', NTB))], dbg.get('nkb', NTB), w_in, chunks, post_k, nht_buf=2, nps_buf=4,
                    pre=lambda nm: env['gen_tables']() if nm == chunks[min(1, len(chunks) - 1)][0] else None)
            while deferred:
                deferred.pop(0)()
            if 'dbg_nc' in dbg.get('dump', ()):
                k.dma('sp', 'dbgnc', dbg_nc[:, :], nc_all[:], r=['nc_all'], w=['dbg_nc'])
            k.barrier()
          k.mem = es


        if STOP >= 3:
          with ExitStack() as ph:
            k.mem = ph
            tag = "qp"
            posi = k.sb("qp_posi", [128, NSLOT], I32); posf = k.sb("qp_posf", [128, NSLOT], F32)
            k.dma('sp', 'qp_pos', posi[:], pos_own[:, :], w=['qp_posi'])
            k.op('dve', lambda e: e.tensor_copy(out=posf[:], in_=posi[:]), r=['qp_posi'], w=['qpposf'])
            k.lastw['qqposf'] = k.lastw['qpposf']
            env = make_post_env(tag, NSLOT, posf, 'qpposf', stg=False)
            load_gain(tag, env, 'gq_f', gq_f, HD ** -0.5); load_gain(tag, env, 'gq_d', gq_d, HD ** -0.5)
            envB = make_post_env("qq", NSLOT, posf, 'qqposf', stg=False, share=env)
            load_gain("qq", envB, 'gq_f', gq_f, HD ** -0.5); load_gain("qq", envB, 'gq_d', gq_d, HD ** -0.5)
            envs2 = ((tag, env), ("qq", envB))
            iqf = k.sb("qp_iqf", [128, 8, 64], F32); iqr = k.sb("qp_iqr", [128, 8, 64], F32); iqb = k.sb("qp_iqb", [128, 8, 64], BF16)
            stgQ = [k.sb("qp_stgQ%d" % i, [128, 4, 128], BF16) for i in range(2)]
            stgIQ = [k.sb("qp_stgIQ%d" % i, [64, 8, 128], BF16) for i in range(2)]
            k.lastw['qpitab'] = k.lastw.get(tag + 'tab', (None, 0))
            cnt = {'n': 0}

            def post_q(name, b, ps, psk):
                i2 = cnt['n'] % 2; cnt['n'] += 1
                if name == 'iw':
                    k.op('dve', lambda e: e.tensor_scalar(out=aw[:, b * NH:(b + 1) * NH], in0=ps[:, 0:16], scalar1=0.03125, scalar2=None, op0=ALU.mult), r=[], w=[psk, 'aw'])
                elif name.startswith('fq') or name.startswith('dq'):
                    h0 = int(name[2:]) * 4
                    isd = name.startswith('dq')
                    tg_, en_ = envs2[b % 2]
                    post_norm_heads(tg_, en_, ps, psk, 4, 'gq_d' if isd else 'gq_f', b if isd else None)
                    stg = stgQ[i2]; stgk = 'qp_stgQ%d' % i2
                    pt = en_['ptT']
                    for h in range(4):
                        k.op('pe', lambda e: e.transpose(pt[:, h * 128:(h + 1) * 128], en_['yb'][:, h, :], ident_b[:]), r=[tg_ + 'yb', 'ident_b'], w=[tg_ + 'ptT'])
                    k.op('act', lambda e: e.copy(out=stg[:, :, :], in_=pt[:, 0:512].rearrange("p (h t) -> p h t", h=4)), r=[], w=[tg_ + 'ptT', stgk])
                    if isd:
                        k.dma('pool', stgk, dqT[:, b, h0:h0 + 4, :], stg[:, :, :], r=[stgk], w=['dqT'])
                    else:
                        k.dma('pool', stgk, fqT[h0:h0 + 4, :, b * 128:(b + 1) * 128].rearrange("h d t -> d h t"), stg[:, :, :], r=[stgk], w=['fqT'])
                else:
                    h0 = int(name[2:]) * 8
                    k.op('act', lambda e: e.copy(out=iqf[:, :, :], in_=ps[:, 0:512].rearrange("p (h d) -> p h d", h=8)), r=[], w=[psk, 'qpiropein'])
                    rope_apply('pool', 'qpi', iqr[:, :, :], iqf[:, :, :], env['cos2'][:, b * 32:(b + 1) * 32], env['sin2'][:, b * 32:(b + 1) * 32], 8, 32,
                               env['t1'][:, :, 0:32], env['t2'][:, :, 0:32])
                    k.op('pool', lambda e: e.tensor_tensor(out=iqb[:, :, :], in0=iqr[:, :, :],
                                                          in1=aw[:, b * NH + h0:b * NH + h0 + 8].unsqueeze(2).to_broadcast([128, 8, 64]), op=ALU.mult),
                         r=['qpiropeout', 'aw'], w=['qp_iqb'])
                    pt = env['ptT']
                    for h in range(8):
                        k.op('pe', lambda e: e.transpose(pt[0:64, h * 128:(h + 1) * 128], iqb[:, h, :], ident_b[:]), r=['qp_iqb', 'ident_b'], w=[tag + 'ptT'])
                    stg = stgIQ[i2]; stgk = 'qp_stgIQ%d' % i2
                    k.op('act', lambda e: e.copy(out=stg[:, :, :], in_=pt[0:64, 0:1024].rearrange("p (h t) -> p h t", h=8)), r=[], w=[tag + 'ptT', stgk])
                    k.dma('pool', stgk, iqT[:, b, h0:h0 + 8, :], stg[:, :, :], r=[stgk], w=['iqT'])

            chunks = [('iw', [(0, C_IW, 16)], 16)]
            chunks += [('iq%d' % i, [(0, C_IQ + i * 512, 512)], 512) for i in range(2)]
            chunks += [('fq%d' % i, [(0, C_FQ + i * 512, 512)], 512) for i in range(4)]
            chunks += [('dq%d' % i, [(0, C_DQ + i * 512, 512)], 512) for i in range(4)]
            project(tag, [hT_own[b] for b in range(NSLOT)], NSLOT, w_in, chunks, post_q, resident=True)
            k.barrier()
          k.mem = es

        if STOP >= 4:
            kio = k.sb("kio", [128, 1024], F32)
            k.dma('sp', 'kio', kio[:], c_kiota[0:1, :].partition_broadcast(128), w=['kio'])
            qrel = k.sb("qrel", [128, NSLOT], F32)
            for s in range(NSLOT):
                k.op('dve', lambda e: e.tensor_scalar(out=qrel[:, s:s + 1], in0=qidx_t[:, s:s + 1], scalar1=float(-1024 * s), scalar2=None, op0=ALU.add), r=['qidx_t'], w=['qrel'])
            NMf = k.sb("NMf", [128, NSLOT, 1024], BF16)
            for s in range(NSLOT):
                k.op('dve', lambda e: e.tensor_scalar(out=NMf[:, s, :], in0=kio[:], scalar1=qrel[:, s:s + 1], scalar2=NEG, op0=ALU.is_gt, op1=ALU.mult), r=['kio', 'qrel'], w=['NMf'])
            cT = k.sb("cT", [16, 1024], BF16)
            with ExitStack() as ph:
                k.mem = ph
                selb = k.sb("selb", [128, NSLOT * NTB], F32); tmpc = k.sb("tmpc", [128, NH, NTB], F32)
                nco = k.sb("nco", [128, NH], F32); ncb = k.sb("ncb", [128, NH], BF16)
                ptc = k.ps("ptc", [128, 1024], BF16)
                k.dma('sp', 'selb', selb[:], selblk[0:1, :].partition_broadcast(128), w=['selb'])
                for s in range(NSLOT):
                    k.op('dve', lambda e: e.tensor_tensor(out=tmpc[:, :, :], in0=nc_all[:, :].rearrange("p (tb h) -> p h tb", h=NH),
                                                         in1=selb[:, s * NTB:(s + 1) * NTB].unsqueeze(1).to_broadcast([128, NH, NTB]), op=ALU.mult), r=['nc_all', 'selb'], w=['tmpc'])
                    k.op('dve', lambda e: e.tensor_reduce(out=nco[:, :], in_=tmpc[:, :, :], axis=AX.X, op=ALU.add), r=['tmpc'], w=['nco'])
                    k.op('dve', lambda e: e.tensor_scalar(out=ncb[:, :], in0=nco[:, :], scalar1=-1.0, scalar2=None, op0=ALU.mult), r=['nco'], w=['ncb'])
                    k.op('pe', lambda e: e.transpose(ptc[0:16, s * 128:(s + 1) * 128], ncb[:, :], ident_b[:]), r=['ncb', 'ident_b'], w=['ptc'])
                k.op('act', lambda e: e.copy(out=cT[:, :], in_=ptc[0:16, :]), r=[], w=['ptc', 'cT'])
                k.barrier()
            k.mem = es

        def attn_evacuate(tagp, accs, acckeys, stg, stgk, rden):
            for s_ in range(len(accs)):
                acc = accs[s_]
                k.op('dve', lambda e: e.reciprocal(out=rden[:, s_:s_ + 1], in_=acc[:, 128:129]), r=[], w=[acckeys[s_], tagp + 'rden'])
                k.op('dve', lambda e: e.tensor_scalar(out=stg[:, s_, :], in0=acc[:, 0:128], scalar1=rden[:, s_:s_ + 1], scalar2=None, op0=ALU.mult),
                     r=[tagp + 'rden'], w=[acckeys[s_], stgk])

        bg = {'gen': None}
        if STOP >= 4:
            cast_scope = ExitStack()
            k.mem = cast_scope
            cfs = [k.sb("cst_f%d" % i, [128, 2048], F32) for i in range(2)]
            cbs = [k.sb("cst_b%d" % i, [128, 2048], BF16) for i in range(2)]
            k.mem = es

            def cast_gen():
                items = []
                for wi, srcw in enumerate((pu, pv)):
                    for rb in range(128):
                        for hf_ in range(2):
                            items.append((srcw[rb * 128:(rb + 1) * 128, hf_ * 2048:(hf_ + 1) * 2048],
                                          uvb[rb * 128:(rb + 1) * 128, wi * D + hf_ * 2048:wi * D + (hf_ + 1) * 2048]))
                n = len(items)
                for i in range(n + 1):
                    if i < n:
                        fi = i % 2
                        k.dma('pool', 'cst_f%d' % fi, cfs[fi][:], items[i][0], w=['cst_f%d' % fi])
                    if i >= 1:
                        j = i - 1; fi = j % 2; bi = j % 2
                        eng = 'pool' if j % 3 == 0 else 'dve'
                        k.op(eng, lambda e: e.tensor_copy(out=cbs[bi][:], in_=cfs[fi][:]), r=['cst_f%d' % fi], w=['cst_b%d' % bi])
                        k.dma('pool', 'cst_b%d' % bi, items[j][1], cbs[bi][:], r=['cst_b%d' % bi], w=['ubvb'])
                    yield
            bg['gen'] = cast_gen()

        def bg_pull(n=1):
            g_ = bg['gen']
            if g_ is None:
                return
            for _ in range(n):
                try:
                    next(g_)
                except StopIteration:
                    bg['gen'] = None
                    return

        if STOP >= 4:
          with ExitStack() as ph:
            k.mem = ph
            kTs = [k.sb("fx_kT%d" % i, [128, S], BF16) for i in range(2)]
            Vs = [k.sb("fx_V%d" % i, [128, NTB, 129], BF16) for i in range(2)]
            qTt = [k.sb("fx_qT%d" % i, [128, 1024], BF16) for i in range(2)]
            Ps = [k.sb("fx_P%d" % i, [128, 512], BF16) for i in range(4)]
            stgs = [k.sb("fx_stg%d" % i, [128, NSLOT, 128], BF16) for i in range(2)]
            rden = k.sb("fx_rden", [128, 8], F32)
            Sb = [k.ps("fx_S%d" % i, [128, 512], F32) for i in range(4)]
            Ab1 = [k.ps("fx_A0_%d" % j, [128, 512], F32) for j in range(3)]
            Ab = [Ab1, Ab1]
            for i in range(2):
                k.op('pool', lambda e: e.memset(Vs[i][:, :, 128:129], 1.0), w=['fx_V%d' % i])
            nS = 0; nP = 0
            for h in range(NH):
                hi = h % 2
                kT = kTs[hi]; V = Vs[hi]; qT = qTt[hi]
                k.dma('sp', 'fx_kT%d' % hi, kT[:], fkT[h], w=['fx_kT%d' % hi])
                k.dma('sp', 'fx_V%d' % hi, V[:, :, 0:128], fv[h], w=['fx_V%d' % hi])
                k.dma('sp', 'fx_qT%d' % hi, qT[:], fqT[h], w=['fx_qT%d' % hi])
                accs = []; acckeys = []
                for s in range(NSLOT):
                    accs.append(Ab[0][s // 3][:, (s % 3) * 160:(s % 3) * 160 + 129]); acckeys.append('fx_A0_%d' % (s // 3))
                for j in range(3):
                    k.op('dve', lambda e: e.memset(Ab[0][j][:, :], 0.0), w=['fx_A0_%d' % j])
                tiles = [(g, kb) for g in range(2) for kb in range(32 * (g + 1))]

                def stageA(n):
                    g, kb = tiles[n]
                    smin = max(4 * g, kb // 8)
                    c0 = smin * 128; c1 = (4 * g + 4) * 128; nn = c1 - c0
                    Sp = Sb[n % 4]; Sk = 'fx_S%d' % (n % 4)
                    P = Ps[n % 4]; Pk = 'fx_P%d' % (n % 4)
                    zone = (kb // 8) >= 4 * g
                    k.op('pe', lambda e: e.matmul(Sp[:, 0:nn], lhsT=kT[:, kb * 128:(kb + 1) * 128], rhs=qT[:, c0:c1], start=True, stop=False),
                         r=['fx_kT%d' % hi, 'fx_qT%d' % hi], w=[Sk])
                    k.op('pe', lambda e: e.matmul(Sp[:, 0:nn], lhsT=sel16[:, h * 128:(h + 1) * 128], rhs=cT[:, c0:c1], start=False, stop=not zone),
                         r=['sel16', 'cT'], w=[Sk])
                    if zone:
                        sz = kb // 8; z = kb - 8 * sz
                        k.op('pe', lambda e: e.matmul(Sp[:, 0:128], lhsT=NMf[:, sz, z * 128:(z + 1) * 128], rhs=ident_b[:], start=False, stop=True),
                             r=['NMf', 'ident_b'], w=[Sk])
                    k.op('act', lambda e: e.activation(out=P[:, 0:nn], in_=Sp[:, 0:nn], func=AF.Exp, bias=nc_all[:, kb * NH + h:kb * NH + h + 1], scale=1.0),
                         r=['nc_all'], w=[Sk, Pk])

                def stageB(n):
                    g, kb = tiles[n]
                    smin = max(4 * g, kb // 8)
                    P = Ps[n % 4]; Pk = 'fx_P%d' % (n % 4)
                    for s in range(smin, 4 * g + 4):
                        k.op('pe', lambda e: e.matmul(accs[s], lhsT=P[:, (s - smin) * 128:(s - smin + 1) * 128], rhs=V[:, kb, :], start=False, stop=False, skip_group_check=True),
                             r=[Pk, 'fx_V%d' % hi], w=[acckeys[s]])
                LA = 2
                for n in range(len(tiles) + LA):
                    if n < len(tiles):
                        stageA(n)
                    if n >= LA:
                        stageB(n - LA)
                    if n % 6 == 5:
                        bg_pull()
                stg = stgs[hi]; stgk = 'fx_stg%d' % hi
                attn_evacuate('fx', accs, acckeys, stg, stgk, rden)
                k.dma('pool', stgk, mix[:, :, h * 128:(h + 1) * 128].rearrange("s i d -> i s d"), stg[:, :, :], r=[stgk], w=['mix'])
            k.barrier()
          k.mem = es

        if STOP >= 5:
          with ExitStack() as ph:
            k.mem = ph
            dkt = k.sb("ds_kT", [128, S], BF16); dvt = k.sb("ds_V", [128, NTB, 129], BF16); ikt = k.sb("ds_ik", [64, S], BF16)
            dqs = [k.sb("ds_q%d" % i, [128, NH, 128], BF16) for i in range(2)]
            iqs = [k.sb("ds_iq%d" % i, [64, NH, 128], BF16) for i in range(2)]
            score = k.sb("ds_score", [128, S], F32); NMd = k.sb("ds_NM0", [128, S], BF16)
            rl = [k.sb("ds_rl%d" % i, [128, 512], F32) for i in range(2)]
            Ps = [k.sb("ds_P%d" % i, [128, 512], BF16) for i in range(4)]
            stgs = [k.sb("ds_stg%d" % i, [128, 4, 128], BF16) for i in range(2)]
            bs = k.sb("ds_bs", [128, 16], F32)
            tau_all = k.sb("ds_tau", [128, NSLOT], F32)
            rden = k.sb("ds_rden", [128, 8], F32)
            Lb = [k.ps("ds_L%d" % i, [128, 512], F32) for i in range(2)]
            Sb = [k.ps("ds_S%d" % i, [128, 512], F32) for i in range(4)]
            Ab1 = [k.ps("ds_A0_%d" % j, [128, 512], F32) for j in range(2)]
            Ab = [Ab1, Ab1]
            k.dma('sp', 'ds_kT', dkt[:], dkT[:, :], w=['ds_kT'])
            k.op('pool', lambda e: e.memset(dvt[:, :, 128:129], 1.0), w=['ds_V'])
            k.dma('sp', 'ds_V', dvt[:, :, 0:128], dv[:, :, :], w=['ds_V'])
            k.dma('sp', 'ds_ik', ikt[:], ikT[:, :], w=['ds_ik'])
            sgv = k.sb("sgn", [128, NSLOT * NH], F32)
            k.op('dve', lambda e: e.tensor_scalar(out=sgv[:], in0=aw[:], scalar1=0.0, scalar2=2.0, op0=ALU.is_ge, op1=ALU.mult), r=['aw'], w=['sgn'])
            k.op('dve', lambda e: e.tensor_scalar(out=sgv[:], in0=sgv[:], scalar1=-1.0, scalar2=None, op0=ALU.add), r=[], w=['sgn'])
            NMds = [NMd, k.sb("ds_NM1", [128, S], BF16)]
            cnt_ = {'L': 0, 'R': 0, 'S': 0, 'A': 0}

            def prep_slot(s):
                si = s % 2; L = 1024 * (s + 1)
                dq = dqs[si]; iq = iqs[si]; NMs = NMds[si]; NMk = 'ds_NM%d' % si
                k.dma('sp', 'ds_q%d' % si, dq[:], dqT[:, s, :, :], w=['ds_q%d' % si])
                k.dma('sp', 'ds_iq%d' % si, iq[:], iqT[:, s, :, :], w=['ds_iq%d' % si])
                for kc in range(L // 512):
                    for h in range(NH):
                        nL = cnt_['L']; cnt_['L'] += 1; nR = cnt_['R']; cnt_['R'] += 1
                        Lp = Lb[nL % 2]; Lk = 'ds_L%d' % (nL % 2)
                        R = rl[nR % 2]; Rk = 'ds_rl%d' % (nR % 2)
                        k.op('pe', lambda e: e.matmul(Lp[:, :], lhsT=iq[:, h, :], rhs=ikt[:, kc * 512:(kc + 1) * 512], start=True, stop=True),
                             r=['ds_iq%d' % si, 'ds_ik'], w=[Lk])
                        k.op('act', lambda e: e.activation(out=R[:, :], in_=Lp[:, :], func=AF.Relu, scale=sgv[:, s * NH + h:s * NH + h + 1]), r=['sgn'], w=[Lk, Rk])
                        if h == 0:
                            k.op('dve', lambda e: e.tensor_scalar(out=score[:, kc * 512:(kc + 1) * 512], in0=R[:, :], scalar1=sgv[:, s * NH + h:s * NH + h + 1], scalar2=None, op0=ALU.mult),
                                 r=[Rk, 'sgn'], w=['ds_score'])
                        else:
                            k.op('dve', lambda e: e.scalar_tensor_tensor(out=score[:, kc * 512:(kc + 1) * 512], in0=R[:, :], scalar=sgv[:, s * NH + h:s * NH + h + 1],
                                                                        in1=score[:, kc * 512:(kc + 1) * 512], op0=ALU.mult, op1=ALU.add), r=[Rk, 'sgn'], w=['ds_score'])
                        yield
                k.op('dve', lambda e: e.tensor_reduce(out=bs[:, 0:1], in_=score[:, 0:L], axis=AX.X, op=ALU.max), r=['ds_score'], w=['ds_bs'])
                k.op('dve', lambda e: e.tensor_reduce(out=bs[:, 1:2], in_=score[:, 0:L], axis=AX.X, op=ALU.min), r=['ds_score'], w=['ds_bs'])
                k.op('dve', lambda e: e.tensor_tensor(out=bs[:, 2:3], in0=bs[:, 0:1], in1=bs[:, 1:2], op=ALU.subtract), r=[], w=['ds_bs'])
                k.op('dve', lambda e: e.tensor_scalar(out=bs[:, 2:3], in0=bs[:, 2:3], scalar1=1.0001, scalar2=2e-3, op0=ALU.mult, op1=ALU.add), r=[], w=['ds_bs'])
                k.op('dve', lambda e: e.tensor_scalar(out=bs[:, 3:4], in0=bs[:, 1:2], scalar1=-1.0, scalar2=1e-3, op0=ALU.mult, op1=ALU.add), r=[], w=['ds_bs'])
                yield
                k.op('pool', lambda e: e.tensor_scalar(out=rl[0][:, :], in0=kio[:, 0:512], scalar1=qrel[:, s:s + 1], scalar2=-1e30, op0=ALU.is_gt, op1=ALU.mult), r=['kio', 'qrel'], w=['ds_rl0'])
                k.op('pool', lambda e: e.tensor_scalar(out=rl[1][:, :], in0=kio[:, 512:1024], scalar1=qrel[:, s:s + 1], scalar2=-1e30, op0=ALU.is_gt, op1=ALU.mult), r=['kio', 'qrel'], w=['ds_rl1'])
                k.op('dve', lambda e: e.tensor_tensor(out=score[:, L - 1024:L - 512], in0=score[:, L - 1024:L - 512], in1=rl[0][:, :], op=ALU.add), r=['ds_rl0'], w=['ds_score'])
                k.op('dve', lambda e: e.tensor_tensor(out=score[:, L - 512:L], in0=score[:, L - 512:L], in1=rl[1][:, :], op=ALU.add), r=['ds_rl1'], w=['ds_score'])
                yield
                thr = float(2 * TOPK - L) - 0.5
                for it in range(NBIS):
                    f = -(2.0 ** -(it + 1))
                    k.op('dve', lambda e: e.tensor_scalar(out=bs[:, 4:5], in0=bs[:, 2:3], scalar1=float(f), scalar2=None, op0=ALU.mult), r=[], w=['ds_bs'])
                    k.op('dve', lambda e: e.tensor_tensor(out=bs[:, 5:6], in0=bs[:, 4:5], in1=bs[:, 3:4], op=ALU.add), r=[], w=['ds_bs'])
                    k.op('act', lambda e: e.activation(out=NMs[:, 0:L], in_=score[:, 0:L], func=AF.Sign, bias=bs[:, 5:6], scale=1.0, accum_out=bs[:, 6:7]),
                         r=['ds_score', 'ds_bs'], w=[NMk, 'ds_bs2'])
                    k.op('dve', lambda e: e.scalar_tensor_tensor(out=bs[:, 7:8], in0=bs[:, 6:7], scalar=thr, in1=bs[:, 4:5], op0=ALU.is_ge, op1=ALU.mult), r=['ds_bs2'], w=['ds_bs'])
                    k.op('dve', lambda e: e.tensor_tensor(out=bs[:, 3:4], in0=bs[:, 3:4], in1=bs[:, 7:8], op=ALU.add), r=[], w=['ds_bs'])
                    yield
                k.op('dve', lambda e: e.tensor_scalar(out=tau_all[:, s:s + 1], in0=bs[:, 3:4], scalar1=-1.0, scalar2=None, op0=ALU.mult), r=['ds_bs'], w=['ds_tau'])
                k.op('dve', lambda e: e.tensor_scalar(out=NMs[:, 0:L], in0=score[:, 0:L], scalar1=tau_all[:, s:s + 1], scalar2=NEG, op0=ALU.is_lt, op1=ALU.mult),
                     r=['ds_score', 'ds_tau'], w=[NMk])
                yield

            def prep_steps(s):
                return 16 * 2 * (s + 1) + 3 + NBIS + 1

            def attend_slot(s, gnext, nsteps_next):
                si = s % 2; nkb = 8 * (s + 1)
                dq = dqs[si]; NMs = NMds[si]; NMk = 'ds_NM%d' % si
                ntiles = 4 * (nkb + 2); done_t = 0; done_bg = 0
                for hg in range(4):
                    ai = cnt_['A'] % 2; cnt_['A'] += 1
                    accs = [Ab[0][j // 2][:, (j % 2) * 160:(j % 2) * 160 + 129] for j in range(4)]
                    acckeys = ['ds_A0_%d' % (j // 2) for j in range(4)]
                    for j in range(2):
                        k.op('dve', lambda e: e.memset(Ab[0][j][:, :], 0.0), w=['ds_A0_%d' % j])
                    nS = cnt_['S']

                    def stageA(kb):
                        n = nS + kb
                        Sp = Sb[n % 4]; Sk = 'ds_S%d' % (n % 4)
                        P = Ps[n % 4]; Pk = 'ds_P%d' % (n % 4)
                        k.op('pe', lambda e: e.matmul(Sp[:, :], lhsT=dkt[:, kb * 128:(kb + 1) * 128], rhs=dq[:, hg * 4:(hg + 1) * 4, :], start=True, stop=False),
                             r=['ds_kT', 'ds_q%d' % si], w=[Sk])
                        for hh in range(4):
                            k.op('pe', lambda e: e.matmul(Sp[:, hh * 128:(hh + 1) * 128], lhsT=NMs[:, kb * 128:(kb + 1) * 128], rhs=ident_b[:], start=False, stop=(hh == 3)),
                                 r=[NMk, 'ident_b'], w=[Sk])
                        k.op('act', lambda e: e.activation(out=P[:, :], in_=Sp[:, :], func=AF.Exp), r=[], w=[Sk, Pk])

                    def stageB(kb):
                        n = nS + kb
                        P = Ps[n % 4]; Pk = 'ds_P%d' % (n % 4)
                        for hh in range(4):
                            k.op('pe', lambda e: e.matmul(accs[hh], lhsT=P[:, hh * 128:(hh + 1) * 128], rhs=dvt[:, kb, :], start=False, stop=False, skip_group_check=True),
                                 r=[Pk, 'ds_V'], w=[acckeys[hh]])
                    LA = 2
                    for kb in range(nkb + LA):
                        if kb < nkb:
                            stageA(kb)
                        if kb >= LA:
                            stageB(kb - LA)
                        if kb % 4 == 3:
                            bg_pull()
                        done_t += 1
                        if gnext is not None:
                            want = (done_t * nsteps_next) // ntiles
                            while done_bg < want:
                                done_bg += 1
                                try:
                                    next(gnext)
                                except StopIteration:
                                    gnext = None
                                    break
                    cnt_['S'] += nkb
                    stg = stgs[ai]; stgk = 'ds_stg%d' % ai
                    attn_evacuate('ds', accs, acckeys, stg, stgk, rden)
                    k.dma('pool', stgk, mix[s, :, 2048 + hg * 512:2048 + (hg + 1) * 512], stg[:, :, :], r=[stgk], w=['mix'])
                if gnext is not None:
                    for _ in gnext:
                        pass

            for _ in prep_slot(0):
                pass
            for s in range(NSLOT):
                gnext = prep_slot(s + 1) if s + 1 < NSLOT else None
                attend_slot(s, gnext, prep_steps(s + 1) if s + 1 < NSLOT else 0)
            if 'dbg_tau' in dbg.get('dump', ()):
                k.dma('sp', 'dbgtau', dbg_tau[:, :], tau_all[:], r=['ds_tau'], w=['dbg_tau'])
            k.barrier()
          k.mem = es

        bg_pull(10000)
        if STOP >= 4:
            k.barrier()
            cast_scope.close()
        if STOP >= 6:
            norm_transpose_phase("mt", [mix[b] for b in range(NSLOT)], NSLOT, None, [mixT[b] for b in range(NSLOT)], do_norm=False, src_bf16=True)
            with ExitStack() as ph:
                k.mem = ph
                xos = [k.sb("wo_xo%d" % i, [128, 512], F32) for i in range(2)]
                cw = {'n': 0}

                def post_wo(name, b, ps, psk):
                    c = int(name[2:]); i2 = cw['n'] % 2; cw['n'] += 1
                    xo = xos[i2]; xk = 'wo_xo%d' % i2
                    k.dma('sp', xk, xo[:], x_own[b * 128:(b + 1) * 128, c * 512:(c + 1) * 512], w=[xk])
                    k.op('dve', lambda e: e.tensor_tensor(out=xo[:], in0=ps[:, :], in1=xo[:], op=ALU.add), r=[], w=[psk, xk])
                    k.dma('pool', xk + 'o', x1[b * 128:(b + 1) * 128, c * 512:(c + 1) * 512], xo[:], r=[xk], w=['x1'])
                project("wo", [mixT[b] for b in range(NSLOT)], NSLOT, w_o, [('wo%d' % i, [(0, i * 512, 512)], 512) for i in range(8)], post_wo, resident=True)
                k.barrier()
            k.mem = es

        if STOP >= 7:
            norm_transpose_phase("n2", [x1[b * 128:(b + 1) * 128, :] for b in range(NSLOT)], NSLOT, ln2_g, [h2T[b] for b in range(NSLOT)],
                                 dst_b16=[h2b[b * 128:(b + 1) * 128, :] for b in range(NSLOT)])
            with ExitStack() as ph:
                k.mem = ph
                yb = k.sb("pq_yb", [128, 4, 128], BF16); stgq = [k.sb("pq_stg%d" % i, [128, 4, 128], BF16) for i in range(2)]
                ptq = k.ps("pq_pt", [128, 1024], BF16)
                cq = {'n': 0}

                def post_pq(name, b, ps, psk):
                    c = int(name[2:]); i2 = cq['n'] % 2; cq['n'] += 1
                    k.op('act', lambda e: e.copy(out=yb[:, :, :], in_=ps[:, 0:512].rearrange("p (h d) -> p h d", h=4)), r=[], w=[psk, 'pq_yb'])
                    for h in range(4):
                        k.op('pe', lambda e: e.transpose(ptq[:, h * 128:(h + 1) * 128], yb[:, h, :], ident_b[:]), r=['pq_yb', 'ident_b'], w=['pq_pt'])
                    stg = stgq[i2]; stgk = 'pq_stg%d' % i2
                    k.op('act', lambda e: e.copy(out=stg[:, :, :], in_=ptq[:, 0:512].rearrange("p (h t) -> p h t", h=4)), r=[], w=['pq_pt', stgk])
                    k.dma('pool', stgk, qTs[b][:, c * 512:(c + 1) * 512], stg[:, :, :], r=[stgk], w=['qTs'])
                project("pq", [h2T[b] for b in range(NSLOT)], NSLOT, wq, [('pq%d' % i, [(0, i * 512, 512)], 512) for i in range(2)], post_pq, resident=True)
                k.barrier()
            k.mem = es
            with ExitStack() as ph:
                k.mem = ph
                kkb = k.sb("pe_kkb", [128, 8, 256], BF16)
                kkf = hbt_early = k.sb("pe_h2b", [128, D], BF16)
                kkf32 = kkf[:, 0:D].bitcast(F32).rearrange("p (h n) -> p h n", h=8)
                k.dma('sp', 'pe_kk', kkf32, kk.rearrange("h p n -> p h n"), w=['pe_h2b0'])
                k.op('dve', lambda e: e.tensor_copy(out=kkb[:], in_=kkf32), r=['pe_h2b0'], w=['pe_kkb'])
                qT = k.sb("pe_qT", [128, 1024], BF16)
                s12 = k.sb("pe_s12", [128, 8, 256], F32); tmpS = k.sb("pe_tmpS", [128, 256], F32)
                V12 = k.sb("pe_V12", [128, 8, 2, 16], F32); I12 = k.sb("pe_I12", [128, 8, 2, 16], U32); If = k.sb("pe_If", [128, 8, 2, 16], F32)
                cand = k.sb("pe_cand", [128, 8, 256], F32); cidx = k.sb("pe_cidx", [128, 8, 256], F32)
                Bt = k.sb("pe_B", [128, 8, 16], F32); Et = k.sb("pe_E", [128, 8, 16], F32); Ei = k.sb("pe_Ei", [128, 128], I32)
                gt = k.sb("pe_gate", [128, 8, 16], F32); gs = k.sb("pe_gs", [128, 16], F32)
                hd = k.sb("pe_hd", [128, 128], F32); at = k.sb("pe_a", [128, 128], F32)
                hbt = hbt_early; x1t = k.sb("pe_x1", [128, D], F32)
                NROW = 5
                rows = [k.sb("pe_row%d" % i, [128, 2 * D], BF16) for i in range(NROW)]
                aj = k.sb("pe_aj", [128, 8], F32)
                accs_ = [k.sb("pe_acc%d" % i, [128, 512], F32) for i in range(2)]; junk = k.sb("pe_junk", [128, D], BF16)
                diags = [k.sb("pe_dg%d" % i, [128, 128], BF16) for i in range(4)]
                ps12 = [k.ps("pe_ps%d" % i, [128, 512], F32) for i in range(8)]
                nrow = 0
                Ei2 = [Ei, k.sb("pe_Ei1", [128, 128], I32)]; gt2 = [gt, k.sb("pe_gate1", [128, 8, 16], F32)]
                hbt2 = [hbt, k.sb("pe_h2b1", [128, D], BF16)]

                def sel_matmul(b):
                    k.dma('sp', 'pe_qT', qT[:], qTs[b], w=['pe_qT'])
                    for h in range(8):
                        pb = ps12[h // 2]
                        k.op('pe', lambda e: e.matmul(pb[:, (h % 2) * 256:(h % 2) * 256 + 256], lhsT=qT[:, h * 128:(h + 1) * 128], rhs=kkb[:, h, :], start=True, stop=True),
                             r=['pe_qT', 'pe_kkb'], w=['pe_ps%d' % (h // 2)])
                    for j in range(4):
                        k.op('act', lambda e: e.copy(out=s12[:, 2 * j:2 * j + 2, :], in_=ps12[j][:, :].rearrange("p (h n) -> p h n", h=2)), r=[], w=['pe_ps%d' % j, 'pe_s12'])

                def sel_gen(b):
                    Eib = Ei2[b % 2]; gtb = gt2[b % 2]; eik = 'pe_Ei%d' % (b % 2); gk_ = 'pe_gate%d' % (b % 2)
                    for h in range(8):
                        for z in range(2):
                            vals = s12[:, h, z * 128:(z + 1) * 128]
                            k.op('dve', lambda e: e.max(out=V12[:, h, z, 0:8], in_=vals), r=['pe_s12'], w=['pe_V12'])
                            k.op('dve', lambda e: e.max_index(out=I12[:, h, z, 0:8], in_max=V12[:, h, z, 0:8], in_values=vals), r=['pe_s12', 'pe_V12'], w=['pe_I12'])
                            k.op('dve', lambda e: e.match_replace(out=tmpS[:, 0:128], in_to_replace=V12[:, h, z, 0:8], in_values=vals, imm_value=-1e30), r=['pe_s12', 'pe_V12'], w=['pe_tmpS'])
                            yield
                            k.op('dve', lambda e: e.max(out=V12[:, h, z, 8:16], in_=tmpS[:, 0:128]), r=['pe_tmpS'], w=['pe_V12'])
                            k.op('dve', lambda e: e.max_index(out=I12[:, h, z, 8:16], in_max=V12[:, h, z, 8:16], in_values=tmpS[:, 0:128]), r=['pe_tmpS', 'pe_V12'], w=['pe_I12'])
                            yield
                    k.op('dve', lambda e: e.tensor_copy(out=If[:], in_=I12[:]), r=['pe_I12'], w=['pe_If'])
                    for h in range(8):
                        ch = cand[:, h, :].rearrange("p (a b) -> p a b", a=16); ci = cidx[:, h, :].rearrange("p (a b) -> p a b", a=16)
                        k.op('dve', lambda e: e.tensor_tensor(out=ch, in0=V12[:, h, 0, :].unsqueeze(2).to_broadcast([128, 16, 16]),
                                                             in1=V12[:, h, 1, :].unsqueeze(1).to_broadcast([128, 16, 16]), op=ALU.add), r=['pe_V12'], w=['pe_cand'])
                        k.op('dve', lambda e: e.scalar_tensor_tensor(out=ci, in0=If[:, h, 0, :].unsqueeze(2).to_broadcast([128, 16, 16]), scalar=128.0,
                                                                    in1=If[:, h, 1, :].unsqueeze(1).to_broadcast([128, 16, 16]), op0=ALU.mult, op1=ALU.add), r=['pe_If'], w=['pe_cidx'])
                        yield
                        k.op('dve', lambda e: e.max(out=Bt[:, h, 0:8], in_=cand[:, h, :]), r=['pe_cand'], w=['pe_B'])
                        k.op('dve', lambda e: e.match_replace(out=tmpS[:, :], in_to_replace=Bt[:, h, 0:8], in_values=cand[:, h, :], imm_value=-1e30), r=['pe_cand', 'pe_B'], w=['pe_tmpS'])
                        k.op('dve', lambda e: e.max(out=Bt[:, h, 8:16], in_=tmpS[:, :]), r=['pe_tmpS'], w=['pe_B'])
                        yield
                        for kk_ in range(16):
                            k.op('dve', lambda e: e.scalar_tensor_tensor(out=tmpS[:, :], in0=cand[:, h, :], scalar=Bt[:, h, kk_:kk_ + 1], in1=cidx[:, h, :],
                                                                        op0=ALU.is_equal, op1=ALU.mult, accum_out=Et[:, h, kk_:kk_ + 1]), r=['pe_cand', 'pe_cidx', 'pe_B'], w=['pe_tmpS', 'pe_E'])
                            if kk_ % 2 == 1:
                                yield
                        k.op('dve', lambda e: e.tensor_scalar(out=gs[:, h:h + 1], in0=Bt[:, h, 0:1], scalar1=-1.0, scalar2=None, op0=ALU.mult), r=['pe_B'], w=['pe_gs'])
                        k.op('act', lambda e: e.activation(out=gtb[:, h, :], in_=Bt[:, h, :], func=AF.Exp, bias=gs[:, h:h + 1], scale=1.0, accum_out=gs[:, 8 + h:9 + h]),
                             r=['pe_B', 'pe_gs'], w=[gk_, 'pe_gs2'])
                        yield
                    k.op('dve', lambda e: e.reciprocal(out=gs[:, 8:16], in_=gs[:, 8:16]), r=['pe_gs2'], w=['pe_gs2'])
                    k.op('dve', lambda e: e.tensor_tensor(out=gtb[:, :, :], in0=gtb[:, :, :], in1=gs[:, 8:16].unsqueeze(2).to_broadcast([128, 8, 16]), op=ALU.mult), r=['pe_gs2'], w=[gk_])
                    k.op('dve', lambda e: e.tensor_scalar(out=Et[:, :, :], in0=Et[:, :, :], scalar1=0.0, scalar2=16383.0, op0=ALU.max, op1=ALU.min), r=[], w=['pe_E'])
                    k.op('dve', lambda e: e.tensor_copy(out=Eib[:, :], in_=Et[:, :, :].rearrange("p h k -> p (h k)")), r=['pe_E'], w=[eik])
                    yield

                SEL_STEPS = 8 * 2 * 2 + 8 * (2 + 8 + 1) + 1

                def expert_loop(b, gnext):
                    nonlocal_n = nrow_box[0]
                    Eib = Ei2[b % 2]; gtb = gt2[b % 2]; eik = 'pe_Ei%d' % (b % 2); gk_ = 'pe_gate%d' % (b % 2)
                    hb_ = hbt2[b % 2]; hbk = 'pe_h2b%d' % (b % 2)
                    k.dma('sp', hbk, hb_[:], h2b[b * 128:(b + 1) * 128, :], w=[hbk])
                    k.dma('sp', 'pe_x1', x1t[:], x1[b * 128:(b + 1) * 128, :], w=['pe_x1'])
                    gflat = gtb[:, :, :].rearrange("p h k -> p (h k)")
                    LOOK = 3

                    def emit_gather(jj):
                        n_ = nonlocal_n + jj
                        k.gather('pe_row%d' % (n_ % NROW), rows[n_ % NROW][:], uvb[:, :], Eib[:, jj:jj + 1], r=[eik, 'ubvb'], w=['pe_row%d' % (n_ % NROW)])

                    def stage1(j):
                        rw = rows[(nonlocal_n + j) % NROW]; rk = 'pe_row%d' % ((nonlocal_n + j) % NROW)
                        a0_ = (j % 4) * 2
                        k.op('dve', lambda e: e.scalar_tensor_tensor(out=junk[:], in0=rw[:, 0:D], scalar=1.0, in1=hb_[:], op0=ALU.mult, op1=ALU.mult, accum_out=hd[:, j:j + 1]),
                             r=[rk, hbk], w=['pe_junk', 'pe_hd%d' % (j % 4)])
                        k.op('act', lambda e: e.activation(out=aj[:, a0_:a0_ + 1], in_=hd[:, j:j + 1], func=AF.Gelu), r=['pe_hd%d' % (j % 4)], w=['pe_aj%d' % (j % 4)])
                        k.op('act', lambda e: e.activation(out=aj[:, a0_ + 1:a0_ + 2], in_=aj[:, a0_:a0_ + 1], func=AF.Copy, scale=gflat[:, j:j + 1]), r=[gk_], w=['pe_aj%d' % (j % 4)])

                    def stage2(j):
                        rw = rows[(nonlocal_n + j) % NROW]; rk = 'pe_row%d' % ((nonlocal_n + j) % NROW)
                        dg = diags[j % 4]; dgk = 'pe_dg%d' % (j % 4); a0_ = (j % 4) * 2
                        k.op('act', lambda e: e.activation(out=dg[:], in_=ident_f[:], func=AF.Copy, scale=aj[:, a0_ + 1:a0_ + 2]), r=['ident_f', 'pe_aj%d' % (j % 4)], w=[dgk])
                        for c in range(8):
                            k.op('pe', lambda e: e.matmul(ps12[c][:, :], lhsT=dg[:], rhs=rw[:, D + c * 512:D + (c + 1) * 512], start=(j == 0), stop=(j == 127)),
                                 r=[dgk, rk], w=['pe_ps%d' % c])
                    for jj in range(LOOK):
                        emit_gather(jj)
                    pulled = 0
                    for j in range(129):
                        if j + LOOK < 128:
                            emit_gather(j + LOOK)
                        if j < 128:
                            stage1(j)
                        if j >= 1:
                            stage2(j - 1)
                        if gnext is not None:
                            want = ((j + 1) * SEL_STEPS) // 120
                            while pulled < want:
                                pulled += 1
                                try:
                                    next(gnext)
                                except StopIteration:
                                    gnext = None
                                    break
                    if gnext is not None:
                        for _ in gnext:
                            pass
                    nrow_box[0] += 128
                    for c in range(8):
                        ac = accs_[c % 2]; ack = 'pe_acc%d' % (c % 2)
                        k.op('dve', lambda e: e.tensor_tensor(out=ac[:, :], in0=ps12[c][:, :], in1=x1t[:, c * 512:(c + 1) * 512], op=ALU.add),
                             r=['pe_x1'], w=['pe_ps%d' % c, ack])
                        k.dma('sp', ack, out_own[b * 128:(b + 1) * 128, c * 512:(c + 1) * 512], ac[:, :], r=[ack], w=['out'])

                nrow_box = [0]
                sel_matmul(0)
                for _ in sel_gen(0):
                    pass
                for b in range(NSLOT):
                    gnext = None
                    if b + 1 < NSLOT:
                        sel_matmul(b + 1)
                        gnext = sel_gen(b + 1)
                    expert_loop(b, gnext)
                k.barrier()
            k.mem = es

        k.barrier()
        print("instructions:", k.ninst)
    return nc


def _blocks(c):
    return [c, 15 - c, 16 + c, 31 - c, 32 + c, 47 - c, 48 + c, 63 - c]


def make_in_maps(inp):
    f = lambda a: np.ascontiguousarray(np.asarray(a), dtype=np.float32)
    x = f(inp["x"])[0]; pos = np.asarray(inp["positions"])[0].astype(np.int32)
    common = {
        "x_all": x, "pos_all": np.ascontiguousarray(pos.reshape(NTB, 128).T),
        "ln1_g": f(inp["ln1_g"]), "ln2_g": f(inp["ln2_g"]), "w_in": f(inp["w_in"])[0], "w_o": f(inp["w_o"])[0],
        "peer_wq": f(inp["peer_wq"])[0], "ffb": f(inp["fox_forget_b"]), "gq_f": f(inp["fox_qn_g"]), "gk_f": f(inp["fox_kn_g"]),
        "gq_d": f(inp["dsa_qn_g"]), "gk_d": f(inp["dsa_kn_g"]), "peer_u": f(inp["peer_u"])[0], "peer_v": f(inp["peer_v"])[0],
    }
    k1 = f(inp["peer_keys1"])[0]; k2 = f(inp["peer_keys2"])[0]
    kkm = np.zeros((8, 128, 256), np.float32)
    for h in range(8):
        kkm[h, 0:64, 0:128] = k1[h].T; kkm[h, 64:128, 128:256] = k2[h].T
    common["kk"] = kkm
    common["c_ident"] = np.eye(128, dtype=np.float32)
    common["c_tri"] = np.triu(np.ones((128, 128), np.float32))
    sel = np.zeros((16, NH * 128), np.float32)
    for h in range(NH):
        sel[h, h * 128:(h + 1) * 128] = 1.0
    common["c_sel16"] = sel
    inv128 = (10000.0 ** (-np.arange(0, 128, 2, dtype=np.float32) / np.float32(128))).astype(np.float32)
    inv64 = (10000.0 ** (-np.arange(0, 64, 2, dtype=np.float32) / np.float32(64))).astype(np.float32)
    common["c_inv"] = np.concatenate([inv128, inv64])[None, :].astype(np.float32)
    common["c_kiota"] = np.arange(1024, dtype=np.float32)[None, :]
    maps = []
    for c in range(8):
        blks = _blocks(c)
        rows = np.concatenate([np.arange(b * 128, (b + 1) * 128) for b in blks])
        m = dict(common)
        m["x_own"] = np.ascontiguousarray(x[rows])
        m["pos_own"] = np.ascontiguousarray(pos[rows].reshape(NSLOT, 128).T)
        m["qidx_p"] = np.ascontiguousarray(rows.astype(np.float32).reshape(NSLOT, 128).T)
        sb = np.zeros((NSLOT, NTB), np.float32)
        for s, b in enumerate(blks):
            sb[s, b] = 1.0
        m["selblk"] = sb.reshape(1, -1)
        maps.append(m)
    return maps


def kernel(**inputs):
    maps = make_in_maps(inputs)
    nc = build()
    res = run_bass_kernel_spmd(nc, maps, core_ids=list(range(8)))
    out = np.zeros((1, S, D), np.float32)
    for c in range(8):
        o = res.results[c]["out_own"]
        for s, b in enumerate(_blocks(c)):
            out[0, b * 128:(b + 1) * 128] = o[s * 128:(s + 1) * 128]
    return out
```

```python
import numpy as np
from contextlib import ExitStack
import concourse.bass as bass
import concourse.mybir as mybir
from concourse.bass_utils import run_bass_kernel_spmd

F32 = mybir.dt.float32; BF16 = mybir.dt.bfloat16; I32 = mybir.dt.int32; U32 = mybir.dt.uint32
AF = mybir.ActivationFunctionType; ALU = mybir.AluOpType; AX = mybir.AxisListType

D = 4096; S = 8192; NTB = 64; NSLOT = 8; HD = 128; NH = 16
INW = 9568
C_FQ, C_FK, C_FV, C_FF, C_DQ, C_DK, C_DV, C_IQ, C_IK, C_IW = 0, 2048, 4096, 6144, 6160, 8208, 8336, 8464, 9488, 9552
EPS = 1e-6
NEG = -30000.0
TOPK = 256
NBIS = 18


class KB:
    def __init__(self, nc, es):
        self.nc = nc; self.es = es; self.mem = es
        self.eng = {'pe': nc.tensor, 'act': nc.scalar, 'dve': nc.vector, 'pool': nc.gpsimd, 'sp': nc.sync}
        self.sem = {}; self.cnt = {}
        for e in ('pe', 'act', 'dve', 'pool'):
            self.sem[e] = es.enter_context(nc.semaphore("sem_" + e)); self.cnt[e] = 0
        self.waited = {e: {} for e in self.eng}
        self.lastw = {}; self.readers = {}
        self.ninst = 0

    def sb(self, name, shape, dt):
        return self.mem.enter_context(self.nc.sbuf_tensor(name, shape, dt))

    def ps(self, name, shape, dt):
        return self.mem.enter_context(self.nc.psum_tensor(name, shape, dt))

    def _wait(self, e, deps):
        best = {}
        for (src, n) in deps:
            if src is None:
                continue
            if src == 'pe' and e == 'pe':
                continue
            if n > best.get(src, 0):
                best[src] = n
        for src, n in best.items():
            if self.waited[e].get(src, 0) >= n:
                continue
            self.waited[e][src] = n
            val = n * 16 if src.startswith('dma:') else n
            self.eng[e].wait_ge(self.sem[src], val); self.ninst += 1

    def _deps(self, r, w):
        deps = []
        for k in r:
            deps.append(self.lastw.get(k, (None, 0)))
        for k in w:
            deps.append(self.lastw.get(k, (None, 0)))
            deps.extend(self.readers.get(k, []))
        return deps

    def _commit(self, tag, r, w):
        for k in r:
            self.readers.setdefault(k, []).append(tag)
        for k in w:
            self.lastw[k] = tag; self.readers[k] = []

    def op(self, e, fn, r=(), w=()):
        self._wait(e, self._deps(r, w))
        inst = fn(self.eng[e]); self.ninst += 1
        self.cnt[e] += 1
        inst.then_inc(self.sem[e], 1)
        self._commit((e, self.cnt[e]), r, w)

    def _dsem(self, key):
        km = self.__dict__.setdefault('keymap', {}); fr = self.__dict__.setdefault('free', [])
        if key in km:
            return km[key]
        if fr:
            src = fr.pop()
        else:
            src = 'dma:%d' % len([s for s in self.sem if s.startswith('dma:')])
            self.sem[src] = self.es.enter_context(self.nc.semaphore("sd_%s" % src[4:])); self.cnt[src] = 0
        km[key] = src
        return src

    def dma(self, q, key, out, in_, r=(), w=(), **kw):
        src = self._dsem(key)
        deps = self._deps(r, w)
        deps.append((src, self.cnt[src]))
        self._wait(q, deps)
        inst = self.eng[q].dma_start(out=out, in_=in_, **kw); self.ninst += 1
        self.cnt[src] += 1
        inst.then_inc(self.sem[src], 16)
        self._commit((src, self.cnt[src]), r, w)

    def gather(self, key, out, table, idx_ap, r=(), w=()):
        src = self._dsem(key)
        deps = self._deps(r, w)
        deps.append((src, self.cnt[src]))
        self._wait('pool', deps)
        inst = self.nc.gpsimd.indirect_dma_start(out=out, out_offset=None, in_=table,
                                                 in_offset=bass.IndirectOffsetOnAxis(ap=idx_ap, axis=0))
        self.ninst += 1
        self.cnt[src] += 1
        inst.then_inc(self.sem[src], 16)
        self._commit((src, self.cnt[src]), r, w)

    def barrier(self):
        srcs = [(s, n) for s, n in self.cnt.items() if n > 0]
        for e in self.eng:
            self._wait(e, srcs)
        self.lastw = {}; self.readers = {}
        self.keymap = {}; self.free = [s for s in self.sem if s.startswith('dma:')]


def build(dbg=None):
    nc = bass.Bass("TRN2", target_bir_lowering=False)
    dbg = dbg or {}
    STOP = dbg.get('stop', 99)

    def din(name, shape, dt=F32):
        return nc.dram_tensor(name, shape, dt, kind="ExternalInput").ap()

    def dscr(name, shape, dt):
        kind = "ExternalOutput" if name in dbg.get('dump', ()) else "Internal"
        return nc.dram_tensor(name, shape, dt, kind=kind).ap()

    x_all = din("x_all", [S, D]); x_own = din("x_own", [1024, D])
    pos_all = din("pos_all", [128, NTB], I32); pos_own = din("pos_own", [128, NSLOT], I32)
    qidx_p = din("qidx_p", [128, NSLOT]); selblk = din("selblk", [1, NSLOT * NTB])
    ln1_g = din("ln1_g", [1, D]); ln2_g = din("ln2_g", [1, D])
    w_in = din("w_in", [D, INW]); w_o = din("w_o", [D, D]); wq = din("peer_wq", [D, 1024])
    ffb = din("ffb", [1, NH]); gq_f = din("gq_f", [1, HD]); gk_f = din("gk_f", [1, HD])
    gq_d = din("gq_d", [1, HD]); gk_d = din("gk_d", [1, HD])
    kk = din("kk", [8, 128, 256]); pu = din("peer_u", [16384, D]); pv = din("peer_v", [16384, D])
    c_ident = din("c_ident", [128, 128]); c_tri = din("c_tri", [128, 128]); c_sel16 = din("c_sel16", [16, NH * 128])
    c_inv = din("c_inv", [1, 96]); c_kiota = din("c_kiota", [1, 1024])
    out_own = nc.dram_tensor("out_own", [1024, D], F32, kind="ExternalOutput").ap()

    hT_all = dscr("hT_all", [NTB, 128, D], BF16); hT_own = dscr("hT_own", [NSLOT, 128, D], BF16)
    fkT = dscr("fkT", [NH, 128, S], BF16); fv = dscr("fv", [NH, 128, NTB, 128], BF16)
    dkT = dscr("dkT", [128, S], BF16); dv = dscr("dv", [128, NTB, 128], BF16); ikT = dscr("ikT", [64, S], BF16)
    fqT = dscr("fqT", [NH, 128, 1024], BF16); dqT = dscr("dqT", [128, NSLOT, NH, 128], BF16)
    iqT = dscr("iqT", [64, NSLOT, NH, 128], BF16)
    mix = dscr("mix", [NSLOT, 128, D], BF16); mixT = dscr("mixT", [NSLOT, 128, D], BF16)
    x1 = dscr("x1", [1024, D], F32); h2b = dscr("h2b", [1024, D], BF16); uvb = dscr("uvb", [16384, 2 * D], BF16); h2T = dscr("h2T", [NSLOT, 128, D], BF16)
    qTs = dscr("qTs", [NSLOT, 128, 1024], BF16)
    dbg_nc = dscr("dbg_nc", [128, NTB * NH], F32)
    dbg_tau = dscr("dbg_tau", [128, NSLOT], F32)

    with ExitStack() as es:
        k = KB(nc, es)
        ident_f = k.sb("ident_f", [128, 128], F32); ident_b = k.sb("ident_b", [128, 128], BF16)
        tri_f = k.sb("tri_f", [128, 128], F32); ones_f = k.sb("ones_f", [128, 128], F32)
        sel16 = k.sb("sel16", [16, NH * 128], BF16); sel16f = k.sb("sel16f", [16, NH * 128], F32)
        inv_t = k.sb("inv_t", [128, 96], F32)
        nc_all = k.sb("nc_all", [128, NTB * NH], F32)
        aw = k.sb("aw", [128, NSLOT * NH], F32)
        qidx_t = k.sb("qidx_t", [128, NSLOT], F32)
        k.dma('sp', 'c0', ident_f[:], c_ident[:, :], w=['ident_f'])
        k.dma('sp', 'c1', tri_f[:], c_tri[:, :], w=['tri_f'])
        k.dma('sp', 'c2', sel16f[:], c_sel16[:, :], w=['sel16f'])
        k.dma('sp', 'c3', inv_t[:], c_inv[0:1, :].partition_broadcast(128), w=['inv_t'])
        k.dma('sp', 'c4', qidx_t[:], qidx_p[:, :], w=['qidx_t'])
        k.op('dve', lambda e: e.tensor_copy(out=ident_b[:], in_=ident_f[:]), r=['ident_f'], w=['ident_b'])
        k.op('dve', lambda e: e.tensor_copy(out=sel16[:], in_=sel16f[:]), r=['sel16f'], w=['sel16'])
        k.op('dve', lambda e: e.memset(ones_f[:], 1.0), w=['ones_f'])

        def rope_tables(m, cos_t, sin_t, posf, ncol, n, inv_off, half, tmp):
            nb = ncol; W = nb * half
            ang = tmp[:, 0:W]; nn = tmp[:, W:2 * W]; ni = tmp[:, 2 * W:3 * W].bitcast(I32); t4 = tmp[:, 3 * W:4 * W]
            for (dst, shift) in ((sin_t, 0.0), (cos_t, np.pi / 2)):
                k.op('dve', lambda e: e.tensor_tensor(out=ang.rearrange("p (b d) -> p b d", b=nb), in0=inv_t[:, inv_off:inv_off + half].unsqueeze(1).to_broadcast([128, nb, half]),
                                                     in1=posf.unsqueeze(2).to_broadcast([128, nb, half]), op=ALU.mult), r=['inv_t', m + 'posf'], w=[m + 'rt'])
                if shift != 0.0:
                    k.op('dve', lambda e: e.tensor_scalar(out=ang, in0=ang, scalar1=float(shift), scalar2=None, op0=ALU.add), r=[m + 'rt'], w=[m + 'rt'])
                k.op('dve', lambda e: e.tensor_scalar(out=nn, in0=ang, scalar1=float(1.0 / (2 * np.pi)), scalar2=None, op0=ALU.mult), r=[m + 'rt'], w=[m + 'rt'])
                k.op('dve', lambda e: e.tensor_copy(out=ni, in_=nn), r=[m + 'rt'], w=[m + 'rt'])
                k.op('dve', lambda e: e.tensor_copy(out=nn, in_=ni), r=[m + 'rt'], w=[m + 'rt'])
                k.op('dve', lambda e: e.scalar_tensor_tensor(out=ang, in0=nn, scalar=-6.28125, in1=ang, op0=ALU.mult, op1=ALU.add), r=[m + 'rt'], w=[m + 'rt'])
                k.op('dve', lambda e: e.scalar_tensor_tensor(out=ang, in0=nn, scalar=float(-(2 * np.pi - 6.28125)), in1=ang, op0=ALU.mult, op1=ALU.add), r=[m + 'rt'], w=[m + 'rt'])
                k.op('dve', lambda e: e.tensor_scalar(out=t4, in0=ang, scalar1=float(np.pi), scalar2=float(-2 * np.pi), op0=ALU.is_gt, op1=ALU.mult), r=[m + 'rt'], w=[m + 'rt'])
                k.op('dve', lambda e: e.tensor_tensor(out=ang, in0=ang, in1=t4, op=ALU.add), r=[m + 'rt'], w=[m + 'rt'])
                k.op('dve', lambda e: e.tensor_scalar(out=t4, in0=ang, scalar1=float(-np.pi), scalar2=float(2 * np.pi), op0=ALU.is_lt, op1=ALU.mult), r=[m + 'rt'], w=[m + 'rt'])
                k.op('dve', lambda e: e.tensor_tensor(out=ang, in0=ang, in1=t4, op=ALU.add), r=[m + 'rt'], w=[m + 'rt'])
                k.op('dve', lambda e: e.tensor_scalar(out=ang, in0=ang, scalar1=3.14159, scalar2=-3.14159, op0=ALU.min, op1=ALU.max), r=[m + 'rt'], w=[m + 'rt'])
                k.op('act', lambda e, dst=dst: e.activation(out=dst, in_=ang, func=AF.Sin), r=[m + 'rt'], w=[m + 'tab'])

        def rope_apply(eng, m, out_ap, in_ap, cos_t, sin_t, nh, half, t1, t2):
            x1v = in_ap[:, :, 0:half]; x2v = in_ap[:, :, half:2 * half]
            cb = cos_t.unsqueeze(1).to_broadcast([128, nh, half]); sbc = sin_t.unsqueeze(1).to_broadcast([128, nh, half])
            rk = [m + 'ropein', m + 'tab']
            k.op(eng, lambda e: e.tensor_tensor(out=t1, in0=x1v, in1=cb, op=ALU.mult), r=rk, w=[m + 't1'])
            k.op(eng, lambda e: e.tensor_tensor(out=t2, in0=x2v, in1=sbc, op=ALU.mult), r=rk, w=[m + 't2'])
            k.op(eng, lambda e: e.tensor_tensor(out=out_ap[:, :, 0:half], in0=t1, in1=t2, op=ALU.subtract), r=[m + 't1', m + 't2'], w=[m + 'ropeout'])
            k.op(eng, lambda e: e.tensor_tensor(out=t1, in0=x2v, in1=cb, op=ALU.mult), r=rk, w=[m + 't1'])
            k.op(eng, lambda e: e.tensor_tensor(out=t2, in0=x1v, in1=sbc, op=ALU.mult), r=rk, w=[m + 't2'])
            k.op(eng, lambda e: e.tensor_tensor(out=out_ap[:, :, half:2 * half], in0=t1, in1=t2, op=ALU.add), r=[m + 't1', m + 't2'], w=[m + 'ropeout'])

        def norm_transpose_phase(tag, src, nblk, gain_dram, dstT, dst_f32=None, do_norm=True, src_bf16=False, dst_b16=None):
            with ExitStack() as ph:
                k.mem = ph
                g1 = k.sb(tag + "g1", [128, D], F32) if do_norm else None
                xts = [k.sb(tag + "xt%d" % i, [128, D], BF16 if src_bf16 else F32) for i in range(2)]
                hbs = [k.sb(tag + "hb%d" % i, [128, D], BF16) for i in range(2)] if not src_bf16 else xts
                hTs = [k.sb(tag + "hT%d" % i, [128, D], BF16) for i in range(2)]
                hfs = [k.sb(tag + "hf%d" % i, [128, D], F32) for i in range(2)] if dst_f32 is not None else None
                junk = k.sb(tag + "junk", [128, D], BF16)
                st = k.sb(tag + "st", [128, 8], F32)
                pts = [k.ps(tag + "pt%d" % i, [128, 1024], BF16) for i in range(4)]
                if do_norm:
                    k.dma('sp', tag + 'g', g1[:], gain_dram[0:1, :].partition_broadcast(128), w=[tag + 'g1'])
                for b in range(nblk):
                    i = b % 2
                    xt = xts[i]; hb = hbs[i]; hT = hTs[i]
                    k.dma('sp', tag + 'x%d' % i, xt[:], src[b], w=[tag + 'xt%d' % i])
                    if do_norm:
                        c0 = (b % 2) * 4
                        k.op('act', lambda e: e.activation(out=junk[:], in_=xt[:], func=AF.Square, accum_out=st[:, c0:c0 + 1]),
                             r=[tag + 'xt%d' % i], w=[tag + 'junk', tag + 'st%d' % i])
                        k.op('dve', lambda e: e.tensor_scalar(out=st[:, c0 + 1:c0 + 2], in0=st[:, c0:c0 + 1], scalar1=1.0 / D, scalar2=EPS,
                                                             op0=ALU.mult, op1=ALU.add), r=[tag + 'st%d' % i], w=[tag + 'st%d' % i])
                        k.op('act', lambda e: e.activation(out=st[:, c0 + 2:c0 + 3], in_=st[:, c0 + 1:c0 + 2], func=AF.Ln), r=[tag + 'st%d' % i], w=[tag + 'st%d' % i])
                        k.op('act', lambda e: e.activation(out=st[:, c0 + 3:c0 + 4], in_=st[:, c0 + 2:c0 + 3], func=AF.Exp, scale=-0.5), r=[tag + 'st%d' % i], w=[tag + 'st%d' % i])
                        if dst_f32 is not None:
                            hf = hfs[i]
                            k.op('dve', lambda e: e.scalar_tensor_tensor(out=hf[:], in0=xt[:], scalar=st[:, c0 + 3:c0 + 4], in1=g1[:], op0=ALU.mult, op1=ALU.mult),
                                 r=[tag + 'xt%d' % i, tag + 'st%d' % i, tag + 'g1'], w=[tag + 'hf%d' % i])
                            k.op('pool', lambda e: e.tensor_copy(out=hb[:], in_=hf[:]), r=[tag + 'hf%d' % i], w=[tag + 'hb%d' % i])
                            k.dma('act', tag + 'hfo%d' % i, dst_f32[b], hf[:], r=[tag + 'hf%d' % i], w=[tag + 'dstf'])
                        else:
                            k.op('dve', lambda e: e.scalar_tensor_tensor(out=hb[:], in0=xt[:], scalar=st[:, c0 + 3:c0 + 4], in1=g1[:], op0=ALU.mult, op1=ALU.mult),
                                 r=[tag + 'xt%d' % i, tag + 'st%d' % i, tag + 'g1'], w=[tag + 'hb%d' % i])
                            if dst_b16 is not None:
                                k.dma('act', tag + 'hbo%d' % i, dst_b16[b], hb[:], r=[tag + 'hb%d' % i], w=[tag + 'dstb'])
                    hkey = (tag + 'hb%d' % i) if not src_bf16 else (tag + 'xt%d' % i)
                    for q in range(4):
                        for j in range(8):
                            kc = q * 8 + j
                            k.op('pe', lambda e: e.transpose(pts[q][:, j * 128:(j + 1) * 128], hb[:, kc * 128:(kc + 1) * 128], ident_b[:]),
                                 r=[hkey, 'ident_b'], w=[tag + 'pt%d' % q])
                        eng = 'act' if q % 2 == 0 else 'dve'
                        if eng == 'act':
                            k.op('act', lambda e: e.copy(out=hT[:, q * 1024:(q + 1) * 1024], in_=pts[q][:, :]), r=[], w=[tag + 'pt%d' % q, tag + 'hT%d' % i])
                        else:
                            k.op('dve', lambda e: e.tensor_copy(out=hT[:, q * 1024:(q + 1) * 1024], in_=pts[q][:, :]), r=[], w=[tag + 'pt%d' % q, tag + 'hT%d' % i])
                    k.dma('pool', tag + 'ho%d' % i, dstT[b], hT[:], r=[tag + 'hT%d' % i], w=[tag + 'dstT'])
                k.barrier()
            k.mem = es

        norm_transpose_phase("na", [x_all[b * 128:(b + 1) * 128, :] for b in range(NTB)], NTB, ln1_g, [hT_all[b] for b in range(NTB)])
        norm_transpose_phase("no", [x_own[b * 128:(b + 1) * 128, :] for b in range(NSLOT)], NSLOT, ln1_g, [hT_own[b] for b in range(NSLOT)])

        def project(tag, hT_src, nblk, wdram, chunks, post, pre=None, nht_buf=3, nps_buf=3, resident=False):
            wfs = [k.sb(tag + "wf%d" % i, [128, 1, 512], F32) for i in range(4)]
            wbs = [k.sb(tag + "wb%d" % i, [128, 32, 512], BF16) for i in range(2)]
            hts = [k.sb(tag + "ht%d" % i, [128, D], BF16) for i in range(nblk if resident else nht_buf)]
            pss = [k.ps(tag + "ps%d" % i, [128, 512], F32) for i in range(nps_buf)]
            wview = wdram.rearrange("(kc p) c -> p kc c", p=128)
            nps = 0; nht = 0; pending = None
            st_ = {'nwf': 0}

            def weight_pieces(ci):
                name_, pieces_, ncols_ = chunks[ci]
                wb_ = wbs[ci % 2]; wbk_ = tag + 'wb%d' % (ci % 2)
                for j in range(32):
                    nwf = st_['nwf']; st_['nwf'] += 1
                    wf = wfs[nwf % 4]; wfk = tag + 'wf%d' % (nwf % 4)
                    for pi, (doff, scol, n) in enumerate(pieces_):
                        k.dma('sp', wfk + 'p%d' % pi, wf[:, :, doff:doff + n], wview[:, j:j + 1, scol:scol + n], w=[wfk])
                    eng = ('dve', 'pool', 'act')[j % 3]
                    if eng == 'act':
                        k.op('act', lambda e: e.copy(out=wb_[:, j:j + 1, 0:ncols_], in_=wf[:, :, 0:ncols_]), r=[wfk], w=[wbk_])
                    else:
                        k.op(eng, lambda e: e.tensor_copy(out=wb_[:, j:j + 1, 0:ncols_], in_=wf[:, :, 0:ncols_]), r=[wfk], w=[wbk_])
                    yield
            for _ in weight_pieces(0):
                pass
            per_blk = -(-32 // max(1, nblk - 1))
            for ci, (name, pieces, ncols) in enumerate(chunks):
                wb = wbs[ci % 2]; wbk = tag + 'wb%d' % (ci % 2)
                wnext = weight_pieces(ci + 1) if ci + 1 < len(chunks) else None
                if pre is not None:
                    pre(name)
                for b in range(nblk):
                    if resident:
                        ht = hts[b]; htk = tag + 'ht%d' % b
                        if ci == 0:
                            k.dma('pool', htk, ht[:], hT_src[b], w=[htk])
                    else:
                        ht = hts[nht % nht_buf]; htk = tag + 'ht%d' % (nht % nht_buf); nht += 1
                        k.dma('sp', htk, ht[:], hT_src[b], w=[htk])
                    ps = pss[nps % nps_buf]; psk = tag + 'ps%d' % (nps % nps_buf); nps += 1
                    for kc in range(32):
                        k.op('pe', lambda e: e.matmul(ps[:, 0:ncols], lhsT=ht[:, kc * 128:(kc + 1) * 128], rhs=wb[:, kc, 0:ncols],
                                                      start=(kc == 0), stop=(kc == 31)), r=[htk, wbk], w=[psk])
                    if pending is not None:
                        post(*pending)
                    pending = (name, b, ps, psk)
                    if wnext is not None and b >= 1:
                        for _ in range(per_blk):
                            if next(wnext, 'end') == 'end':
                                wnext = None
                                break
                if wnext is not None:
                    for _ in wnext:
                        pass
            if pending is not None:
                post(*pending)

        def make_post_env(tag, nblk, posf_tile, pos_key, light=False, stg=True, share=None):
            env = {}
            env['ssq'] = k.sb(tag + "ssq", [128, 16], F32)
            env['yb'] = k.sb(tag + "yb", [128, 4, 128], BF16)
            if not light:
                env['yf'] = k.sb(tag + "yf", [128, 4, 128], F32)
                env['yr'] = k.sb(tag + "yr", [128, 4, 128], F32)
                env['t1'] = k.sb(tag + "t1", [128, 8, 64], F32); env['t2'] = k.sb(tag + "t2", [128, 8, 64], F32)
            env['junk'] = k.sb(tag + "pj", [128, 128], F32)
            if stg:
                env['stgT'] = [k.sb(tag + "stgT%d" % i, [128, 4, 512], BF16) for i in range(2)]
                env['stgV'] = [k.sb(tag + "stgV%d" % i, [128, 4, 4, 128], BF16) for i in range(2)]
            env['ptT'] = k.ps(tag + "ptT", [128, 1024], BF16)
            env['gt'] = {}
            if share is not None:
                for nm_ in ('cos', 'sin', 'cos2', 'sin2'):
                    env[nm_] = share[nm_]
                k.lastw[tag + 'tab'] = k.lastw.get(share['tag'] + 'tab', (None, 0))
                return env
            env['tag'] = tag
            env['cos'] = k.sb(tag + "cos", [128, nblk * 64], F32); env['sin'] = k.sb(tag + "sin", [128, nblk * 64], F32)
            env['cos2'] = k.sb(tag + "cos2", [128, nblk * 32], F32); env['sin2'] = k.sb(tag + "sin2", [128, nblk * 32], F32)
            env['rtmp'] = k.sb(tag + "rtmp", [128, 2048], F32)
            def gen_tables():
                for b in range(0, nblk, 8):
                    rope_tables(tag, env['cos'][:, b * 64:(b + 8) * 64], env['sin'][:, b * 64:(b + 8) * 64], posf_tile[:, b:b + 8], 8, None, 0, 64, env['rtmp'])
                    rope_tables(tag, env['cos2'][:, b * 32:(b + 8) * 32], env['sin2'][:, b * 32:(b + 8) * 32], posf_tile[:, b:b + 8], 8, None, 64, 32, env['rtmp'])
            if light:
                env['gen_tables'] = gen_tables
            else:
                gen_tables()
            return env

        def load_gain(tag, env, name, gdram, mul):
            gt = k.sb(tag + "g_" + name, [128, 128], F32)
            k.dma('sp', tag + 'g_' + name, gt[:], gdram[0:1, :].partition_broadcast(128), w=[tag + 'g_' + name])
            if mul != 1.0:
                k.op('dve', lambda e: e.tensor_scalar(out=gt[:], in0=gt[:], scalar1=float(mul), scalar2=None, op0=ALU.mult), r=[], w=[tag + 'g_' + name])
            env['gt'][name] = (gt, tag + 'g_' + name)

        def post_norm_heads(tag, env, ps, psk, nh, gname, rope_b):
            ssq = env['ssq']; gt, gk = env['gt'][gname]
            for h in range(nh):
                k.op('act', lambda e: e.activation(out=env['junk'][:], in_=ps[:, h * 128:(h + 1) * 128], func=AF.Square, accum_out=ssq[:, h:h + 1]),
                     r=[], w=[psk, tag + 'pj', tag + 'ssq'])
            k.op('dve', lambda e: e.tensor_scalar(out=ssq[:, 4:4 + nh], in0=ssq[:, 0:nh], scalar1=1.0 / HD, scalar2=EPS, op0=ALU.mult, op1=ALU.add), r=[], w=[tag + 'ssq'])
            k.op('act', lambda e: e.activation(out=ssq[:, 8:8 + nh], in_=ssq[:, 4:4 + nh], func=AF.Ln), r=[], w=[tag + 'ssq'])
            k.op('act', lambda e: e.activation(out=ssq[:, 12:12 + nh], in_=ssq[:, 8:8 + nh], func=AF.Exp, scale=-0.5), r=[], w=[tag + 'ssq'])
            dst = env['yf'] if rope_b is not None else env['yb']
            dk_ = tag + ('ropein' if rope_b is not None else 'yb')
            for h in range(nh):
                k.op('dve', lambda e: e.scalar_tensor_tensor(out=dst[:, h, :], in0=ps[:, h * 128:(h + 1) * 128], scalar=ssq[:, 12 + h:13 + h], in1=gt[:],
                                                            op0=ALU.mult, op1=ALU.mult), r=[tag + 'ssq', gk], w=[psk, dk_])
            if rope_b is not None:
                b = rope_b
                rope_apply('pool', tag, env['yr'][:, 0:nh, :], env['yf'][:, 0:nh, :], env['cos'][:, b * 64:(b + 1) * 64], env['sin'][:, b * 64:(b + 1) * 64],
                           nh, 64, env['t1'][:, 0:nh, :], env['t2'][:, 0:nh, :])
                k.op('pool', lambda e: e.tensor_copy(out=env['yb'][:, 0:nh, :], in_=env['yr'][:, 0:nh, :]), r=[tag + 'ropeout'], w=[tag + 'yb'])

        def transpose_heads_to_stg(tag, env, nh, b4, stg, stgk, width=128, src=None, srck=None):
            src = env['yb'] if src is None else src; srck = (tag + 'yb') if srck is None else srck
            pt = env['ptT']
            for h in range(nh):
                k.op('pe', lambda e: e.transpose(pt[0:width, h * 128:(h + 1) * 128], src[:, h, 0:width], ident_b[:]), r=[srck, 'ident_b'], w=[tag + 'ptT'])
            k.op('act', lambda e: e.copy(out=stg[0:width, 0:nh, b4 * 128:(b4 + 1) * 128],
                                         in_=pt[0:width, 0:nh * 128].rearrange("p (h t) -> p h t", h=nh)), r=[], w=[tag + 'ptT', stgk])

        if STOP >= 2:
          with ExitStack() as ph:
            k.mem = ph
            tag = "kp"
            posi = k.sb("kp_posi", [128, NTB], I32); posf = k.sb("kp_posf", [128, NTB], F32)
            k.dma('sp', 'kp_pos', posi[:], pos_all[:, :], w=['kp_posi'])
            k.op('dve', lambda e: e.tensor_copy(out=posf[:], in_=posi[:]), r=['kp_posi'], w=['kpposf'])
            env = make_post_env(tag, NTB, posf, 'kpposf', light=True)
            load_gain(tag, env, 'gk_f', gk_f, 1.0); load_gain(tag, env, 'gk_d', gk_d, 1.0)
            fbt = k.sb("kp_fb", [128, NH], F32)
            k.dma('sp', 'kp_fb', fbt[:], ffb[0:1, :].partition_broadcast(128), w=['kp_fb'])
            prefix = k.sb("kp_prefix", [128, NH], F32); nlf = k.sb("kp_nlf", [128, NH], F32)
            k.op('dve', lambda e: e.memset(prefix[:], 0.0), w=['kp_prefix'])
            psc = k.ps("kp_psc", [128, 512], F32)
            stgI = [k.sb("kp_stgI%d" % i, [64, 1, 512], BF16) for i in range(2)]
            stgD = [k.sb("kp_stgD%d" % i, [128, 1, 512], BF16) for i in range(2)]
            stgDV = [k.sb("kp_stgDV%d" % i, [128, 4, 128], BF16) for i in range(2)]
            xms = [k.sb("kp_xm%d" % i, [128, 4, 336], F32) for i in range(2)]
            nlf4 = k.sb("kp_nlf4", [128, 4, NH], F32); cs4 = k.sb("kp_cs4", [128, 128], F32)
            sq4 = k.sb("kp_sq4", [128, 4, 128], F32); ss4 = k.sb("kp_ss4", [128, 16], F32)
            dkb4 = k.sb("kp_dkb4", [128, 4, 128], BF16); ikb4 = k.sb("kp_ikb4", [128, 4, 64], BF16)
            r4a = k.sb("kp_r4a", [128, 4, 64], F32); r4b = k.sb("kp_r4b", [128, 4, 64], F32)

            def rope4(m, outb, xin, xin_key, cosv, sinv, half):
                x1v = xin[:, :, 0:half]; x2v = xin[:, :, half:2 * half]
                ta = r4a[:, :, 0:half]; tb = r4b[:, :, 0:half]
                rk = [xin_key, tag + 'tab']
                k.op('dve', lambda e: e.tensor_tensor(out=ta, in0=x1v, in1=cosv, op=ALU.mult), r=rk, w=['kp_r4a'])
                k.op('pool', lambda e: e.tensor_tensor(out=tb, in0=x2v, in1=sinv, op=ALU.mult), r=rk, w=['kp_r4b'])
                k.op('dve', lambda e: e.tensor_tensor(out=outb[:, :, 0:half], in0=ta, in1=tb, op=ALU.subtract), r=['kp_r4a', 'kp_r4b'], w=[m + '_out'])
                k.op('dve', lambda e: e.tensor_tensor(out=ta, in0=x2v, in1=cosv, op=ALU.mult), r=rk, w=['kp_r4a'])
                k.op('pool', lambda e: e.tensor_tensor(out=tb, in0=x1v, in1=sinv, op=ALU.mult), r=rk, w=['kp_r4b'])
                k.op('dve', lambda e: e.tensor_tensor(out=outb[:, :, half:2 * half], in0=ta, in1=tb, op=ALU.add), r=['kp_r4a', 'kp_r4b'], w=[m + '_out'])

            deferred = []

            def post_k(name, b, ps, psk):
                b4 = b % 4; g4 = b // 4; i2 = g4 % 2
                if name.startswith('fk'):
                    h0 = int(name[2:]) * 4
                    post_norm_heads(tag, env, ps, psk, 4, 'gk_f', None)
                    stg = env['stgT'][i2]; stgk = tag + 'stgT%d' % i2
                    transpose_heads_to_stg(tag, env, 4, b4, stg, stgk)
                    if b4 == 3:
                        k.dma('pool', stgk, fkT[h0:h0 + 4, :, g4 * 512:(g4 + 1) * 512].rearrange("h d t -> d h t"), stg[:, :, :], r=[stgk], w=['fkT'])
                elif name.startswith('fv'):
                    h0 = int(name[2:]) * 4
                    stg = env['stgV'][i2]; stgk = tag + 'stgV%d' % i2
                    k.op('act', lambda e: e.copy(out=stg[:, :, b4, :], in_=ps[:, 0:512].rearrange("p (h d) -> p h d", h=4)), r=[], w=[psk, stgk])
                    if b4 == 3:
                        k.dma('pool', stgk, fv[h0:h0 + 4, :, g4 * 4:(g4 + 1) * 4, :].rearrange("h j b d -> j h b d"), stg[:, :, :, :], r=[stgk], w=['fv'])
                else:
                    xm = xms[i2]; xk = 'kp_xm%d' % i2
                    k.op('act', lambda e: e.copy(out=xm[:, b4, :], in_=ps[:, 0:336]), r=[], w=[psk, xk])
                    if b4 == 1:
                        while deferred:
                            deferred.pop(0)()
                    if b4 != 3:
                        return
                    b0 = g4 * 4
                    k.op('dve', lambda e: e.tensor_tensor(out=nlf4[:, :, :], in0=xm[:, :, 0:16], in1=fbt[:, :].unsqueeze(1).to_broadcast([128, 4, NH]), op=ALU.add), r=[xk, 'kp_fb'], w=['kp_nlf'])
                    k.op('act', lambda e: e.activation(out=nlf4[:, :, :], in_=nlf4[:, :, :], func=AF.Exp, scale=-1.0), r=[], w=['kp_nlf'])
                    k.op('act', lambda e: e.activation(out=nlf4[:, :, :], in_=nlf4[:, :, :], func=AF.Ln, bias=1.0), r=[], w=['kp_nlf'])
                    nl2 = nlf4[:, :, :].rearrange("p b h -> p (b h)")
                    k.op('pe', lambda e: e.matmul(psc[:, 0:64], lhsT=tri_f[:], rhs=nl2, start=True, stop=True), r=['tri_f', 'kp_nlf'], w=['kp_psc'])
                    k.op('pe', lambda e: e.matmul(psc[:, 64:128], lhsT=ones_f[:], rhs=nl2, start=True, stop=True), r=['ones_f', 'kp_nlf'], w=['kp_psc'])
                    k.op('dve', lambda e: e.tensor_copy(out=cs4[:, :], in_=psc[:, 0:128]), r=[], w=['kp_psc', 'kp_cs4'])
                    for j in range(4):
                        bb = b0 + j
                        k.op('dve', lambda e: e.tensor_tensor(out=nc_all[:, bb * NH:(bb + 1) * NH], in0=cs4[:, j * NH:(j + 1) * NH], in1=prefix[:], op=ALU.add), r=['kp_cs4', 'kp_prefix'], w=['nc_all'])
                        k.op('dve', lambda e: e.tensor_tensor(out=prefix[:], in0=cs4[:, 64 + j * NH:64 + (j + 1) * NH], in1=prefix[:], op=ALU.add), r=['kp_cs4'], w=['kp_prefix'])
                    xk_ = xm[:, :, 16:144]
                    k.op('pool', lambda e: e.tensor_tensor(out=sq4[:, :, :], in0=xk_, in1=xk_, op=ALU.mult), r=[xk], w=['kp_sq4'])
                    k.op('dve', lambda e: e.tensor_reduce(out=ss4[:, 0:4], in_=sq4[:, :, :], axis=AX.X, op=ALU.add), r=['kp_sq4'], w=['kp_ss4'])
                    k.op('dve', lambda e: e.tensor_scalar(out=ss4[:, 4:8], in0=ss4[:, 0:4], scalar1=1.0 / HD, scalar2=EPS, op0=ALU.mult, op1=ALU.add), r=[], w=['kp_ss4'])
                    k.op('act', lambda e: e.activation(out=ss4[:, 8:12], in_=ss4[:, 4:8], func=AF.Ln), r=[], w=['kp_ss4'])
                    k.op('act', lambda e: e.activation(out=ss4[:, 12:16], in_=ss4[:, 8:12], func=AF.Exp, scale=-0.5), r=[], w=['kp_ss4'])
                    gt_, gk_ = env['gt']['gk_d']
                    k.op('dve', lambda e: e.tensor_tensor(out=sq4[:, :, :], in0=xk_, in1=ss4[:, 12:16].unsqueeze(2).to_broadcast([128, 4, 128]), op=ALU.mult), r=[xk, 'kp_ss4'], w=['kp_sq4'])
                    k.op('dve', lambda e: e.tensor_tensor(out=sq4[:, :, :], in0=sq4[:, :, :], in1=gt_[:, :].unsqueeze(1).to_broadcast([128, 4, 128]), op=ALU.mult), r=[gk_], w=['kp_sq4'])
                    cos4 = env['cos'][:, b0 * 64:(b0 + 4) * 64].rearrange("p (b d) -> p b d", b=4); sin4 = env['sin'][:, b0 * 64:(b0 + 4) * 64].rearrange("p (b d) -> p b d", b=4)
                    rope4('kpd', dkb4, sq4, 'kp_sq4', cos4, sin4, 64)
                    sd = stgD[i2]; sdk = 'kp_stgD%d' % i2
                    pt = env['ptT']

                    def partB1():
                        for j in range(4):
                            k.op('pe', lambda e: e.transpose(pt[:, j * 128:(j + 1) * 128], dkb4[:, j, :], ident_b[:]), r=['kpd_out', 'ident_b'], w=[tag + 'ptT'])
                        k.op('act', lambda e: e.copy(out=sd[:, 0, :], in_=pt[:, 0:512]), r=[], w=[tag + 'ptT', sdk])
                        k.dma('pool', sdk, dkT[:, g4 * 512:(g4 + 1) * 512], sd[:, 0, :], r=[sdk], w=['dkT'])
                    sv = stgDV[i2]; svk = 'kp_stgDV%d' % i2
                    k.op('act', lambda e: e.copy(out=sv[:, :, :], in_=xm[:, :, 144:272]), r=[xk], w=[svk])
                    k.dma('pool', svk, dv[:, g4 * 4:(g4 + 1) * 4, :], sv[:, :, :], r=[svk], w=['dv'])
                    cos24 = env['cos2'][:, b0 * 32:(b0 + 4) * 32].rearrange("p (b d) -> p b d", b=4); sin24 = env['sin2'][:, b0 * 32:(b0 + 4) * 32].rearrange("p (b d) -> p b d", b=4)
                    rope4('kpi', ikb4, xm[:, :, 272:336], xk, cos24, sin24, 32)
                    si = stgI[i2]; sik = 'kp_stgI%d' % i2

                    def partB2():
                        for j in range(4):
                            k.op('pe', lambda e: e.transpose(pt[0:64, j * 128:(j + 1) * 128], ikb4[:, j, :], ident_b[:]), r=['kpi_out', 'ident_b'], w=[tag + 'ptT'])
                        k.op('act', lambda e: e.copy(out=si[:, 0, :], in_=pt[0:64, 0:512]), r=[], w=[tag + 'ptT', sik])
                        k.dma('pool', sik, ikT[:, g4 * 512:(g4 + 1) * 512], si[:, 0, :], r=[sik], w=['ikT'])
                    deferred.append(lambda: (partB1(), partB2()))

            k.lastw['kpitab'] = k.lastw.get(tag + 'tab', (None, 0))
            chunks = [('fk%d' % i, [(0, C_FK + i * 512, 512)], 512) for i in range(4)]
            chunks += [('fv%d' % i, [(0, C_FV + i * 512, 512)], 512) for i in range(4)]
            chunks += [('mixed', [(0, C_FF, 16), (16, C_DK, 128), (144, C_DV, 128), (272, C_IK, 64)], 336)]
            if 'kchunks' in dbg:
                chunks = [c for c in chunks if c[0] in dbg['kchunks']]
            project(tag, [hT_all[b] for b in range(dbg.get('nkb', NTB))], dbg.get('nkb', NTB), w_in, chunks, post_k, nht_buf=2, nps_buf=4,
                    pre=lambda nm: env['gen_tables']() if nm == chunks[min(1, len(chunks) - 1)][0] else None)
            while deferred:
                deferred.pop(0)()
            if 'dbg_nc' in dbg.get('dump', ()):
                k.dma('sp', 'dbgnc', dbg_nc[:, :], nc_all[:], r=['nc_all'], w=['dbg_nc'])
            k.barrier()
          k.mem = es


        if STOP >= 3:
          with ExitStack() as ph:
            k.mem = ph
            tag = "qp"
            posi = k.sb("qp_posi", [128, NSLOT], I32); posf = k.sb("qp_posf", [128, NSLOT], F32)
            k.dma('sp', 'qp_pos', posi[:], pos_own[:, :], w=['qp_posi'])
            k.op('dve', lambda e: e.tensor_copy(out=posf[:], in_=posi[:]), r=['qp_posi'], w=['qpposf'])
            k.lastw['qqposf'] = k.lastw['qpposf']
            env = make_post_env(tag, NSLOT, posf, 'qpposf', stg=False)
            load_gain(tag, env, 'gq_f', gq_f, HD ** -0.5); load_gain(tag, env, 'gq_d', gq_d, HD ** -0.5)
            envB = make_post_env("qq", NSLOT, posf, 'qqposf', stg=False, share=env)
            load_gain("qq", envB, 'gq_f', gq_f, HD ** -0.5); load_gain("qq", envB, 'gq_d', gq_d, HD ** -0.5)
            envs2 = ((tag, env), ("qq", envB))
            iqf = k.sb("qp_iqf", [128, 8, 64], F32); iqr = k.sb("qp_iqr", [128, 8, 64], F32); iqb = k.sb("qp_iqb", [128, 8, 64], BF16)
            stgQ = [k.sb("qp_stgQ%d" % i, [128, 4, 128], BF16) for i in range(2)]
            stgIQ = [k.sb("qp_stgIQ%d" % i, [64, 8, 128], BF16) for i in range(2)]
            k.lastw['qpitab'] = k.lastw.get(tag + 'tab', (None, 0))
            cnt = {'n': 0}

            def post_q(name, b, ps, psk):
                i2 = cnt['n'] % 2; cnt['n'] += 1
                if name == 'iw':
                    k.op('dve', lambda e: e.tensor_scalar(out=aw[:, b * NH:(b + 1) * NH], in0=ps[:, 0:16], scalar1=0.03125, scalar2=None, op0=ALU.mult), r=[], w=[psk, 'aw'])
                elif name.startswith('fq') or name.startswith('dq'):
                    h0 = int(name[2:]) * 4
                    isd = name.startswith('dq')
                    tg_, en_ = envs2[b % 2]
                    post_norm_heads(tg_, en_, ps, psk, 4, 'gq_d' if isd else 'gq_f', b if isd else None)
                    stg = stgQ[i2]; stgk = 'qp_stgQ%d' % i2
                    pt = en_['ptT']
                    for h in range(4):
                        k.op('pe', lambda e: e.transpose(pt[:, h * 128:(h + 1) * 128], en_['yb'][:, h, :], ident_b[:]), r=[tg_ + 'yb', 'ident_b'], w=[tg_ + 'ptT'])
                    k.op('act', lambda e: e.copy(out=stg[:, :, :], in_=pt[:, 0:512].rearrange("p (h t) -> p h t", h=4)), r=[], w=[tg_ + 'ptT', stgk])
                    if isd:
                        k.dma('pool', stgk, dqT[:, b, h0:h0 + 4, :], stg[:, :, :], r=[stgk], w=['dqT'])
                    else:
                        k.dma('pool', stgk, fqT[h0:h0 + 4, :, b * 128:(b + 1) * 128].rearrange("h d t -> d h t"), stg[:, :, :], r=[stgk], w=['fqT'])
                else:
                    h0 = int(name[2:]) * 8
                    k.op('act', lambda e: e.copy(out=iqf[:, :, :], in_=ps[:, 0:512].rearrange("p (h d) -> p h d", h=8)), r=[], w=[psk, 'qpiropein'])
                    rope_apply('pool', 'qpi', iqr[:, :, :], iqf[:, :, :], env['cos2'][:, b * 32:(b + 1) * 32], env['sin2'][:, b * 32:(b + 1) * 32], 8, 32,
                               env['t1'][:, :, 0:32], env['t2'][:, :, 0:32])
                    k.op('pool', lambda e: e.tensor_tensor(out=iqb[:, :, :], in0=iqr[:, :, :],
                                                          in1=aw[:, b * NH + h0:b * NH + h0 + 8].unsqueeze(2).to_broadcast([128, 8, 64]), op=ALU.mult),
                         r=['qpiropeout', 'aw'], w=['qp_iqb'])
                    pt = env['ptT']
                    for h in range(8):
                        k.op('pe', lambda e: e.transpose(pt[0:64, h * 128:(h + 1) * 128], iqb[:, h, :], ident_b[:]), r=['qp_iqb', 'ident_b'], w=[tag + 'ptT'])
                    stg = stgIQ[i2]; stgk = 'qp_stgIQ%d' % i2
                    k.op('act', lambda e: e.copy(out=stg[:, :, :], in_=pt[0:64, 0:1024].rearrange("p (h t) -> p h t", h=8)), r=[], w=[tag + 'ptT', stgk])
                    k.dma('pool', stgk, iqT[:, b, h0:h0 + 8, :], stg[:, :, :], r=[stgk], w=['iqT'])

            chunks = [('iw', [(0, C_IW, 16)], 16)]
            chunks += [('iq%d' % i, [(0, C_IQ + i * 512, 512)], 512) for i in range(2)]
            chunks += [('fq%d' % i, [(0, C_FQ + i * 512, 512)], 512) for i in range(4)]
            chunks += [('dq%d' % i, [(0, C_DQ + i * 512, 512)], 512) for i in range(4)]
            project(tag, [hT_own[b] for b in range(NSLOT)], NSLOT, w_in, chunks, post_q, resident=True)
            k.barrier()
          k.mem = es

        if STOP >= 4:
            kio = k.sb("kio", [128, 1024], F32)
            k.dma('sp', 'kio', kio[:], c_kiota[0:1, :].partition_broadcast(128), w=['kio'])
            qrel = k.sb("qrel", [128, NSLOT], F32)
            for s in range(NSLOT):
                k.op('dve', lambda e: e.tensor_scalar(out=qrel[:, s:s + 1], in0=qidx_t[:, s:s + 1], scalar1=float(-1024 * s), scalar2=None, op0=ALU.add), r=['qidx_t'], w=['qrel'])
            NMf = k.sb("NMf", [128, NSLOT, 1024], BF16)
            for s in range(NSLOT):
                k.op('dve', lambda e: e.tensor_scalar(out=NMf[:, s, :], in0=kio[:], scalar1=qrel[:, s:s + 1], scalar2=NEG, op0=ALU.is_gt, op1=ALU.mult), r=['kio', 'qrel'], w=['NMf'])
            cT = k.sb("cT", [16, 1024], BF16)
            with ExitStack() as ph:
                k.mem = ph
                selb = k.sb("selb", [128, NSLOT * NTB], F32); tmpc = k.sb("tmpc", [128, NH, NTB], F32)
                nco = k.sb("nco", [128, NH], F32); ncb = k.sb("ncb", [128, NH], BF16)
                ptc = k.ps("ptc", [128, 1024], BF16)
                k.dma('sp', 'selb', selb[:], selblk[0:1, :].partition_broadcast(128), w=['selb'])
                for s in range(NSLOT):
                    k.op('dve', lambda e: e.tensor_tensor(out=tmpc[:, :, :], in0=nc_all[:, :].rearrange("p (tb h) -> p h tb", h=NH),
                                                         in1=selb[:, s * NTB:(s + 1) * NTB].unsqueeze(1).to_broadcast([128, NH, NTB]), op=ALU.mult), r=['nc_all', 'selb'], w=['tmpc'])
                    k.op('dve', lambda e: e.tensor_reduce(out=nco[:, :], in_=tmpc[:, :, :], axis=AX.X, op=ALU.add), r=['tmpc'], w=['nco'])
                    k.op('dve', lambda e: e.tensor_scalar(out=ncb[:, :], in0=nco[:, :], scalar1=-1.0, scalar2=None, op0=ALU.mult), r=['nco'], w=['ncb'])
                    k.op('pe', lambda e: e.transpose(ptc[0:16, s * 128:(s + 1) * 128], ncb[:, :], ident_b[:]), r=['ncb', 'ident_b'], w=['ptc'])
                k.op('act', lambda e: e.copy(out=cT[:, :], in_=ptc[0:16, :]), r=[], w=['ptc', 'cT'])
                k.barrier()
            k.mem = es

        def attn_evacuate(tagp, accs, acckeys, stg, stgk, rden):
            for s_ in range(len(accs)):
                acc = accs[s_]
                k.op('dve', lambda e: e.reciprocal(out=rden[:, s_:s_ + 1], in_=acc[:, 128:129]), r=[], w=[acckeys[s_], tagp + 'rden'])
                k.op('dve', lambda e: e.tensor_scalar(out=stg[:, s_, :], in0=acc[:, 0:128], scalar1=rden[:, s_:s_ + 1], scalar2=None, op0=ALU.mult),
                     r=[tagp + 'rden'], w=[acckeys[s_], stgk])

        bg = {'gen': None}
        if STOP >= 4:
            cast_scope = ExitStack()
            k.mem = cast_scope
            cfs = [k.sb("cst_f%d" % i, [128, 2048], F32) for i in range(2)]
            cbs = [k.sb("cst_b%d" % i, [128, 2048], BF16) for i in range(2)]
            k.mem = es

            def cast_gen():
                items = []
                for wi, srcw in enumerate((pu, pv)):
                    for rb in range(128):
                        for hf_ in range(2):
                            items.append((srcw[rb * 128:(rb + 1) * 128, hf_ * 2048:(hf_ + 1) * 2048],
                                          uvb[rb * 128:(rb + 1) * 128, wi * D + hf_ * 2048:wi * D + (hf_ + 1) * 2048]))
                n = len(items)
                for i in range(n + 1):
                    if i < n:
                        fi = i % 2
                        k.dma('pool', 'cst_f%d' % fi, cfs[fi][:], items[i][0], w=['cst_f%d' % fi])
                    if i >= 1:
                        j = i - 1; fi = j % 2; bi = j % 2
                        eng = 'pool' if j % 3 == 0 else 'dve'
                        k.op(eng, lambda e: e.tensor_copy(out=cbs[bi][:], in_=cfs[fi][:]), r=['cst_f%d' % fi], w=['cst_b%d' % bi])
                        k.dma('pool', 'cst_b%d' % bi, items[j][1], cbs[bi][:], r=['cst_b%d' % bi], w=['ubvb'])
                    yield
            bg['gen'] = cast_gen()

        def bg_pull(n=1):
            g_ = bg['gen']
            if g_ is None:
                return
            for _ in range(n):
                try:
                    next(g_)
                except StopIteration:
                    bg['gen'] = None
                    return

        if STOP >= 4:
          with ExitStack() as ph:
            k.mem = ph
            kTs = [k.sb("fx_kT%d" % i, [128, S], BF16) for i in range(2)]
            Vs = [k.sb("fx_V%d" % i, [128, NTB, 129], BF16) for i in range(2)]
            qTt = [k.sb("fx_qT%d" % i, [128, 1024], BF16) for i in range(2)]
            Ps = [k.sb("fx_P%d" % i, [128, 512], BF16) for i in range(4)]
            stgs = [k.sb("fx_stg%d" % i, [128, NSLOT, 128], BF16) for i in range(2)]
            rden = k.sb("fx_rden", [128, 8], F32)
            Sb = [k.ps("fx_S%d" % i, [128, 512], F32) for i in range(4)]
            Ab1 = [k.ps("fx_A0_%d" % j, [128, 512], F32) for j in range(3)]
            Ab = [Ab1, Ab1]
            for i in range(2):
                k.op('pool', lambda e: e.memset(Vs[i][:, :, 128:129], 1.0), w=['fx_V%d' % i])
            nS = 0; nP = 0
            for h in range(NH):
                hi = h % 2
                kT = kTs[hi]; V = Vs[hi]; qT = qTt[hi]
                k.dma('sp', 'fx_kT%d' % hi, kT[:], fkT[h], w=['fx_kT%d' % hi])
                k.dma('sp', 'fx_V%d' % hi, V[:, :, 0:128], fv[h], w=['fx_V%d' % hi])
                k.dma('sp', 'fx_qT%d' % hi, qT[:], fqT[h], w=['fx_qT%d' % hi])
                accs = []; acckeys = []
                for s in range(NSLOT):
                    accs.append(Ab[0][s // 3][:, (s % 3) * 160:(s % 3) * 160 + 129]); acckeys.append('fx_A0_%d' % (s // 3))
                for j in range(3):
                    k.op('dve', lambda e: e.memset(Ab[0][j][:, :], 0.0), w=['fx_A0_%d' % j])
                tiles = [(g, kb) for g in range(2) for kb in range(32 * (g + 1))]

                def stageA(n):
                    g, kb = tiles[n]
                    smin = max(4 * g, kb // 8)
                    c0 = smin * 128; c1 = (4 * g + 4) * 128; nn = c1 - c0
                    Sp = Sb[n % 4]; Sk = 'fx_S%d' % (n % 4)
                    P = Ps[n % 4]; Pk = 'fx_P%d' % (n % 4)
                    zone = (kb // 8) >= 4 * g
                    k.op('pe', lambda e: e.matmul(Sp[:, 0:nn], lhsT=kT[:, kb * 128:(kb + 1) * 128], rhs=qT[:, c0:c1], start=True, stop=False),
                         r=['fx_kT%d' % hi, 'fx_qT%d' % hi], w=[Sk])
                    k.op('pe', lambda e: e.matmul(Sp[:, 0:nn], lhsT=sel16[:, h * 128:(h + 1) * 128], rhs=cT[:, c0:c1], start=False, stop=not zone),
                         r=['sel16', 'cT'], w=[Sk])
                    if zone:
                        sz = kb // 8; z = kb - 8 * sz
                        k.op('pe', lambda e: e.matmul(Sp[:, 0:128], lhsT=NMf[:, sz, z * 128:(z + 1) * 128], rhs=ident_b[:], start=False, stop=True),
                             r=['NMf', 'ident_b'], w=[Sk])
                    k.op('act', lambda e: e.activation(out=P[:, 0:nn], in_=Sp[:, 0:nn], func=AF.Exp, bias=nc_all[:, kb * NH + h:kb * NH + h + 1], scale=1.0),
                         r=['nc_all'], w=[Sk, Pk])

                def stageB(n):
                    g, kb = tiles[n]
                    smin = max(4 * g, kb // 8)
                    P = Ps[n % 4]; Pk = 'fx_P%d' % (n % 4)
                    for s in range(smin, 4 * g + 4):
                        k.op('pe', lambda e: e.matmul(accs[s], lhsT=P[:, (s - smin) * 128:(s - smin + 1) * 128], rhs=V[:, kb, :], start=False, stop=False, skip_group_check=True),
                             r=[Pk, 'fx_V%d' % hi], w=[acckeys[s]])
                LA = 2
                for n in range(len(tiles) + LA):
                    if n < len(tiles):
                        stageA(n)
                    if n >= LA:
                        stageB(n - LA)
                    if n % 6 == 5:
                        bg_pull()
                stg = stgs[hi]; stgk = 'fx_stg%d' % hi
                attn_evacuate('fx', accs, acckeys, stg, stgk, rden)
                k.dma('pool', stgk, mix[:, :, h * 128:(h + 1) * 128].rearrange("s i d -> i s d"), stg[:, :, :], r=[stgk], w=['mix'])
            k.barrier()
          k.mem = es

        if STOP >= 5:
          with ExitStack() as ph:
            k.mem = ph
            dkt = k.sb("ds_kT", [128, S], BF16); dvt = k.sb("ds_V", [128, NTB, 129], BF16); ikt = k.sb("ds_ik", [64, S], BF16)
            dqs = [k.sb("ds_q%d" % i, [128, NH, 128], BF16) for i in range(2)]
            iqs = [k.sb("ds_iq%d" % i, [64, NH, 128], BF16) for i in range(2)]
            score = k.sb("ds_score", [128, S], F32); NMd = k.sb("ds_NM0", [128, S], BF16)
            rl = [k.sb("ds_rl%d" % i, [128, 512], F32) for i in range(2)]
            Ps = [k.sb("ds_P%d" % i, [128, 512], BF16) for i in range(4)]
            stgs = [k.sb("ds_stg%d" % i, [128, 4, 128], BF16) for i in range(2)]
            bs = k.sb("ds_bs", [128, 16], F32)
            tau_all = k.sb("ds_tau", [128, NSLOT], F32)
            rden = k.sb("ds_rden", [128, 8], F32)
            Lb = [k.ps("ds_L%d" % i, [128, 512], F32) for i in range(2)]
            Sb = [k.ps("ds_S%d" % i, [128, 512], F32) for i in range(4)]
            Ab1 = [k.ps("ds_A0_%d" % j, [128, 512], F32) for j in range(2)]
            Ab = [Ab1, Ab1]
            k.dma('sp', 'ds_kT', dkt[:], dkT[:, :], w=['ds_kT'])
            k.op('pool', lambda e: e.memset(dvt[:, :, 128:129], 1.0), w=['ds_V'])
            k.dma('sp', 'ds_V', dvt[:, :, 0:128], dv[:, :, :], w=['ds_V'])
            k.dma('sp', 'ds_ik', ikt[:], ikT[:, :], w=['ds_ik'])
            sgv = k.sb("sgn", [128, NSLOT * NH], F32)
            k.op('dve', lambda e: e.tensor_scalar(out=sgv[:], in0=aw[:], scalar1=0.0, scalar2=2.0, op0=ALU.is_ge, op1=ALU.mult), r=['aw'], w=['sgn'])
            k.op('dve', lambda e: e.tensor_scalar(out=sgv[:], in0=sgv[:], scalar1=-1.0, scalar2=None, op0=ALU.add), r=[], w=['sgn'])
            NMds = [NMd, k.sb("ds_NM1", [128, S], BF16)]
            cnt_ = {'L': 0, 'R': 0, 'S': 0, 'A': 0}

            def prep_slot(s):
                si = s % 2; L = 1024 * (s + 1)
                dq = dqs[si]; iq = iqs[si]; NMs = NMds[si]; NMk = 'ds_NM%d' % si
                k.dma('sp', 'ds_q%d' % si, dq[:], dqT[:, s, :, :], w=['ds_q%d' % si])
                k.dma('sp', 'ds_iq%d' % si, iq[:], iqT[:, s, :, :], w=['ds_iq%d' % si])
                for kc in range(L // 512):
                    for h in range(NH):
                        nL = cnt_['L']; cnt_['L'] += 1; nR = cnt_['R']; cnt_['R'] += 1
                        Lp = Lb[nL % 2]; Lk = 'ds_L%d' % (nL % 2)
                        R = rl[nR % 2]; Rk = 'ds_rl%d' % (nR % 2)
                        k.op('pe', lambda e: e.matmul(Lp[:, :], lhsT=iq[:, h, :], rhs=ikt[:, kc * 512:(kc + 1) * 512], start=True, stop=True),
                             r=['ds_iq%d' % si, 'ds_ik'], w=[Lk])
                        k.op('act', lambda e: e.activation(out=R[:, :], in_=Lp[:, :], func=AF.Relu, scale=sgv[:, s * NH + h:s * NH + h + 1]), r=['sgn'], w=[Lk, Rk])
                        if h == 0:
                            k.op('dve', lambda e: e.tensor_scalar(out=score[:, kc * 512:(kc + 1) * 512], in0=R[:, :], scalar1=sgv[:, s * NH + h:s * NH + h + 1], scalar2=None, op0=ALU.mult),
                                 r=[Rk, 'sgn'], w=['ds_score'])
                        else:
                            k.op('dve', lambda e: e.scalar_tensor_tensor(out=score[:, kc * 512:(kc + 1) * 512], in0=R[:, :], scalar=sgv[:, s * NH + h:s * NH + h + 1],
                                                                        in1=score[:, kc * 512:(kc + 1) * 512], op0=ALU.mult, op1=ALU.add), r=[Rk, 'sgn'], w=['ds_score'])
                        yield
                k.op('dve', lambda e: e.tensor_reduce(out=bs[:, 0:1], in_=score[:, 0:L], axis=AX.X, op=ALU.max), r=['ds_score'], w=['ds_bs'])
                k.op('dve', lambda e: e.tensor_reduce(out=bs[:, 1:2], in_=score[:, 0:L], axis=AX.X, op=ALU.min), r=['ds_score'], w=['ds_bs'])
                k.op('dve', lambda e: e.tensor_tensor(out=bs[:, 2:3], in0=bs[:, 0:1], in1=bs[:, 1:2], op=ALU.subtract), r=[], w=['ds_bs'])
                k.op('dve', lambda e: e.tensor_scalar(out=bs[:, 2:3], in0=bs[:, 2:3], scalar1=1.0001, scalar2=2e-3, op0=ALU.mult, op1=ALU.add), r=[], w=['ds_bs'])
                k.op('dve', lambda e: e.tensor_scalar(out=bs[:, 3:4], in0=bs[:, 1:2], scalar1=-1.0, scalar2=1e-3, op0=ALU.mult, op1=ALU.add), r=[], w=['ds_bs'])
                yield
                k.op('pool', lambda e: e.tensor_scalar(out=rl[0][:, :], in0=kio[:, 0:512], scalar1=qrel[:, s:s + 1], scalar2=-1e30, op0=ALU.is_gt, op1=ALU.mult), r=['kio', 'qrel'], w=['ds_rl0'])
                k.op('pool', lambda e: e.tensor_scalar(out=rl[1][:, :], in0=kio[:, 512:1024], scalar1=qrel[:, s:s + 1], scalar2=-1e30, op0=ALU.is_gt, op1=ALU.mult), r=['kio', 'qrel'], w=['ds_rl1'])
                k.op('dve', lambda e: e.tensor_tensor(out=score[:, L - 1024:L - 512], in0=score[:, L - 1024:L - 512], in1=rl[0][:, :], op=ALU.add), r=['ds_rl0'], w=['ds_score'])
                k.op('dve', lambda e: e.tensor_tensor(out=score[:, L - 512:L], in0=score[:, L - 512:L], in1=rl[1][:, :], op=ALU.add), r=['ds_rl1'], w=['ds_score'])
                yield
                thr = float(2 * TOPK - L) - 0.5
                for it in range(NBIS):
                    f = -(2.0 ** -(it + 1))
                    k.op('dve', lambda e: e.tensor_scalar(out=bs[:, 4:5], in0=bs[:, 2:3], scalar1=float(f), scalar2=None, op0=ALU.mult), r=[], w=['ds_bs'])
                    k.op('dve', lambda e: e.tensor_tensor(out=bs[:, 5:6], in0=bs[:, 4:5], in1=bs[:, 3:4], op=ALU.add), r=[], w=['ds_bs'])
                    k.op('act', lambda e: e.activation(out=NMs[:, 0:L], in_=score[:, 0:L], func=AF.Sign, bias=bs[:, 5:6], scale=1.0, accum_out=bs[:, 6:7]),
                         r=['ds_score', 'ds_bs'], w=[NMk, 'ds_bs2'])
                    k.op('dve', lambda e: e.scalar_tensor_tensor(out=bs[:, 7:8], in0=bs[:, 6:7], scalar=thr, in1=bs[:, 4:5], op0=ALU.is_ge, op1=ALU.mult), r=['ds_bs2'], w=['ds_bs'])
                    k.op('dve', lambda e: e.tensor_tensor(out=bs[:, 3:4], in0=bs[:, 3:4], in1=bs[:, 7:8], op=ALU.add), r=[], w=['ds_bs'])
                    yield
                k.op('dve', lambda e: e.tensor_scalar(out=tau_all[:, s:s + 1], in0=bs[:, 3:4], scalar1=-1.0, scalar2=None, op0=ALU.mult), r=['ds_bs'], w=['ds_tau'])
                k.op('dve', lambda e: e.tensor_scalar(out=NMs[:, 0:L], in0=score[:, 0:L], scalar1=tau_all[:, s:s + 1], scalar2=NEG, op0=ALU.is_lt, op1=ALU.mult),
                     r=['ds_score', 'ds_tau'], w=[NMk])
                yield

            def prep_steps(s):
                return 16 * 2 * (s + 1) + 3 + NBIS + 1

            def attend_slot(s, gnext, nsteps_next):
                si = s % 2; nkb = 8 * (s + 1)
                dq = dqs[si]; NMs = NMds[si]; NMk = 'ds_NM%d' % si
                ntiles = 4 * (nkb + 2); done_t = 0; done_bg = 0
                for hg in range(4):
                    ai = cnt_['A'] % 2; cnt_['A'] += 1
                    accs = [Ab[0][j // 2][:, (j % 2) * 160:(j % 2) * 160 + 129] for j in range(4)]
                    acckeys = ['ds_A0_%d' % (j // 2) for j in range(4)]
                    for j in range(2):
                        k.op('dve', lambda e: e.memset(Ab[0][j][:, :], 0.0), w=['ds_A0_%d' % j])
                    nS = cnt_['S']

                    def stageA(kb):
                        n = nS + kb
                        Sp = Sb[n % 4]; Sk = 'ds_S%d' % (n % 4)
                        P = Ps[n % 4]; Pk = 'ds_P%d' % (n % 4)
                        k.op('pe', lambda e: e.matmul(Sp[:, :], lhsT=dkt[:, kb * 128:(kb + 1) * 128], rhs=dq[:, hg * 4:(hg + 1) * 4, :], start=True, stop=False),
                             r=['ds_kT', 'ds_q%d' % si], w=[Sk])
                        for hh in range(4):
                            k.op('pe', lambda e: e.matmul(Sp[:, hh * 128:(hh + 1) * 128], lhsT=NMs[:, kb * 128:(kb + 1) * 128], rhs=ident_b[:], start=False, stop=(hh == 3)),
                                 r=[NMk, 'ident_b'], w=[Sk])
                        k.op('act', lambda e: e.activation(out=P[:, :], in_=Sp[:, :], func=AF.Exp), r=[], w=[Sk, Pk])

                    def stageB(kb):
                        n = nS + kb
                        P = Ps[n % 4]; Pk = 'ds_P%d' % (n % 4)
                        for hh in range(4):
                            k.op('pe', lambda e: e.matmul(accs[hh], lhsT=P[:, hh * 128:(hh + 1) * 128], rhs=dvt[:, kb, :], start=False, stop=False, skip_group_check=True),
                                 r=[Pk, 'ds_V'], w=[acckeys[hh]])
                    LA = 2
                    for kb in range(nkb + LA):
                        if kb < nkb:
                            stageA(kb)
                        if kb >= LA:
                            stageB(kb - LA)
                        if kb % 4 == 3:
                            bg_pull()
                        done_t += 1
                        if gnext is not None:
                            want = (done_t * nsteps_next) // ntiles
                            while done_bg < want:
                                done_bg += 1
                                try:
                                    next(gnext)
                                except StopIteration:
                                    gnext = None
                                    break
                    cnt_['S'] += nkb
                    stg = stgs[ai]; stgk = 'ds_stg%d' % ai
                    attn_evacuate('ds', accs, acckeys, stg, stgk, rden)
                    k.dma('pool', stgk, mix[s, :, 2048 + hg * 512:2048 + (hg + 1) * 512], stg[:, :, :], r=[stgk], w=['mix'])
                if gnext is not None:
                    for _ in gnext:
                        pass

            for _ in prep_slot(0):
                pass
            for s in range(NSLOT):
                gnext = prep_slot(s + 1) if s + 1 < NSLOT else None
                attend_slot(s, gnext, prep_steps(s + 1) if s + 1 < NSLOT else 0)
            if 'dbg_tau' in dbg.get('dump', ()):
                k.dma('sp', 'dbgtau', dbg_tau[:, :], tau_all[:], r=['ds_tau'], w=['dbg_tau'])
            k.barrier()
          k.mem = es

        bg_pull(10000)
        if STOP >= 4:
            k.barrier()
            cast_scope.close()
        if STOP >= 6:
            norm_transpose_phase("mt", [mix[b] for b in range(NSLOT)], NSLOT, None, [mixT[b] for b in range(NSLOT)], do_norm=False, src_bf16=True)
            with ExitStack() as ph:
                k.mem = ph
                xos = [k.sb("wo_xo%d" % i, [128, 512], F32) for i in range(2)]
                cw = {'n': 0}

                def post_wo(name, b, ps, psk):
                    c = int(name[2:]); i2 = cw['n'] % 2; cw['n'] += 1
                    xo = xos[i2]; xk = 'wo_xo%d' % i2
                    k.dma('sp', xk, xo[:], x_own[b * 128:(b + 1) * 128, c * 512:(c + 1) * 512], w=[xk])
                    k.op('dve', lambda e: e.tensor_tensor(out=xo[:], in0=ps[:, :], in1=xo[:], op=ALU.add), r=[], w=[psk, xk])
                    k.dma('pool', xk + 'o', x1[b * 128:(b + 1) * 128, c * 512:(c + 1) * 512], xo[:], r=[xk], w=['x1'])
                project("wo", [mixT[b] for b in range(NSLOT)], NSLOT, w_o, [('wo%d' % i, [(0, i * 512, 512)], 512) for i in range(8)], post_wo, resident=True)
                k.barrier()
            k.mem = es

        if STOP >= 7:
            norm_transpose_phase("n2", [x1[b * 128:(b + 1) * 128, :] for b in range(NSLOT)], NSLOT, ln2_g, [h2T[b] for b in range(NSLOT)],
                                 dst_b16=[h2b[b * 128:(b + 1) * 128, :] for b in range(NSLOT)])
            with ExitStack() as ph:
                k.mem = ph
                yb = k.sb("pq_yb", [128, 4, 128], BF16); stgq = [k.sb("pq_stg%d" % i, [128, 4, 128], BF16) for i in range(2)]
                ptq = k.ps("pq_pt", [128, 1024], BF16)
                cq = {'n': 0}

                def post_pq(name, b, ps, psk):
                    c = int(name[2:]); i2 = cq['n'] % 2; cq['n'] += 1
                    k.op('act', lambda e: e.copy(out=yb[:, :, :], in_=ps[:, 0:512].rearrange("p (h d) -> p h d", h=4)), r=[], w=[psk, 'pq_yb'])
                    for h in range(4):
                        k.op('pe', lambda e: e.transpose(ptq[:, h * 128:(h + 1) * 128], yb[:, h, :], ident_b[:]), r=['pq_yb', 'ident_b'], w=['pq_pt'])
                    stg = stgq[i2]; stgk = 'pq_stg%d' % i2
                    k.op('act', lambda e: e.copy(out=stg[:, :, :], in_=ptq[:, 0:512].rearrange("p (h t) -> p h t", h=4)), r=[], w=['pq_pt', stgk])
                    k.dma('pool', stgk, qTs[b][:, c * 512:(c + 1) * 512], stg[:, :, :], r=[stgk], w=['qTs'])
                project("pq", [h2T[b] for b in range(NSLOT)], NSLOT, wq, [('pq%d' % i, [(0, i * 512, 512)], 512) for i in range(2)], post_pq, resident=True)
                k.barrier()
            k.mem = es
            with ExitStack() as ph:
                k.mem = ph
                kkb = k.sb("pe_kkb", [128, 8, 256], BF16)
                kkf = hbt_early = k.sb("pe_h2b", [128, D], BF16)
                kkf32 = kkf[:, 0:D].bitcast(F32).rearrange("p (h n) -> p h n", h=8)
                k.dma('sp', 'pe_kk', kkf32, kk.rearrange("h p n -> p h n"), w=['pe_h2b0'])
                k.op('dve', lambda e: e.tensor_copy(out=kkb[:], in_=kkf32), r=['pe_h2b0'], w=['pe_kkb'])
                qT = k.sb("pe_qT", [128, 1024], BF16)
                s12 = k.sb("pe_s12", [128, 8, 256], F32); tmpS = k.sb("pe_tmpS", [128, 256], F32)
                V12 = k.sb("pe_V12", [128, 8, 2, 16], F32); I12 = k.sb("pe_I12", [128, 8, 2, 16], U32); If = k.sb("pe_If", [128, 8, 2, 16], F32)
                cand = k.sb("pe_cand", [128, 8, 256], F32); cidx = k.sb("pe_cidx", [128, 8, 256], F32)
                Bt = k.sb("pe_B", [128, 8, 16], F32); Et = k.sb("pe_E", [128, 8, 16], F32); Ei = k.sb("pe_Ei", [128, 128], I32)
                gt = k.sb("pe_gate", [128, 8, 16], F32); gs = k.sb("pe_gs", [128, 16], F32)
                hd = k.sb("pe_hd", [128, 128], F32); at = k.sb("pe_a", [128, 128], F32)
                hbt = hbt_early; x1t = k.sb("pe_x1", [128, D], F32)
                NROW = 5
                rows = [k.sb("pe_row%d" % i, [128, 2 * D], BF16) for i in range(NROW)]
                aj = k.sb("pe_aj", [128, 8], F32)
                accs_ = [k.sb("pe_acc%d" % i, [128, 512], F32) for i in range(2)]; junk = k.sb("pe_junk", [128, D], BF16)
                diags = [k.sb("pe_dg%d" % i, [128, 128], BF16) for i in range(4)]
                ps12 = [k.ps("pe_ps%d" % i, [128, 512], F32) for i in range(8)]
                nrow = 0
                Ei2 = [Ei, k.sb("pe_Ei1", [128, 128], I32)]; gt2 = [gt, k.sb("pe_gate1", [128, 8, 16], F32)]
                hbt2 = [hbt, k.sb("pe_h2b1", [128, D], BF16)]

                def sel_matmul(b):
                    k.dma('sp', 'pe_qT', qT[:], qTs[b], w=['pe_qT'])
                    for h in range(8):
                        pb = ps12[h // 2]
                        k.op('pe', lambda e: e.matmul(pb[:, (h % 2) * 256:(h % 2) * 256 + 256], lhsT=qT[:, h * 128:(h + 1) * 128], rhs=kkb[:, h, :], start=True, stop=True),
                             r=['pe_qT', 'pe_kkb'], w=['pe_ps%d' % (h // 2)])
                    for j in range(4):
                        k.op('act', lambda e: e.copy(out=s12[:, 2 * j:2 * j + 2, :], in_=ps12[j][:, :].rearrange("p (h n) -> p h n", h=2)), r=[], w=['pe_ps%d' % j, 'pe_s12'])

                def sel_gen(b):
                    Eib = Ei2[b % 2]; gtb = gt2[b % 2]; eik = 'pe_Ei%d' % (b % 2); gk_ = 'pe_gate%d' % (b % 2)
                    for h in range(8):
                        for z in range(2):
                            vals = s12[:, h, z * 128:(z + 1) * 128]
                            k.op('dve', lambda e: e.max(out=V12[:, h, z, 0:8], in_=vals), r=['pe_s12'], w=['pe_V12'])
                            k.op('dve', lambda e: e.max_index(out=I12[:, h, z, 0:8], in_max=V12[:, h, z, 0:8], in_values=vals), r=['pe_s12', 'pe_V12'], w=['pe_I12'])
                            k.op('dve', lambda e: e.match_replace(out=tmpS[:, 0:128], in_to_replace=V12[:, h, z, 0:8], in_values=vals, imm_value=-1e30), r=['pe_s12', 'pe_V12'], w=['pe_tmpS'])
                            yield
                            k.op('dve', lambda e: e.max(out=V12[:, h, z, 8:16], in_=tmpS[:, 0:128]), r=['pe_tmpS'], w=['pe_V12'])
                            k.op('dve', lambda e: e.max_index(out=I12[:, h, z, 8:16], in_max=V12[:, h, z, 8:16], in_values=tmpS[:, 0:128]), r=['pe_tmpS', 'pe_V12'], w=['pe_I12'])
                            yield
                    k.op('dve', lambda e: e.tensor_copy(out=If[:], in_=I12[:]), r=['pe_I12'], w=['pe_If'])
                    for h in range(8):
                        ch = cand[:, h, :].rearrange("p (a b) -> p a b", a=16); ci = cidx[:, h, :].rearrange("p (a b) -> p a b", a=16)
                        k.op('dve', lambda e: e.tensor_tensor(out=ch, in0=V12[:, h, 0, :].unsqueeze(2).to_broadcast([128, 16, 16]),
                                                             in1=V12[:, h, 1, :].unsqueeze(1).to_broadcast([128, 16, 16]), op=ALU.add), r=['pe_V12'], w=['pe_cand'])
                        k.op('dve', lambda e: e.scalar_tensor_tensor(out=ci, in0=If[:, h, 0, :].unsqueeze(2).to_broadcast([128, 16, 16]), scalar=128.0,
                                                                    in1=If[:, h, 1, :].unsqueeze(1).to_broadcast([128, 16, 16]), op0=ALU.mult, op1=ALU.add), r=['pe_If'], w=['pe_cidx'])
                        yield
                        k.op('dve', lambda e: e.max(out=Bt[:, h, 0:8], in_=cand[:, h, :]), r=['pe_cand'], w=['pe_B'])
                        k.op('dve', lambda e: e.match_replace(out=tmpS[:, :], in_to_replace=Bt[:, h, 0:8], in_values=cand[:, h, :], imm_value=-1e30), r=['pe_cand', 'pe_B'], w=['pe_tmpS'])
                        k.op('dve', lambda e: e.max(out=Bt[:, h, 8:16], in_=tmpS[:, :]), r=['pe_tmpS'], w=['pe_B'])
                        yield
                        for kk_ in range(16):
                            k.op('dve', lambda e: e.scalar_tensor_tensor(out=tmpS[:, :], in0=cand[:, h, :], scalar=Bt[:, h, kk_:kk_ + 1], in1=cidx[:, h, :],
                                                                        op0=ALU.is_equal, op1=ALU.mult, accum_out=Et[:, h, kk_:kk_ + 1]), r=['pe_cand', 'pe_cidx', 'pe_B'], w=['pe_tmpS', 'pe_E'])
                            if kk_ % 2 == 1:
                                yield
                        k.op('dve', lambda e: e.tensor_scalar(out=gs[:, h:h + 1], in0=Bt[:, h, 0:1], scalar1=-1.0, scalar2=None, op0=ALU.mult), r=['pe_B'], w=['pe_gs'])
                        k.op('act', lambda e: e.activation(out=gtb[:, h, :], in_=Bt[:, h, :], func=AF.Exp, bias=gs[:, h:h + 1], scale=1.0, accum_out=gs[:, 8 + h:9 + h]),
                             r=['pe_B', 'pe_gs'], w=[gk_, 'pe_gs2'])
                        yield
                    k.op('dve', lambda e: e.reciprocal(out=gs[:, 8:16], in_=gs[:, 8:16]), r=['pe_gs2'], w=['pe_gs2'])
                    k.op('dve', lambda e: e.tensor_tensor(out=gtb[:, :, :], in0=gtb[:, :, :], in1=gs[:, 8:16].unsqueeze(2).to_broadcast([128, 8, 16]), op=ALU.mult), r=['pe_gs2'], w=[gk_])
                    k.op('dve', lambda e: e.tensor_scalar(out=Et[:, :, :], in0=Et[:, :, :], scalar1=0.0, scalar2=16383.0, op0=ALU.max, op1=ALU.min), r=[], w=['pe_E'])
                    k.op('dve', lambda e: e.tensor_copy(out=Eib[:, :], in_=Et[:, :, :].rearrange("p h k -> p (h k)")), r=['pe_E'], w=[eik])
                    yield

                SEL_STEPS = 8 * 2 * 2 + 8 * (2 + 8 + 1) + 1

                def expert_loop(b, gnext):
                    nonlocal_n = nrow_box[0]
                    Eib = Ei2[b % 2]; gtb = gt2[b % 2]; eik = 'pe_Ei%d' % (b % 2); gk_ = 'pe_gate%d' % (b % 2)
                    hb_ = hbt2[b % 2]; hbk = 'pe_h2b%d' % (b % 2)
                    k.dma('sp', hbk, hb_[:], h2b[b * 128:(b + 1) * 128, :], w=[hbk])
                    k.dma('sp', 'pe_x1', x1t[:], x1[b * 128:(b + 1) * 128, :], w=['pe_x1'])
                    gflat = gtb[:, :, :].rearrange("p h k -> p (h k)")
                    LOOK = 3

                    def emit_gather(jj):
                        n_ = nonlocal_n + jj
                        k.gather('pe_row%d' % (n_ % NROW), rows[n_ % NROW][:], uvb[:, :], Eib[:, jj:jj + 1], r=[eik, 'ubvb'], w=['pe_row%d' % (n_ % NROW)])

                    def stage1(j):
                        rw = rows[(nonlocal_n + j) % NROW]; rk = 'pe_row%d' % ((nonlocal_n + j) % NROW)
                        a0_ = (j % 4) * 2
                        k.op('dve', lambda e: e.scalar_tensor_tensor(out=junk[:], in0=rw[:, 0:D], scalar=1.0, in1=hb_[:], op0=ALU.mult, op1=ALU.mult, accum_out=hd[:, j:j + 1]),
                             r=[rk, hbk], w=['pe_junk', 'pe_hd%d' % (j % 4)])
                        k.op('act', lambda e: e.activation(out=aj[:, a0_:a0_ + 1], in_=hd[:, j:j + 1], func=AF.Gelu), r=['pe_hd%d' % (j % 4)], w=['pe_aj%d' % (j % 4)])
                        k.op('act', lambda e: e.activation(out=aj[:, a0_ + 1:a0_ + 2], in_=aj[:, a0_:a0_ + 1], func=AF.Copy, scale=gflat[:, j:j + 1]), r=[gk_], w=['pe_aj%d' % (j % 4)])

                    def stage2(j):
                        rw = rows[(nonlocal_n + j) % NROW]; rk = 'pe_row%d' % ((nonlocal_n + j) % NROW)
                        dg = diags[j % 4]; dgk = 'pe_dg%d' % (j % 4); a0_ = (j % 4) * 2
                        k.op('act', lambda e: e.activation(out=dg[:], in_=ident_f[:], func=AF.Copy, scale=aj[:, a0_ + 1:a0_ + 2]), r=['ident_f', 'pe_aj%d' % (j % 4)], w=[dgk])
                        for c in range(8):
                            k.op('pe', lambda e: e.matmul(ps12[c][:, :], lhsT=dg[:], rhs=rw[:, D + c * 512:D + (c + 1) * 512], start=(j == 0), stop=(j == 127)),
                                 r=[dgk, rk], w=['pe_ps%d' % c])
                    for jj in range(LOOK):
                        emit_gather(jj)
                    pulled = 0
                    for j in range(129):
                        if j + LOOK < 128:
                            emit_gather(j + LOOK)
                        if j < 128:
                            stage1(j)
                        if j >= 1:
                            stage2(j - 1)
                        if gnext is not None:
                            want = ((j + 1) * SEL_STEPS) // 120
                            while pulled < want:
                                pulled += 1
                                try:
                                    next(gnext)
                                except StopIteration:
                                    gnext = None
                                    break
                    if gnext is not None:
                        for _ in gnext:
                            pass
                    nrow_box[0] += 128
                    for c in range(8):
                        ac = accs_[c % 2]; ack = 'pe_acc%d' % (c % 2)
                        k.op('dve', lambda e: e.tensor_tensor(out=ac[:, :], in0=ps12[c][:, :], in1=x1t[:, c * 512:(c + 1) * 512], op=ALU.add),
                             r=['pe_x1'], w=['pe_ps%d' % c, ack])
                        k.dma('sp', ack, out_own[b * 128:(b + 1) * 128, c * 512:(c + 1) * 512], ac[:, :], r=[ack], w=['out'])

                nrow_box = [0]
                sel_matmul(0)
                for _ in sel_gen(0):
                    pass
                for b in range(NSLOT):
                    gnext = None
                    if b + 1 < NSLOT:
                        sel_matmul(b + 1)
                        gnext = sel_gen(b + 1)
                    expert_loop(b, gnext)
                k.barrier()
            k.mem = es

        k.barrier()
        print("instructions:", k.ninst)
    return nc


def _blocks(c):
    return [c, 15 - c, 16 + c, 31 - c, 32 + c, 47 - c, 48 + c, 63 - c]


def make_in_maps(inp):
    f = lambda a: np.ascontiguousarray(np.asarray(a), dtype=np.float32)
    x = f(inp["x"])[0]; pos = np.asarray(inp["positions"])[0].astype(np.int32)
    common = {
        "x_all": x, "pos_all": np.ascontiguousarray(pos.reshape(NTB, 128).T),
        "ln1_g": f(inp["ln1_g"]), "ln2_g": f(inp["ln2_g"]), "w_in": f(inp["w_in"])[0], "w_o": f(inp["w_o"])[0],
        "peer_wq": f(inp["peer_wq"])[0], "ffb": f(inp["fox_forget_b"]), "gq_f": f(inp["fox_qn_g"]), "gk_f": f(inp["fox_kn_g"]),
        "gq_d": f(inp["dsa_qn_g"]), "gk_d": f(inp["dsa_kn_g"]), "peer_u": f(inp["peer_u"])[0], "peer_v": f(inp["peer_v"])[0],
    }
    k1 = f(inp["peer_keys1"])[0]; k2 = f(inp["peer_keys2"])[0]
    kkm = np.zeros((8, 128, 256), np.float32)
    for h in range(8):
        kkm[h, 0:64, 0:128] = k1[h].T; kkm[h, 64:128, 128:256] = k2[h].T
    common["kk"] = kkm
    common["c_ident"] = np.eye(128, dtype=np.float32)
    common["c_tri"] = np.triu(np.ones((128, 128), np.float32))
    sel = np.zeros((16, NH * 128), np.float32)
    for h in range(NH):
        sel[h, h * 128:(h + 1) * 128] = 1.0
    common["c_sel16"] = sel
    inv128 = (10000.0 ** (-np.arange(0, 128, 2, dtype=np.float32) / np.float32(128))).astype(np.float32)
    inv64 = (10000.0 ** (-np.arange(0, 64, 2, dtype=np.float32) / np.float32(64))).astype(np.float32)
    common["c_inv"] = np.concatenate([inv128, inv64])[None, :].astype(np.float32)
    common["c_kiota"] = np.arange(1024, dtype=np.float32)[None, :]
    maps = []
    for c in range(8):
        blks = _blocks(c)
        rows = np.concatenate([np.arange(b * 128, (b + 1) * 128) for b in blks])
        m = dict(common)
        m["x_own"] = np.ascontiguousarray(x[rows])
        m["pos_own"] = np.ascontiguousarray(pos[rows].reshape(NSLOT, 128).T)
        m["qidx_p"] = np.ascontiguousarray(rows.astype(np.float32).reshape(NSLOT, 128).T)
        sb = np.zeros((NSLOT, NTB), np.float32)
        for s, b in enumerate(blks):
            sb[s, b] = 1.0
        m["selblk"] = sb.reshape(1, -1)
        maps.append(m)
    return maps


def kernel(**inputs):
    maps = make_in_maps(inputs)
    nc = build()
    res = run_bass_kernel_spmd(nc, maps, core_ids=list(range(8)))
    out = np.zeros((1, S, D), np.float32)
    for c in range(8):
        o = res.results[c]["out_own"]
        for s, b in enumerate(_blocks(c)):
            out[0, b * 128:(b + 1) * 128] = o[s * 128:(s + 1) * 128]
    return out
```
